# Optimizing a Trainium2 kernel written in Bass

```python
import math
import jax
import jax.numpy as jnp
from jax import lax
import numpy as np

D_MODEL = 1024
BATCH = 4
SEQ = 8192
DEPTH = 4

CHUNK = 64
NORM_EPS = 1e-6
A_HEAD_DIM = 64
A_WIDTH = D_MODEL // 2
A_HEADS = A_WIDTH // A_HEAD_DIM
A_DECAY_LORA = 64
A_ICLR_LORA = 64
A_VRES_LORA = 32
A_GATE_LORA = 128
A_GN_EPS = 64e-5
B_HEAD_DIM = 128
B_WIDTH = D_MODEL // 2
B_HEADS = B_WIDTH // B_HEAD_DIM
B_CONV = 4
EVEN_IN = 3 * A_WIDTH + 4 * B_WIDTH + 2 * B_HEADS
MIX_WIDTH = A_WIDTH + B_WIDTH
C_HEAD_DIM = 64
C_HEADS = D_MODEL // C_HEAD_DIM
C_WIDTH = C_HEADS * C_HEAD_DIM
C_WINDOW_CHUNKS = 8
C_MAX_REL = 256
D_FF = 4 * D_MODEL
PLE_DIM = 256
N_EVEN = (DEPTH + 1) // 2
N_ODD = DEPTH // 2

kernel_name = 'hybrid_rwkv7_gdn_chunkattn_trunk'


def rmsnorm(x, g, eps=NORM_EPS):
    xf = x.astype(jnp.float32)
    y = xf * lax.rsqrt(jnp.mean(xf * xf, -1, keepdims=True) + eps)
    return (y * g.astype(jnp.float32)).astype(x.dtype)


def l2norm(x, eps=1e-6):
    xf = x.astype(jnp.float32)
    return xf * lax.rsqrt(jnp.sum(xf * xf, -1, keepdims=True) + eps)


def token_shift(z):
    return jnp.pad(z, ((0, 0), (1, 0), (0, 0)))[:, :-1]


def causal_depthwise_conv(z, w):
    kw = w.shape[0]
    zp = jnp.pad(z, ((0, 0), (kw - 1, 0), (0, 0)))
    return lax.conv_general_dilated(zp, w[:, None, :].astype(z.dtype), window_strides=(1,), padding='VALID',
                                    dimension_numbers=('NWC', 'WIO', 'NWC'), feature_group_count=z.shape[-1])


def wkv7_scan(r, w, k, v, a, b):
    bsz, seq, nh, n = r.shape
    xs = tuple(jnp.swapaxes(z, 0, 1) for z in (r, w, k, v, a, b))

    def step(state, inp):
        r_t, w_t, k_t, v_t, a_t, b_t = inp
        sa = jnp.einsum('bhij,bhj->bhi', state, a_t)
        state = state * w_t[:, :, None, :] + sa[..., None] * b_t[:, :, None, :] + v_t[..., None] * k_t[:, :, None, :]
        return state, jnp.einsum('bhij,bhj->bhi', state, r_t)

    s0 = jnp.zeros((bsz, nh, n, n), jnp.float32)
    _, ys = lax.scan(step, s0, xs)
    return jnp.swapaxes(ys, 0, 1)


def rwkv7_group(h, dh, r, k, v, v_first, mu_proj, mu_lora, w0, w1, w2, a0, a1, a2, g1, g2,
                k_k, k_a, r_k, ln_g, ln_b, vres):
    bsz, seq, _ = h.shape
    heads = lambda z: z.reshape(bsz, seq, A_HEADS, A_HEAD_DIM)
    r = r + (token_shift(r) - r) * mu_proj[0]
    k = k + (token_shift(k) - k) * mu_proj[1]
    v = v + (token_shift(v) - v) * mu_proj[2]
    xw = h + dh * mu_lora[0]
    xa = h + dh * mu_lora[1]
    xg = h + dh * mu_lora[2]
    w_log = -jax.nn.softplus(-(w0 + jnp.tanh(xw @ w1) @ w2).astype(jnp.float32)) - 0.5
    decay = jnp.exp(-jnp.exp(w_log))
    a = jax.nn.sigmoid(a0 + (xa @ a1) @ a2)
    g = jax.nn.sigmoid(xg @ g1) @ g2
    kk = l2norm(heads(k * k_k))
    k = k * (1.0 + (a - 1.0) * k_a)
    if vres is None:
        v_first = v
    else:
        v_mu, v0, v1, v2 = vres
        xv = h + dh * v_mu
        v = v + (v_first - v) * jax.nn.sigmoid(v0 + (xv @ v1) @ v2)
    rh, kh, vh, ah = [heads(z).astype(jnp.float32) for z in (r, k, v, a)]
    y = wkv7_scan(rh, heads(decay), kh, vh, -kk, kk * ah)
    mean = jnp.mean(y, -1, keepdims=True)
    var = jnp.mean(jnp.square(y - mean), -1, keepdims=True)
    gn_g = ln_g.astype(jnp.float32).reshape(A_HEADS, A_HEAD_DIM)
    gn_b = ln_b.astype(jnp.float32).reshape(A_HEADS, A_HEAD_DIM)
    y = (y - mean) * lax.rsqrt(var + A_GN_EPS) * gn_g + gn_b
    y = y + jnp.sum(rh * kh * r_k.astype(jnp.float32), -1, keepdims=True) * vh
    y = y.reshape(bsz, seq, A_WIDTH) * g.astype(jnp.float32)
    return y.astype(h.dtype), v_first


def chunk_gated_delta_rule(q, k, v, beta, g):
    bsz, seq, nh, dk = q.shape
    dv = v.shape[-1]
    n = seq // CHUNK

    def to_chunks(z):
        z = z.astype(jnp.float32).reshape((bsz, n, CHUNK) + z.shape[2:])
        return jnp.moveaxis(z, 3, 1)

    q, k, v, beta, g = [to_chunks(z) for z in (q, k, v, beta, g)]
    gc = jnp.cumsum(g, -1)
    causal = jnp.tril(jnp.ones((CHUNK, CHUNK), bool))
    strict = jnp.tril(jnp.ones((CHUNK, CHUNK), bool), -1)
    decay = jnp.exp(jnp.where(causal, gc[..., :, None] - gc[..., None, :], -jnp.inf))
    kb = k * beta[..., None]
    amat = jnp.where(strict, jnp.einsum('bhncd,bhnsd->bhncs', kb, k) * decay, 0.0)
    rhs = jnp.concatenate([v * beta[..., None], kb * jnp.exp(gc)[..., None]], -1)
    sol = lax.linalg.triangular_solve(amat, rhs, left_side=True, lower=True, unit_diagonal=True)
    u, w = sol[..., :dv], sol[..., dv:]
    qk = jnp.einsum('bhncd,bhnsd->bhncs', q, k) * decay
    q_dec = q * jnp.exp(gc)[..., None]
    k_dec = k * jnp.exp(gc[..., -1:] - gc)[..., None]
    g_last = jnp.exp(gc[..., -1])
    xs = tuple(jnp.moveaxis(z, 2, 0) for z in (q_dec, k_dec, w, u, qk, g_last))

    def step(state, inp):
        qd, kd, wc, uc, qkc, gl = inp
        v_new = uc - jnp.einsum('bhcd,bhde->bhce', wc, state)
        o = jnp.einsum('bhcd,bhde->bhce', qd, state) + jnp.einsum('bhcs,bhse->bhce', qkc, v_new)
        state = state * gl[..., None, None] + jnp.einsum('bhcd,bhce->bhde', kd, v_new)
        return state, o

    s0 = jnp.zeros((bsz, nh, dk, dv), jnp.float32)
    _, o = lax.scan(step, s0, xs)
    return o.transpose(1, 0, 3, 2, 4).reshape(bsz, seq, nh, dv)


def gdn_group(qkv, gate, beta_logit, alpha_logit, conv_w, a_log, dt_bias, norm_g):
    bsz, seq, _ = qkv.shape
    qkv = jax.nn.silu(causal_depthwise_conv(qkv, conv_w))
    heads = lambda z: z.reshape(bsz, seq, B_HEADS, B_HEAD_DIM)
    q = l2norm(heads(qkv[..., :B_WIDTH])) * (B_HEAD_DIM ** -0.5)
    k = l2norm(heads(qkv[..., B_WIDTH:2 * B_WIDTH]))
    v = heads(qkv[..., 2 * B_WIDTH:])
    beta = jax.nn.sigmoid(beta_logit.astype(jnp.float32))
    g = -jnp.exp(a_log.astype(jnp.float32)) * jax.nn.softplus((alpha_logit + dt_bias).astype(jnp.float32))
    o = chunk_gated_delta_rule(q, k, v, beta, g)
    o = rmsnorm(o, norm_g) * jax.nn.silu(heads(gate).astype(jnp.float32))
    return o.reshape(bsz, seq, B_WIDTH).astype(qkv.dtype)


def chunk_band_attention(h, w_qkv, q_g, k_g, rel_bias):
    bsz, seq, _ = h.shape
    qkv = (h @ w_qkv).reshape(bsz, seq, 3, C_HEADS, C_HEAD_DIM)
    q = rmsnorm(qkv[:, :, 0], q_g) * (C_HEAD_DIM ** -0.5)
    k = rmsnorm(qkv[:, :, 1], k_g)
    v = qkv[:, :, 2]
    pad = C_WINDOW_CHUNKS * CHUNK
    band = pad + CHUNK
    kp = jnp.pad(k, ((0, 0), (pad, 0), (0, 0), (0, 0)))
    vp = jnp.pad(v, ((0, 0), (pad, 0), (0, 0), (0, 0)))
    rel = jnp.arange(CHUNK)[:, None] + pad - jnp.arange(band)[None, :]
    bias = rel_bias[:, jnp.clip(rel, -C_MAX_REL, C_MAX_REL) + C_MAX_REL].astype(jnp.float32)

    def one_chunk(c):
        start = c * CHUNK
        q_c = lax.dynamic_slice_in_dim(q, start, CHUNK, axis=1)
        k_b = lax.dynamic_slice_in_dim(kp, start, band, axis=1)
        v_b = lax.dynamic_slice_in_dim(vp, start, band, axis=1)
        s = jnp.einsum('bqhd,bkhd->bhqk', q_c, k_b).astype(jnp.float32) + bias
        valid = (start - pad + jnp.arange(band)) >= 0
        s = jnp.where(valid, s, -jnp.inf)
        prob = jax.nn.softmax(s, -1).astype(v.dtype)
        return jnp.einsum('bhqk,bkhd->bqhd', prob, v_b)

    o = lax.map(one_chunk, jnp.arange(seq // CHUNK))
    return jnp.moveaxis(o, 0, 1).reshape(bsz, seq, C_WIDTH)


def setup_inputs(seed: int = 0) -> dict:
    key = jax.random.key(seed)
    keys = jax.random.split(key, 48)
    ctr = [0]

    def nk():
        kk = keys[ctr[0]]
        ctr[0] += 1
        return kk

    def nrm(shape, scale=1.0):
        return scale * jax.random.normal(nk(), shape, jnp.float32)

    def unif(shape, lo, hi):
        return jax.random.uniform(nk(), shape, jnp.float32, lo, hi)

    D, E, O, EV = D_MODEL, N_EVEN, N_ODD, N_EVEN - 1
    dt = jnp.exp(unif((E, B_HEADS), math.log(1e-3), math.log(1e-1)))
    return {
        'x': nrm((BATCH, SEQ, D)),
        'p': nrm((DEPTH, BATCH, SEQ, PLE_DIM)),
        'norm_mix_g': 1.0 + nrm((DEPTH, D), 0.05),
        'norm_ffn_g': 1.0 + nrm((DEPTH, D), 0.05),
        'even_w_in': nrm((E, D, EVEN_IN), D ** -0.5),
        'rwkv_mu_proj': unif((E, 3, A_WIDTH), 0.0, 1.0),
        'rwkv_mu_lora': unif((E, 3, D), 0.0, 1.0),
        'rwkv_w0': unif((E, A_WIDTH), -6.0, 0.0),
        'rwkv_w1': nrm((E, D, A_DECAY_LORA), D ** -0.5),
        'rwkv_w2': nrm((E, A_DECAY_LORA, A_WIDTH), 0.5 * A_DECAY_LORA ** -0.5),
        'rwkv_a0': nrm((E, A_WIDTH), 0.5),
        'rwkv_a1': nrm((E, D, A_ICLR_LORA), D ** -0.5),
        'rwkv_a2': nrm((E, A_ICLR_LORA, A_WIDTH), 0.5 * A_ICLR_LORA ** -0.5),
        'rwkv_g1': nrm((E, D, A_GATE_LORA), D ** -0.5),
        'rwkv_g2': nrm((E, A_GATE_LORA, A_WIDTH), A_GATE_LORA ** -0.5),
        'rwkv_k_k': 0.85 + nrm((E, A_WIDTH), 0.05),
        'rwkv_k_a': 1.0 + nrm((E, A_WIDTH), 0.05),
        'rwkv_r_k': nrm((E, A_HEADS, A_HEAD_DIM), 0.1),
        'rwkv_ln_g': 1.0 + nrm((E, A_WIDTH), 0.05),
        'rwkv_ln_b': nrm((E, A_WIDTH), 0.02),
        'rwkv_v_mu': unif((EV, D), 0.0, 1.0),
        'rwkv_v0': nrm((EV, A_WIDTH), 0.5),
        'rwkv_v1': nrm((EV, D, A_VRES_LORA), D ** -0.5),
        'rwkv_v2': nrm((EV, A_VRES_LORA, A_WIDTH), 0.5 * A_VRES_LORA ** -0.5),
        'gdn_conv_w': nrm((E, B_CONV, 3 * B_WIDTH), B_CONV ** -0.5),
        'gdn_a_log': jnp.log(unif((E, B_HEADS), 1.0, 16.0)),
        'gdn_dt_bias': dt + jnp.log(-jnp.expm1(-dt)),
        'gdn_norm_g': 1.0 + nrm((E, B_HEAD_DIM), 0.05),
        'even_w_out': nrm((E, MIX_WIDTH, D), MIX_WIDTH ** -0.5),
        'attn_w_qkv': nrm((O, D, 3 * C_WIDTH), D ** -0.5),
        'attn_q_g': 1.0 + nrm((O, C_HEAD_DIM), 0.05),
        'attn_k_g': 1.0 + nrm((O, C_HEAD_DIM), 0.05),
        'attn_rel_bias': nrm((O, C_HEADS, 2 * C_MAX_REL + 1), 0.5),
        'attn_w_out': nrm((O, C_WIDTH, D), C_WIDTH ** -0.5),
        'mlp_w1': nrm((DEPTH, D, D_FF), D ** -0.5),
        'mlp_w2': nrm((DEPTH, D_FF, D), D_FF ** -0.5),
        'ple_w_proj': nrm((DEPTH, PLE_DIM, D), PLE_DIM ** -0.5),
        'ple_norm_g': 1.0 + nrm((DEPTH, D), 0.05),
        'ple_w_gate': nrm((DEPTH, D, D), D ** -0.5),
    }


def reference(x, p, norm_mix_g, norm_ffn_g, even_w_in, rwkv_mu_proj, rwkv_mu_lora, rwkv_w0, rwkv_w1, rwkv_w2,
              rwkv_a0, rwkv_a1, rwkv_a2, rwkv_g1, rwkv_g2, rwkv_k_k, rwkv_k_a, rwkv_r_k, rwkv_ln_g, rwkv_ln_b,
              rwkv_v_mu, rwkv_v0, rwkv_v1, rwkv_v2, gdn_conv_w, gdn_a_log, gdn_dt_bias, gdn_norm_g, even_w_out,
              attn_w_qkv, attn_q_g, attn_k_g, attn_rel_bias, attn_w_out, mlp_w1, mlp_w2,
              ple_w_proj, ple_norm_g, ple_w_gate):
    o_ak, o_av, o_bq = A_WIDTH, 2 * A_WIDTH, 3 * A_WIDTH
    o_bg = o_bq + 3 * B_WIDTH
    o_bb = o_bg + B_WIDTH
    o_ba = o_bb + B_HEADS
    v_first = None
    for i in range(DEPTH):
        h = rmsnorm(x, norm_mix_g[i])
        if i % 2 == 0:
            e = i // 2
            proj = h @ even_w_in[e]
            dh = token_shift(h) - h
            vres = None if e == 0 else (rwkv_v_mu[e - 1], rwkv_v0[e - 1], rwkv_v1[e - 1], rwkv_v2[e - 1])
            y_a, v_first = rwkv7_group(h, dh, proj[..., :o_ak], proj[..., o_ak:o_av], proj[..., o_av:o_bq], v_first,
                                       rwkv_mu_proj[e], rwkv_mu_lora[e], rwkv_w0[e], rwkv_w1[e], rwkv_w2[e],
                                       rwkv_a0[e], rwkv_a1[e], rwkv_a2[e], rwkv_g1[e], rwkv_g2[e],
                                       rwkv_k_k[e], rwkv_k_a[e], rwkv_r_k[e], rwkv_ln_g[e], rwkv_ln_b[e], vres)
            y_b = gdn_group(proj[..., o_bq:o_bg], proj[..., o_bg:o_bb], proj[..., o_bb:o_ba], proj[..., o_ba:],
                            gdn_conv_w[e], gdn_a_log[e], gdn_dt_bias[e], gdn_norm_g[e])
            mix = jnp.concatenate([y_a, y_b], -1) @ even_w_out[e]
        else:
            o = i // 2
            mix = chunk_band_attention(h, attn_w_qkv[o], attn_q_g[o], attn_k_g[o], attn_rel_bias[o]) @ attn_w_out[o]
        x = x + mix
        hf = rmsnorm(x, norm_ffn_g[i])
        x = x + jnp.square(jax.nn.relu(hf @ mlp_w1[i])) @ mlp_w2[i]
        x = x + rmsnorm(p[i] @ ple_w_proj[i], ple_norm_g[i]) * jax.nn.sigmoid(x @ ple_w_gate[i])
    return x
```

```python
import math
import contextlib
import numpy as np
import concourse.bass as bass
import concourse.mybir as mybir
from concourse.bass_utils import run_bass_kernel_spmd

F32 = mybir.dt.float32
BF16 = mybir.dt.bfloat16
AF = mybir.ActivationFunctionType
ALU = mybir.AluOpType
AX = mybir.AxisListType

SEM_LIMIT = 16000
N_DMA_SEMS = 10


class V:
    __slots__ = ("ap", "keys")

    def __init__(self, ap, keys):
        self.ap = ap
        self.keys = tuple(keys)

    def __getitem__(self, idx):
        return V(self.ap[idx], self.keys)

    def k(self, *sub):
        return V(self.ap, tuple((k0,) + tuple(sub) for k0 in self.keys))

    def re(self, pat, **kw):
        return V(self.ap.rearrange(pat, **kw), self.keys)

    def bc(self, shape):
        return V(self.ap.to_broadcast(shape), self.keys)


class Prog:
    ENGS = ("pe", "act", "dve", "pool", "sp")

    def __init__(self, nc):
        self.nc = nc
        self.es = contextlib.ExitStack()
        self.gs = contextlib.ExitStack()
        self.prefix = ""
        self.barrier = {}
        self.phase_dma = set()
        self.q = {e: [] for e in self.ENGS}
        self.cnt = {e: 0 for e in self.ENGS}
        self.waited = {e: {} for e in self.ENGS}
        self.last_w = {}
        self.readers = {}
        self.sems = {}
        self.dma_sems = {}
        self.dma_cnt = {}
        self.dma_rr = {e: 0 for e in self.ENGS}
        self.nbuf = 0
        self.out_tokens = []

    def sb(self, name, shape, dtype):
        name = self.prefix + name
        t = self.es.enter_context(self.nc.sbuf_tensor(name, list(shape), dtype))
        return V(t.ap() if hasattr(t, "ap") and callable(t.ap) else t[:], (name,))

    def ps(self, name, shape, dtype):
        name = self.prefix + name
        t = self.es.enter_context(self.nc.psum_tensor(name, list(shape), dtype))
        return V(t.ap() if hasattr(t, "ap") and callable(t.ap) else t[:], (name,))

    def dram(self, name, shape, dtype, kind):
        t = self.nc.dram_tensor(name, list(shape), dtype, kind=kind)
        return V(t.ap(), ("dram:" + name,))

    def _sem(self, name):
        if name not in self.sems:
            self.sems[name] = self.gs.enter_context(self.nc.semaphore(name))
        return self.sems[name]

    def _token(self, eng):
        self.cnt[eng] += 1
        i = self.cnt[eng]
        ep = (i - 1) // SEM_LIMIT
        return ("c", eng, ep, (i - 1) % SEM_LIMIT + 1)

    def _dma_token(self, eng):
        j = self.dma_rr[eng] % N_DMA_SEMS
        self.dma_rr[eng] += 1
        name = "d_%s_%d" % (eng, j)
        n = self.dma_cnt.get(name, 0) + 1
        self.dma_cnt[name] = n
        ep = (n - 1) // (SEM_LIMIT // 16)
        nn = (n - 1) % (SEM_LIMIT // 16) + 1
        prev = None
        if nn > 1:
            prev = ("d", name, ep, (nn - 1) * 16)
        return ("d", name, ep, nn * 16), prev

    def _deps(self, reads, writes):
        deps = set()
        for k in reads:
            if k in self.last_w:
                deps.add(self.last_w[k])
        for k in writes:
            if k in self.last_w:
                deps.add(self.last_w[k])
            for r in self.readers.get(k, ()):
                deps.add(r)
        return deps

    def _commit(self, tok, reads, writes):
        for k in writes:
            self.last_w[k] = tok
            self.readers[k] = []
        for k in reads:
            if k not in writes:
                self.readers.setdefault(k, []).append(tok)

    def _waits(self, eng, deps, force=()):
        out = []
        w = self.waited[eng]
        best = {}
        for d in deps:
            kind, nm, ep, val = d
            if kind == "c" and nm == eng and eng == "pe" and d not in force:
                continue
            key = (kind, nm, ep)
            if w.get(key, 0) >= val:
                continue
            if best.get(key, 0) < val:
                best[key] = val
        for key, val in best.items():
            w[key] = val
            out.append((key, val))
        return out

    def op(self, eng, fn, reads, writes, after=None):
        reads = [k for v in reads for k in (v.keys if isinstance(v, V) else (v,))]
        writes = [k for v in writes for k in (v.keys if isinstance(v, V) else (v,))]
        deps = self._deps(reads, writes)
        force = ()
        if after is not None:
            deps.add(after)
            force = (after,)
        if eng in self.barrier:
            b = self.barrier.pop(eng)
            deps |= b
            force = tuple(force) + tuple(b)
        waits = self._waits(eng, deps, force)
        tok = self._token(eng)
        self.q[eng].append((waits, fn, tok))
        self._commit(tok, reads, writes)
        return tok

    def dma(self, eng, out, in_, **kw):
        reads = list(in_.keys)
        writes = list(out.keys)
        deps = self._deps(reads, writes)
        tok, prev = self._dma_token(eng)
        if prev is not None:
            deps.add(prev)
        if eng in self.barrier:
            deps |= self.barrier.pop(eng)
        waits = self._waits(eng, deps)
        o, i = out.ap, in_.ap
        self.q[eng].append((waits, lambda e: e.dma_start(out=o, in_=i, **kw), tok))
        self._commit(tok, reads, writes)
        self.phase_dma.add(tok)
        if any(k.startswith("dram:") for k in writes if isinstance(k, str)):
            self.out_tokens.append(tok)
        return tok

    def cc(self, kind, op, groups, in_, out, inc=1):
        reads = list(in_.keys)
        writes = list(out.keys)
        deps = self._deps(reads, writes)
        if inc == 16:
            tok, prev = self._dma_token("pool")
        else:
            self.ccn = getattr(self, "ccn", 0) + 1
            tok, prev = ("k", "cc", 0, self.ccn), (("k", "cc", 0, self.ccn - 1) if self.ccn > 1 else None)
        if prev is not None:
            deps.add(prev)
        waits = self._waits("pool", deps)
        o, i = out.ap, in_.ap
        self.q["pool"].append((waits, lambda e: e.collective_compute(kind, op, replica_groups=groups, ins=[i], outs=[o]), tok))
        self._commit(tok, reads, writes)
        self.phase_dma.add(tok)
        return tok

    def mm(self, out, lhsT, rhs, start=True, stop=True, after=None):
        o, l, r = out.ap, lhsT.ap, rhs.ap
        rd = [lhsT, rhs] + ([] if start else [out])
        return self.op("pe", lambda e: e.matmul(o, l, r, start=start, stop=stop), rd, [out], after=after)

    def tr(self, out, in_, ident):
        o, i, d = out.ap, in_.ap, ident.ap
        return self.op("pe", lambda e: e.transpose(o, i, d), [in_, ident], [out])

    def act(self, out, in_, func, bias=None, scale=1.0, accum=None, eng="act"):
        o, i = out.ap, in_.ap
        rd = [in_]
        kw = {}
        if bias is not None:
            if isinstance(bias, V):
                rd.append(bias)
                kw["bias"] = bias.ap
            else:
                kw["bias"] = bias
        if isinstance(scale, V):
            rd.append(scale)
            kw["scale"] = scale.ap
        else:
            kw["scale"] = scale
        wr = [out]
        if accum is not None:
            kw["accum_out"] = accum.ap
            wr.append(accum)
        return self.op("act", lambda e: e.activation(o, i, func, **kw), rd, wr)

    def tt(self, out, a, b, op, eng="dve"):
        o, x, y = out.ap, a.ap, b.ap
        return self.op(eng, lambda e: e.tensor_tensor(o, x, y, op), [a, b], [out])

    def ts(self, out, a, s1, op0, s2=None, op1=None, eng="dve", accum=None):
        o, x = out.ap, a.ap
        rd = [a]
        if isinstance(s1, V):
            rd.append(s1)
            s1 = s1.ap
        if isinstance(s2, V):
            rd.append(s2)
            s2 = s2.ap
        wr = [out]
        kw = {}
        if accum is not None:
            kw["accum_out"] = accum.ap
            wr.append(accum)
        if op1 is None:
            return self.op(eng, lambda e: e.tensor_scalar(o, x, s1, None, op0, **kw), rd, wr)
        return self.op(eng, lambda e: e.tensor_scalar(o, x, s1, s2, op0, op1, **kw), rd, wr)

    def stt(self, out, a, s, b, op0, op1, eng="dve"):
        o, x, y = out.ap, a.ap, b.ap
        rd = [a, b]
        if isinstance(s, V):
            rd.append(s)
            s = s.ap
        return self.op(eng, lambda e: e.scalar_tensor_tensor(o, x, s, y, op0, op1), rd, [out])

    def copy(self, out, in_, eng="dve"):
        o, i = out.ap, in_.ap
        if eng == "act":
            return self.op("act", lambda e: e.copy(o, i), [in_], [out])
        return self.op(eng, lambda e: e.tensor_copy(o, i), [in_], [out])

    def memset(self, out, val, eng="pool"):
        o = out.ap
        return self.op(eng, lambda e: e.memset(o, val), [], [out])

    def red(self, out, in_, op=None, eng="dve"):
        o, i = out.ap, in_.ap
        op = op or ALU.add
        return self.op(eng, lambda e: e.tensor_reduce(o, i, AX.X, op), [in_], [out])

    def _semh(self, key):
        kind, nm, ep = key
        return self._sem("%s_%s_%d" % (kind, nm, ep))

    def flush(self, final=False):
        nc = self.nc
        fin = self._waits("sp", set(self.out_tokens)) if final else []
        for e in self.ENGS:
            for waits, fn, tok in self.q[e]:
                self._semh(tok[:3])
                for key, val in waits:
                    self._semh(key)
        for key, val in fin:
            self._semh(key)
        qs = self.q
        semh = self._semh

        def run(eng_name):
            def body(e):
                for waits, fn, tok in qs[eng_name]:
                    for key, val in waits:
                        e.wait_ge(semh(key), val)
                    ins = fn(e)
                    ins.then_inc(semh(tok[:3]), 16 if tok[0] == "d" else 1)
                if eng_name == "sp":
                    for key, val in fin:
                        e.wait_ge(semh(key), val)
            return body

        with nc.Block() as block:
            block.tensor(run("pe"))
            block.scalar(run("act"))
            block.vector(run("dve"))
            block.gpsimd(run("pool"))
            block.sync(run("sp"))
        self.q = {e: [] for e in self.ENGS}

    def end_phase(self, next_prefix):
        toks = set(self.phase_dma)
        for e in ("pe", "act", "dve", "pool"):
            if self.cnt[e] > 0:
                i = self.cnt[e]
                toks.add(("c", e, (i - 1) // SEM_LIMIT, (i - 1) % SEM_LIMIT + 1))
        self.flush()
        self.es.close()
        self.es = contextlib.ExitStack()
        self.phase_dma = set()
        self.barrier = {e: set(toks) for e in self.ENGS}
        self.prefix = next_prefix

    def emit(self):
        self.flush(final=True)
        self.es.close()
        self.gs.close()


EPS = 1e-6


def load_w(P, dst, src, K, ncols, stage, ci):
    sv = src.re("(k p) n -> p k n", p=128)
    engs = ("dve", "pool", "act")
    step = stage[0].ap.shape[-1]
    for k in range(K):
        for c0 in range(0, ncols, step):
            c1 = min(ncols, c0 + step)
            st = stage[ci[0] % len(stage)]
            P.dma("sp", st[:, 0:c1 - c0], sv[:, k, c0:c1])
            P.copy(dst[:, k, c0:c1], st[:, 0:c1 - c0], eng=engs[ci[0] % 3])
            ci[0] += 1


def rms_feat(P, x1, sq, ssps, rstd, ones, K, N, scale_dim):
    P.act(sq, x1, AF.Square)
    for k in range(K):
        P.mm(ssps, ones, sq[:, k, :], start=(k == 0), stop=(k == K - 1))
    P.act(rstd, ssps, AF.Sqrt, bias=EPS, scale=1.0 / scale_dim)
    P.op("dve", lambda e: e.reciprocal(rstd.ap, rstd.ap), [rstd], [rstd])


POST_IN = {"gf": [128, 8], "gp": [128, 8], "w1": [1024, 4096], "w2": [4096, 1024], "wg": [1024, 1024], "wp": [256, 1024]}


def build_post(NT=4096, N=128):
    nc = bass.Bass("TRN2", target_bir_lowering=False)
    P = Prog(nc)
    D, PD = 1024, 256
    io = {k: P.dram(k, sh, F32, "ExternalInput") for k, sh in POST_IN.items()}
    for k in ("xT", "m0", "m1"):
        io[k] = P.dram(k, [D, NT], F32, "ExternalInput")
    io["pT"] = P.dram("pT", [PD, NT], F32, "ExternalInput")
    io["oT"] = P.dram("oT", [D, NT], F32, "ExternalOutput")
    phase_post(P, io, NT, N)
    P.emit()
    return nc


def phase_post(P, io, NT=4096, N=128):
    D, FF, PD = 1024, 4096, 256
    pT, gf, gp, w1, w2, wg, wp = (io[k] for k in ("pT", "gf", "gp", "w1", "w2", "wg", "wp"))
    xT, m0, oT = io.get("xT"), io.get("m0"), io.get("oT")
    m1 = io.get("m1")

    w1b = P.sb("w1b", [128, 8, FF], BF16)
    w2b = P.sb("w2b", [128, 32, D], BF16)
    wgb = P.sb("wgb", [128, 8, D], BF16)
    wpb = P.sb("wpb", [128, 2, D], BF16)
    stage = [P.sb("stg%d" % i, [128, 2048], F32) for i in range(2)]
    gfs = P.sb("gfs", [128, 8], F32)
    gps = P.sb("gps", [128, 8], F32)
    ones = P.sb("ones", [128, 128], BF16)
    x1 = P.sb("x1", [128, 8, N], F32)
    ma = P.sb("ma", [128, 8, N], F32)
    mb = P.sb("mb", [128, 8, N], F32)
    sq = P.sb("sq", [128, 8, N], BF16)
    h = P.sb("h", [128, 8, N], BF16)
    tmp = P.sb("tmp", [128, 8, N], F32)
    a1 = P.sb("a1", [128, 32, N], BF16)
    rl = [P.sb("rl%d" % i, [128, N], F32) for i in range(2)]
    x2b = P.sb("x2b", [128, 8, N], BF16)
    sg = P.sb("sg", [128, 8, N], F32)
    ee = P.sb("ee", [128, 8, N], F32)
    rstd = P.sb("rstd", [128, N], F32)
    rstd2 = P.sb("rstd2", [128, N], F32)
    pst = P.sb("pst", [128, 2, N], F32)
    pbb = P.sb("pbb", [128, 2, N], BF16)
    pb = [P.ps("pb%d" % i, [128, 512], F32) for i in range(8)]

    P.memset(ones, 1.0)
    P.dma("sp", gfs, gf)
    P.dma("sp", gps, gp)
    ci = [0]
    load_w(P, w1b, w1, 8, FF, stage, ci)
    load_w(P, w2b, w2, 32, D, stage, ci)
    load_w(P, wgb, wg, 8, D, stage, ci)
    load_w(P, wpb, wp, 2, D, stage, ci)

    _acc = lambda v: (lambda t0, n, _v=v.re("(k p) t -> p k t", p=128): _v[:, :, t0:t0 + n])
    x_at = io.get("x_at") or _acc(xT)
    m0_at = io.get("m0_at") or _acc(m0)
    o_at = io.get("o_at") or _acc(oT)
    m1v = m1.re("(k p) t -> p k t", p=128) if m1 is not None else None
    pv = pT.re("(k p) t -> p k t", p=128)
    gfb = V(gfs.ap.unsqueeze(2).to_broadcast([128, 8, N]), gfs.keys)
    gpb = V(gps.ap.unsqueeze(2).to_broadcast([128, 8, N]), gps.keys)

    def bcN(r):
        return V(r.ap.unsqueeze(1).to_broadcast([128, 8, N]), r.keys)

    for t in range(NT // N):
        ts_ = slice(t * N, (t + 1) * N)
        P.dma("sp", x1, x_at(t * N, N))
        P.dma("sp", ma, m0_at(t * N, N))
        if m1v is not None:
            P.dma("sp", mb, m1v[:, :, ts_])
        P.dma("sp", pst, pv[:, :, ts_])
        P.tt(x1, x1, ma, ALU.add)
        if m1v is not None:
            P.tt(x1, x1, mb, ALU.add, eng="pool")
        rms_feat(P, x1, sq, pb[0][:, 0:N], rstd, ones, 8, N, 1024.0)
        P.tt(tmp, x1, gfb, ALU.mult, eng="pool")
        P.tt(h, tmp, bcN(rstd), ALU.mult)
        for fc in range(32):
            pu = pb[1 + fc % 2][:, 0:N]
            for k in range(8):
                P.mm(pu, w1b[:, k, fc * 128:(fc + 1) * 128], h[:, k, :], start=(k == 0), stop=(k == 7))
            r = rl[fc % 2]
            P.act(r, pu, AF.Relu)
            P.tt(a1[:, fc, :], r, r, ALU.mult, eng=("pool" if fc % 2 else "dve"))
        for dc in range(8):
            pd = pb[3 + dc % 2][:, 0:N]
            for fk in range(32):
                P.mm(pd, w2b[:, fk, dc * 128:(dc + 1) * 128], a1[:, fk, :], start=(fk == 0), stop=(fk == 31))
            P.tt(x1[:, dc, :], x1[:, dc, :], pd, ALU.add)
        P.copy(x2b, x1, eng="pool")
        for dc in range(8):
            pg = pb[5 + dc % 2][:, 0:N]
            for k in range(8):
                P.mm(pg, wgb[:, k, dc * 128:(dc + 1) * 128], x2b[:, k, :], start=(k == 0), stop=(k == 7))
            P.act(sg[:, dc, :], pg, AF.Sigmoid)
        P.copy(pbb, pst, eng="pool")
        for dc in range(8):
            pp = pb[7][:, (dc % 4) * N:(dc % 4 + 1) * N]
            for k in range(2):
                P.mm(pp, wpb[:, k, dc * 128:(dc + 1) * 128], pbb[:, k, :], start=(k == 0), stop=(k == 1))
            P.copy(ee[:, dc, :], pp, eng="dve")
        rms_feat(P, ee, sq, pb[0][:, 0:N], rstd2, ones, 8, N, 1024.0)
        P.tt(tmp, ee, gpb, ALU.mult, eng="pool")
        P.tt(tmp, tmp, bcN(rstd2), ALU.mult)
        P.tt(tmp, tmp, sg, ALU.mult, eng="pool")
        P.tt(tmp, tmp, x1, ALU.add)
        P.dma("pool", o_at(t * N, N), tmp)


ATTN_IN = {"gm": [128, 8], "wq": [1024, 512], "wk": [1024, 512], "wv": [1024, 512], "wo": [512, 1024],
           "qg": [128, 1], "kg": [128, 1], "bE": [128, 8 * 5 * 64], "bO": [128, 8 * 5 * 64]}


def build_attn(T=8192):
    nc = bass.Bass("TRN2", target_bir_lowering=False)
    P = Prog(nc)
    io = {k: P.dram(k, sh, F32, "ExternalInput") for k, sh in ATTN_IN.items()}
    io["xT"] = P.dram("xT", [1024, T], F32, "ExternalInput")
    io["mT"] = P.dram("mT", [1024, T], F32, "ExternalOutput")
    phase_attn(P, io, T)
    P.emit()
    return nc


def phase_attn(P, io, T=8192):
    N = 128
    D = 1024
    gm, wq, wk, wv, wo, qg, kg, bE, bO = (io[k] for k in ("gm", "wq", "wk", "wv", "wo", "qg", "kg", "bE", "bO"))
    xT, mT = io.get("xT"), io.get("mT")

    wqb = P.sb("wqb", [128, 8, 512], BF16)
    wkb = P.sb("wkb", [128, 8, 512], BF16)
    wvb = P.sb("wvb", [128, 8, 512], BF16)
    wob = P.sb("wob", [128, 4, D], BF16)
    stage = [P.sb("stg%d" % i, [128, 512], F32) for i in range(2)]
    gms = P.sb("gms", [128, 8], F32)
    qgs = P.sb("qgs", [128, 1], F32)
    kgs = P.sb("kgs", [128, 1], F32)
    bEs = P.sb("bEs", [128, 8, 5, 64], F32)
    bOs = P.sb("bOs", [128, 8, 5, 64], F32)
    ones = P.sb("ones", [128, 128], BF16)
    blk = P.sb("blk", [128, 128], BF16)
    kTa = P.sb("kTa", [128, 4, T], BF16)
    Vt = P.sb("Vt", [128, T // 128, 512], BF16)
    qTt = P.sb("qTt", [128, 4, N], BF16)
    x1 = P.sb("x1", [128, 8, N], F32)
    sq = P.sb("sq", [128, 8, N], BF16)
    h = P.sb("h", [128, 8, N], BF16)
    tmp = P.sb("tmp", [128, 8, N], F32)
    rstd = P.sb("rstd", [128, N], F32)
    raw = [P.sb("raw%d" % i, [128, N], F32) for i in range(2)]
    sq1 = [P.sb("sq1%d" % i, [128, N], BF16) for i in range(2)]
    rs1 = [P.sb("rs1%d" % i, [128, N], F32) for i in range(2)]
    sc = [P.sb("sc%d" % i, [128, 5, 64], F32) for i in range(2)]
    pT = [P.sb("pT%d" % i, [128, 5, 64], BF16) for i in range(2)]
    rden = [P.sb("rden%d" % i, [128, 64], F32) for i in range(2)]
    ao = P.sb("ao", [128, 4, N], BF16)
    mo = tmp
    pb = [P.ps("pb%d" % i, [128, 512], F32) for i in range(8)]

    P.memset(ones, 1.0)
    P.memset(blk, 0.0)
    P.memset(blk[0:64, 0:64], 1.0)
    P.memset(blk[64:128, 64:128], 1.0)
    P.dma("sp", gms, gm)
    P.dma("sp", qgs, qg)
    P.dma("sp", kgs, kg)
    P.dma("sp", bEs.re("p a b c -> p (a b c)"), bE)
    P.dma("sp", bOs.re("p a b c -> p (a b c)"), bO)
    P.ts(qgs, qgs, 0.125, ALU.mult)
    ci = [0]
    load_w(P, wqb, wq, 8, 512, stage, ci)
    load_w(P, wkb, wk, 8, 512, stage, ci)
    load_w(P, wvb, wv, 8, 512, stage, ci)
    load_w(P, wob, wo, 4, D, stage, ci)

    x_at = io.get("x_at") or (lambda t0, n, _v=xT.re("(k p) t -> p k t", p=128): _v[:, :, t0:t0 + n])
    m_at = io.get("m_at") or (lambda t0, n, _v=mT.re("(k p) t -> p k t", p=128): _v[:, :, t0:t0 + n])
    gmb = V(gms.ap.unsqueeze(2).to_broadcast([128, 8, N]), gms.keys)
    cnt = [0]

    for t in range(T // N):
        ts_ = slice(t * N, (t + 1) * N)
        P.dma("sp", x1, x_at(t * N, N))
        rms_feat(P, x1, sq, pb[0][:, 0:N], rstd, ones, 8, N, 1024.0)
        P.tt(tmp, x1, gmb, ALU.mult, eng="pool")
        P.tt(h, tmp, V(rstd.ap.unsqueeze(1).to_broadcast([128, 8, N]), rstd.keys), ALU.mult)
        for which in range(2):
            wb = wqb if which == 0 else wkb
            gs = qgs if which == 0 else kgs
            for g in range(4):
                i = cnt[0] % 2
                cnt[0] += 1
                pq = pb[1 + i][:, 0:N]
                for k in range(8):
                    P.mm(pq, wb[:, k, g * 128:(g + 1) * 128], h[:, k, :], start=(k == 0), stop=(k == 7))
                P.copy(raw[i], pq, eng="act")
                P.tt(sq1[i], raw[i], raw[i], ALU.mult, eng="pool")
                ps2 = pb[3 + i][:, 0:N]
                P.mm(ps2, blk, sq1[i])
                P.act(rs1[i], ps2, AF.Sqrt, bias=EPS, scale=1.0 / 64)
                r_ = rs1[i]
                P.op("dve", lambda e, r_=r_: e.reciprocal(r_.ap, r_.ap), [r_], [r_])
                dst = qTt[:, g, :] if which == 0 else kTa[:, g, ts_]
                P.stt(dst, raw[i], gs[:, 0:1], rs1[i], ALU.mult, ALU.mult)
        pv_ = pb[5]
        for k in range(8):
            P.mm(pv_, h[:, k, :], wvb[:, k, :], start=(k == 0), stop=(k == 7))
        P.copy(Vt[:, t, :], pv_, eng="act")
        for cc in range(2):
            c = 2 * t + cc
            qs = slice(cc * 64, (cc + 1) * 64)
            slots = []
            if c % 2 == 0:
                for s in range(4):
                    slots.append(((c - 8) // 2 + s, 0, 128))
                slots.append((c // 2, 0, 64))
                bias = bEs
            else:
                slots.append(((c - 9) // 2, 64, 128))
                for s in range(1, 5):
                    slots.append(((c - 9) // 2 + s, 0, 128))
                bias = bOs
            valid = [(s, b, p0, p1) for s, (b, p0, p1) in enumerate(slots) if b >= 0]
            s_lo = valid[0][0]
            for hh in range(8):
                g, off = hh // 2, (hh % 2) * 64
                i = cnt[0] % 2
                cnt[0] += 1
                pS = pb[6][:, (i * 256):(i * 256) + 320] if False else pb[6 + i][:, 0:320]
                pS3 = pS.re("p (s q) -> p s q", s=5)
                for (s, b, p0, p1) in valid:
                    P.mm(pS3[:, s, :], kTa[off:off + 64, g, b * 128:(b + 1) * 128], qTt[off:off + 64, g, qs])
                P.tt(sc[i][:, s_lo:5, :], pS3[:, s_lo:5, :], bias[:, hh, s_lo:5, :], ALU.add)
                P.act(pT[i][:, s_lo:5, :], sc[i][:, s_lo:5, :], AF.Exp)
                pden = pb[3 + i][:, 128:192]
                po = pb[3 + i][:, 192:256]
                for j, (s, b, p0, p1) in enumerate(valid):
                    P.mm(pden, ones[p0:p1, :], pT[i][p0:p1, s, :], start=(j == 0), stop=(j == len(valid) - 1))
                for j, (s, b, p0, p1) in enumerate(valid):
                    P.mm(po, Vt[p0:p1, b, g * 128:(g + 1) * 128], pT[i][p0:p1, s, :], start=(j == 0), stop=(j == len(valid) - 1))
                rd = rden[i]
                P.op("dve", lambda e, rd=rd, pden=pden: e.reciprocal(rd.ap, pden.ap), [pden], [rd])
                P.tt(ao[off:off + 64, g, qs], po[off:off + 64, :], rd[off:off + 64, :], ALU.mult)
        for dc in range(8):
            pm = pb[1 + dc % 2][:, 256:256 + N]
            for g in range(4):
                P.mm(pm, wob[:, g, dc * 128:(dc + 1) * 128], ao[:, g, :], start=(g == 0), stop=(g == 3))
            P.copy(mo[:, dc, :], pm, eng=("act" if dc % 2 else "dve"))
        P.dma("pool", m_at(t * N, N), mo)


def attn_bias_layouts(rel_bias8):
    H = rel_bias8.shape[0]
    p = np.arange(128)[:, None, None]
    s = np.arange(5)[None, :, None]
    q = np.arange(64)[None, None, :]
    kk = p % 64
    hi = (p >= 64).astype(np.int64)
    outs = []
    for odd in (0, 1):
        if not odd:
            m = 2 * s + hi
            ok = (m <= 8)
        else:
            m = 2 * s - 1 + hi
            ok = (m >= 0)
        m = np.clip(m, 0, 8) + 0 * q
        rel = q - kk + 64 * (8 - m)
        idx = np.clip(rel, -256, 256) + 256
        ok = np.broadcast_to(ok, idx.shape)
        g = rel_bias8[:, idx]
        g = np.where(ok[None], g, np.float32(0))
        outs.append(np.ascontiguousarray(np.transpose(g, (1, 0, 2, 3)).reshape(128, H * 5 * 64)).astype(np.float32))
    return outs


NEG = -30000.0


EVEN_IN = {"gm": [128, 8], "wr": [1024, 256], "wk": [1024, 256], "wv": [1024, 256], "wbq": [1024, 256], "wbk": [1024, 256],
           "wbv": [1024, 256], "wgate": [1024, 256], "wba": [1024, 4], "mup": [128, 6], "mul": [128, 32], "cols": [128, 16],
           "w1": [1024, 64], "a1": [1024, 64], "g1": [1024, 128], "v1": [1024, 32], "w2": [64, 256], "a2": [64, 256],
           "g2": [128, 256], "v2": [32, 256], "convw": [128, 24], "gcols": [128, 8], "wo": [512, 1024], "lvm": [64, 6 * 2 * 64]}


def build_even(T=8192, vres=False, stop=99):
    nc = bass.Bass("TRN2", target_bir_lowering=False)
    P = Prog(nc)
    io = {k: P.dram(k, sh, F32, "ExternalInput") for k, sh in EVEN_IN.items()}
    io["xT"] = P.dram("xT", [1024, T], F32, "ExternalInput")
    if vres:
        io["vfi"] = P.dram("vfi", [256, T], F32, "ExternalInput")
    else:
        io["vfo"] = P.dram("vfo", [256, T], F32, "ExternalOutput")
    io["mT"] = P.dram("mT", [1024, T], F32, "ExternalOutput")
    phase_even(P, io, T, vres)
    P.emit()
    return nc


def phase_even(P, io, T=8192, vres=False):
    stop = 99
    N = 256
    NC = N // 64
    HW = N + 3
    D = 1024
    xT = io.get("xT"); gm = io["gm"]
    w_rkv = [io["wr"], io["wk"], io["wv"]]
    w_qkv = [io["wbq"], io["wbk"], io["wbv"]]
    wgate = io["wgate"]; wba = io["wba"]; mup = io["mup"]; mul = io["mul"]; cols = io["cols"]
    lw1 = [io["w1"], io["a1"], io["g1"], io["v1"]]
    lw2 = [io["w2"], io["a2"], io["g2"], io["v2"]]
    convw = io["convw"]; gcols = io["gcols"]; wo = io["wo"]; lvm = io["lvm"]
    vfi = io.get("vfi"); vfo = io.get("vfo"); mT = io.get("mT")

    S = lambda name, shape, dt=F32: P.sb(name, shape, dt)
    stage = [S("stg%d" % i, [128, 1024]) for i in range(2)]
    wrkvb = [S("wrkvb%d" % i, [128, 8, 256], BF16) for i in range(3)]
    wqkvb = [S("wqkvb%d" % i, [128, 8, 256], BF16) for i in range(3)]
    wgateb = S("wgateb", [128, 8, 256], BF16)
    wbab = S("wbab", [128, 8, 4], BF16)
    wbar = S("wbar", [128, 8, 4, 128], BF16)
    lcols = [64, 64, 128, 32]
    l1b = [S("l1b%d" % i, [128, 8, lcols[i]], BF16) for i in range(4)]
    l1A = [S("l1A%d" % i, [128, 8, lcols[i]], BF16) for i in range(4)]
    l1B = [S("l1B%d" % i, [128, 8, lcols[i]], BF16) for i in range(4)]
    l2b = [S("l2b%d" % i, [128, 1, 256], BF16) for i in range(4)]
    wob = S("wob", [128, 4, D], BF16)
    gms = S("gms", [128, 8]); mups = S("mups", [128, 6]); omups = S("omups", [128, 6])
    muls = S("muls", [128, 32]); omuls = S("omuls", [128, 32])
    colss = S("colss", [128, 16]); convs = S("convs", [128, 24]); gcs = S("gcs", [128, 8])
    ones = S("ones", [128, 128], BF16); blk = S("blk", [128, 128], BF16); ident = S("ident", [128, 128], BF16)
    identf = S("identf", [128, 128])
    m5 = S("m5", [64, 5, 64]); I2 = S("I2", [64, 2, 64])
    lvms = S("lvms", [64, 6, 2, 64])
    nmU = S("nmU", [64, 64]); nmLs = S("nmLs", [64, 64]); sU01 = S("sU01", [64, 64])
    pb = [P.ps("pb%d" % i, [128, 512], F32) for i in range(7)]
    ptb = P.ps("ptb", [128, 1024], BF16)

    P.memset(ones, 1.0); P.memset(blk, 0.0)
    P.memset(blk[0:64, 0:64], 1.0); P.memset(blk[64:128, 64:128], 1.0)
    P.memset(identf, 1.0)
    idf = identf
    P.op("pool", lambda e: e.affine_select(idf.ap, idf.ap, [[-1, 128]], ALU.is_ge, 0.0, base=0, channel_multiplier=1), [idf], [idf])
    P.op("pool", lambda e: e.affine_select(idf.ap, idf.ap, [[1, 128]], ALU.is_ge, 0.0, base=0, channel_multiplier=-1), [idf], [idf])
    P.copy(ident, identf)
    def tri(dst, cmp, fill, init):
        P.memset(dst, init)
        d = dst
        sgn = 1
        if cmp == ALU.is_lt:
            cmp, sgn = ALU.is_gt, -1
        elif cmp == ALU.is_le:
            cmp, sgn = ALU.is_ge, -1
        P.op("pool", lambda e: e.affine_select(d.ap, d.ap, [[-sgn, 64]], cmp, fill, base=0, channel_multiplier=sgn), [d], [d])
    tri(m5[:, 0, :], ALU.is_gt, 0.0, 1.0)
    tri(m5[:, 1, :], ALU.is_lt, 0.0, 1.0)
    tri(m5[:, 2, :], ALU.is_lt, 0.0, 1.0)
    tri(m5[:, 3, :], ALU.is_le, 0.0, 1.0)
    tri(m5[:, 4, :], ALU.is_le, 0.0, 1.0)
    P.copy(I2[:, 0, :], identf[0:64, 0:64]); P.copy(I2[:, 1, :], identf[0:64, 0:64])
    tri(nmU, ALU.is_le, NEG, 0.0)
    tri(nmLs, ALU.is_gt, NEG, 0.0)
    tri(sU01, ALU.is_lt, 0.0, 1.0)

    P.dma("sp", lvms.re("p a b c -> p (a b c)"), lvm)
    for dst, src in ((gms, gm), (mups, mup), (muls, mul), (colss, cols), (convs, convw), (gcs, gcols)):
        P.dma("sp", dst, src)
    P.ts(omups, mups, -1.0, ALU.mult, 1.0, ALU.add)
    P.ts(omuls, muls, -1.0, ALU.mult, 1.0, ALU.add)
    ci = [0]
    for i in range(3):
        load_w(P, wrkvb[i], w_rkv[i], 8, 256, stage, ci)
        load_w(P, wqkvb[i], w_qkv[i], 8, 256, stage, ci)
    load_w(P, wgateb, wgate, 8, 256, stage, ci)
    load_w(P, wbab, wba, 8, 4, stage, ci)
    P.copy(wbar, V(wbab.ap.unsqueeze(3).to_broadcast([128, 8, 4, 128]), wbab.keys))
    nl = 4 if vres else 3
    for i in range(nl):
        load_w(P, l1b[i], lw1[i], 8, lcols[i], stage, ci)
        mi = i
        for k in range(8):
            P.ts(l1A[i][:, k, :], l1b[i][:, k, :], omuls[:, mi * 8 + k:mi * 8 + k + 1], ALU.mult, eng=("pool" if k % 2 else "dve"))
            P.ts(l1B[i][:, k, :], l1b[i][:, k, :], muls[:, mi * 8 + k:mi * 8 + k + 1], ALU.mult, eng=("dve" if k % 2 else "pool"))
        st = stage[ci[0] % 2]; ci[0] += 1
        P.dma("sp", st[0:lcols[i], 0:256], lw2[i])
        P.copy(l2b[i][0:lcols[i], 0, :], st[0:lcols[i], 0:256])
    load_w(P, wob, wo, 4, D, stage, ci)
    nea = S("nea", [128, 2])
    P.act(nea, gcs[:, 0:2], AF.Exp)
    P.ts(nea, nea, -1.0, ALU.mult)

    hT = S("hT", [128, 8, HW], BF16)
    x1 = S("x1", [128, 8, N]); sq = S("sq", [128, 8, N], BF16); tmp = S("tmp", [128, 8, N]); rstd = S("rstd", [128, N])
    P.memset(hT[:, :, 0:3], 0.0)
    FM = lambda name, dt=F32: [S("%s%d" % (name, g), [128, N], dt) for g in range(2)]
    rr, kr, vr = FM("rr"), FM("kr"), FM("vr")
    lw, aa, gg, bon = FM("lw"), FM("aa"), FM("gg"), FM("bon")
    kkn, k2 = FM("kkn"), FM("k2")
    t1, t2, t3 = S("t1", [128, N]), S("t2", [128, N]), S("t3", [128, N])
    tb = S("tb", [128, N], BF16)
    clA, clB = S("clA", [128, N]), S("clB", [128, N])
    Wt, Wi, Wp, Wd = S("Wt", [128, N]), S("Wi", [128, N]), S("Wp", [128, N]), S("Wd", [128, N])
    WC = [S("WC%d" % g, [128, NC]) for g in range(2)]
    rt, kt, at, bt = FM("rt", BF16), FM("kt", BF16), FM("at", BF16), FM("bt", BF16)
    bd, kd, vb = FM("bd", BF16), FM("kd", BF16), FM("vb", BF16)
    tok = [S("tok%d" % g, [64, NC, 3, 128], BF16) for g in range(2)]
    dl = [S("dl%d" % i, [128, N], BF16) for i in range(4)]
    yT = FM("yT")
    ycat = S("ycat", [128, 4, N], BF16)
    mo = S("mo", [128, 8, N])
    Sf = [S("Sf%d" % g, [128, 64]) for g in range(2)]
    Sb = [S("Sb%d" % g, [128, 128], BF16) for g in range(2)]
    for g in range(2):
        P.memset(Sf[g], 0.0); P.memset(Sb[g], 0.0)
    NI = 6
    AM = [S("AM%d" % i, [64, 5, 64], BF16) for i in range(NI)]
    AB = [S("AB%d" % i, [64, 2, 64], BF16) for i in range(NI)]
    LL = [S("LL%d" % i, [64, 6, 2, 64], BF16) for i in range(NI)]
    PQ = [S("PQ%d" % i, [64, 2, 64], BF16) for i in range(NI)]
    Xs = [S("Xs%d" % i, [64, 128], BF16) for i in range(NI)]
    Zs = [S("Zs%d" % i, [64, 128], BF16) for i in range(NI)]
    for i in range(NI):
        P.memset(Zs[i], 0.0)
    qn, kn, qd = FM("qn", BF16), FM("kn", BF16), FM("qd", BF16)
    vg, sgate = FM("vg"), FM("sgate")
    vgb = FM("vgb", BF16)
    betab, gcb, egc = FM("betab"), FM("gcb"), FM("egc")
    gcol = [S("gcol%d" % g, [64, NC]) for g in range(2)]
    bcol = [S("bcol%d" % g, [64, NC]) for g in range(2)]
    nbw = [S("nbw%d" % g, [64, NC]) for g in range(2)]
    dcol = [S("dcol%d" % g, [64, NC]) for g in range(2)]
    egl = [S("egl%d" % g, [128, NC]) for g in range(2)]
    M3 = [S("M3%d" % g, [64, NC, 3, 64]) for g in range(2)]
    d3 = S("d3", [64, NC, 64]); d3b = S("d3b", [64, NC, 64])
    ktok = [S("ktok%d" % g, [64, NC, 128], BF16) for g in range(2)]
    bvf = [S("bvf%d" % g, [64, NC, 128]) for g in range(2)]
    Gf = [S("Gf%d" % g, [128, 128]) for g in range(2)]
    Gb = [S("Gb%d" % g, [128, 128], BF16) for g in range(2)]
    for g in range(2):
        P.memset(Gf[g], 0.0); P.memset(Gb[g], 0.0)
    yg = FM("yg")

    x_at = io.get("x_at") or (lambda t0, n, _v=xT.re("(k p) t -> p k t", p=128): _v[:, :, t0:t0 + n])
    m_at = io.get("m_at") or (lambda t0, n, _v=mT.re("(k p) t -> p k t", p=128): _v[:, :, t0:t0 + n])
    gmb = V(gms.ap.unsqueeze(2).to_broadcast([128, 8, N]), gms.keys)
    C = lambda g, j: colss[:, 2 * j + g:2 * j + g + 1]
    cur = slice(3, HW); prv = slice(2, HW - 1)
    rot = [0]

    def bank():
        rot[0] += 1
        return pb[rot[0] % 7]

    def v3(x):
        return x.re("p (c t) -> p c t", c=NC)

    def cumsum(src, dA, dB, np_=128):
        a = v3(src)
        bufs = [v3(dA), v3(dB)]
        i = 0
        for s in (1, 2, 4, 8, 16, 32):
            d = bufs[i % 2]
            P.tt(d[0:np_, :, s:], a[0:np_, :, s:], a[0:np_, :, :64 - s], ALU.add)
            P.copy(d[0:np_, :, :s], a[0:np_, :, :s], eng="pool")
            a = d
            i += 1
        return dB if i % 2 == 0 else dA

    def rsq(dst, ps, scale, eps):
        P.act(dst, ps, AF.Sqrt, bias=eps, scale=scale)
        P.op("dve", lambda e: e.reciprocal(dst.ap, dst.ap), [dst], [dst])

    def levels(insts):
        for i in insts:
            src = V(AM[i].ap[:, 0:2, :].unsqueeze(1).to_broadcast([64, 6, 2, 64]), AM[i].keys)
            P.tt(LL[i], src, lvms, ALU.mult, eng=("pool" if i % 2 else "dve"))
        for i in insts:
            P.tt(PQ[i], LL[i][:, 0, :, :], I2, ALU.add)
        for lvl in range(1, 6):
            pls = {}
            for i in insts:
                pl = bank()[0:64, 0:256].re("p (s q) -> p s q", s=4)
                pls[i] = pl
                P.mm(pl[:, 0, :], LL[i][:, lvl, 1, :], PQ[i][:, 0, :])
                P.mm(pl[:, 1, :], LL[i][:, lvl, 0, :], PQ[i][:, 1, :])
            for i in insts:
                P.copy(AB[i], pls[i][:, 0:2, :], eng="act")
            for i in insts:
                P.mm(pls[i][:, 2, :], PQ[i][:, 1, :], AB[i][:, 0, :])
                P.mm(pls[i][:, 3, :], PQ[i][:, 0, :], AB[i][:, 1, :])
            for i in insts:
                P.tt(PQ[i], PQ[i], pls[i][:, 2:4, :], ALU.add)

    for t in range(T // N):
        ts_ = slice(t * N, (t + 1) * N)
        P.dma("sp", x1, x_at(t * N, N))
        rms_feat(P, x1, sq, pb[0][:, 0:N], rstd, ones, 8, N, 1024.0)
        P.tt(tmp, x1, gmb, ALU.mult, eng="pool")
        P.tt(hT[:, :, cur], tmp, V(rstd.ap.unsqueeze(1).to_broadcast([128, 8, N]), rstd.keys), ALU.mult)
        for i in range(nl):
            pd_ = bank()[0:lcols[i], 0:N]
            for k in range(8):
                P.mm(pd_, l1A[i][:, k, :], hT[:, k, cur], start=(k == 0), stop=False)
                P.mm(pd_, l1B[i][:, k, :], hT[:, k, prv], start=False, stop=(k == 7))
            if i == 0:
                P.act(dl[i][0:lcols[i], :], pd_, AF.Tanh)
            elif i == 2:
                P.act(dl[i][0:lcols[i], :], pd_, AF.Sigmoid)
            else:
                P.copy(dl[i][0:lcols[i], :], pd_, eng="act")
        for g in range(2):
            gs = slice(g * 128, (g + 1) * 128)
            for j, dst in enumerate((rr[g], kr[g], vr[g])):
                pz = bank()[:, 0:HW]
                for k in range(8):
                    P.mm(pz, wrkvb[j][:, k, gs], hT[:, k, :], start=(k == 0), stop=(k == 7))
                P.ts(t1, pz[:, prv], mups[:, 2 * j + g:2 * j + g + 1], ALU.mult)
                P.stt(dst, pz[:, cur], omups[:, 2 * j + g:2 * j + g + 1], t1, ALU.mult, ALU.add)
            pu = bank()[:, 0:N]
            P.mm(pu, l2b[0][0:64, 0, gs], dl[0][0:64, :])
            P.act(lw[g], pu, AF.Sigmoid, bias=C(g, 0))
            P.ts(lw[g], lw[g], -math.exp(-0.5), ALU.mult, eng="pool")
            pu = bank()[:, 0:N]
            P.mm(pu, l2b[1][0:64, 0, gs], dl[1][0:64, :])
            P.act(aa[g], pu, AF.Sigmoid, bias=C(g, 1))
            pu = bank()[:, 0:N]
            P.mm(pu, l2b[2][:, 0, gs], dl[2])
            P.copy(gg[g], pu, eng="act")
            if vres:
                pu = bank()[:, 0:N]
                P.mm(pu, l2b[3][0:32, 0, gs], dl[3][0:32, :])
                P.act(t2, pu, AF.Sigmoid, bias=C(g, 6))
                P.dma("sp", t3, vfi[g * 128:(g + 1) * 128, ts_])
                P.tt(t3, t3, vr[g], ALU.subtract)
                P.tt(t3, t3, t2, ALU.mult)
                P.tt(vr[g], vr[g], t3, ALU.add)
            else:
                P.dma("pool", vfo[g * 128:(g + 1) * 128, ts_], vr[g])
            P.ts(t1, kr[g], C(g, 2), ALU.mult)
            P.tt(tb, t1, t1, ALU.mult, eng="pool")
            pk = bank()[:, 0:N]
            P.mm(pk, blk, tb)
            rsq(t2, pk, 1.0, 1e-6)
            P.tt(kkn[g], t1, t2, ALU.mult)
            P.ts(t1, aa[g], -1.0, ALU.add, C(g, 3), ALU.mult)
            P.stt(k2[g], t1, 1.0, kr[g], ALU.add, ALU.mult)
            P.tt(t1, rr[g], k2[g], ALU.mult, eng="pool")
            P.ts(tb, t1, C(g, 7), ALU.mult)
            pk = bank()[:, 0:N]
            P.mm(pk, blk, tb)
            P.tt(bon[g], pk, vr[g], ALU.mult)
            cl = cumsum(lw[g], clA, clB)
            P.act(Wt, cl, AF.Exp)
            P.act(Wi, cl, AF.Exp, scale=-1.0)
            P.tt(t1, cl, lw[g], ALU.subtract)
            P.act(Wp, t1, AF.Exp)
            cl3 = v3(cl)
            P.tt(v3(t1), V(cl3.ap[:, :, 63:64].to_broadcast([128, NC, 64]), cl.keys), cl3, ALU.subtract)
            P.act(Wd, t1, AF.Exp)
            P.act(WC[g], cl3[:, :, 63], AF.Exp)
            P.tt(rt[g], rr[g], Wt, ALU.mult)
            P.tt(kt[g], k2[g], Wi, ALU.mult, eng="pool")
            P.stt(at[g], kkn[g], -1.0, Wp, ALU.mult, ALU.mult)
            P.tt(t2, kkn[g], aa[g], ALU.mult, eng="pool")
            P.tt(bt[g], t2, Wi, ALU.mult)
            P.tt(bd[g], t2, Wd, ALU.mult, eng="pool")
            P.tt(kd[g], k2[g], Wd, ALU.mult)
            P.copy(vb[g], vr[g], eng="pool")
            for cc in range(NC):
                cs = slice(cc * 64, (cc + 1) * 64)
                ptr = ptb[0:64, (cc % 2) * 384:(cc % 2) * 384 + 384].re("p (s c) -> p s c", s=3)
                for j, src in enumerate((bd[g], kd[g], vb[g])):
                    P.tr(ptr[:, j, :], src[:, cs], ident)
                P.copy(tok[g][:, cc, :, :], ptr, eng=("act" if cc % 2 else "dve"))
        for g in range(2):
            gs = slice(g * 128, (g + 1) * 128)
            outs = []
            for j in range(3):
                pz = bank()[:, 0:HW]
                for k in range(8):
                    P.mm(pz, wqkvb[j][:, k, gs], hT[:, k, :], start=(k == 0), stop=(k == 7))
                cw = lambda tap: convs[:, (j * 2 + g) * 4 + tap:(j * 2 + g) * 4 + tap + 1]
                P.ts(t1, pz[:, 0:N], cw(0), ALU.mult)
                for tap in (1, 2, 3):
                    P.stt(t1, pz[:, tap:tap + N], cw(tap), t1, ALU.mult, ALU.add)
                dst = (t2, t3, vg[g])[j]
                P.act(dst, t1, AF.Silu)
            for src, dstb, sc_ in ((t2, qn[g], 128.0 ** -0.5), (t3, kn[g], 1.0)):
                P.tt(tb, src, src, ALU.mult, eng="pool")
                pk = bank()[:, 0:N]
                P.mm(pk, ones, tb)
                rsq(t1, pk, 1.0, 1e-6)
                P.stt(dstb, src, sc_, t1, ALU.mult, ALU.mult)
            P.copy(vgb[g], vg[g], eng="pool")
            pz = bank()[:, 0:N]
            for k in range(8):
                P.mm(pz, wgateb[:, k, gs], hT[:, k, cur], start=(k == 0), stop=(k == 7))
            P.act(sgate[g], pz, AF.Silu)
            pz = bank()[:, 0:N]
            for k in range(8):
                P.mm(pz, wbar[:, k, g, :], hT[:, k, cur], start=(k == 0), stop=(k == 7))
            P.act(betab[g], pz, AF.Sigmoid)
            pz = bank()[:, 0:N]
            for k in range(8):
                P.mm(pz, wbar[:, k, 2 + g, :], hT[:, k, cur], start=(k == 0), stop=(k == 7))
            P.act(t1, pz, AF.Exp, bias=gcs[:, 2 + g:3 + g])
            P.act(t1, t1, AF.Ln, bias=1.0)
            P.ts(t2, t1, nea[:, g:g + 1], ALU.mult)
            gc = cumsum(t2, clA, clB)
            P.copy(gcb[g], gc, eng="pool")
            g3 = v3(gcb[g])
            P.act(egc[g], gcb[g], AF.Exp)
            P.tt(qd[g], qn[g], egc[g], ALU.mult)
            P.act(egl[g], g3[:, :, 63], AF.Exp)
            idb = V(identf.ap[0:64, 0:64].unsqueeze(1).to_broadcast([64, NC, 64]), identf.keys)
            P.tt(d3, g3[0:64], idb, ALU.mult)
            P.red(gcol[g], d3)
            P.tt(d3, v3(betab[g])[0:64], idb, ALU.mult)
            P.red(bcol[g], d3)
            P.act(nbw[g], gcol[g], AF.Exp)
            P.stt(nbw[g], nbw[g], -1.0, bcol[g], ALU.mult, ALU.mult)
            P.tt(dcol[g], g3[0:64, :, 63], gcol[g], ALU.subtract)
            P.act(dcol[g], dcol[g], AF.Exp)
            gcolb = V(gcol[g].ap.unsqueeze(2).to_broadcast([64, NC, 64]), gcol[g].keys)
            bcolb = V(bcol[g].ap.unsqueeze(2).to_broadcast([64, NC, 64]), bcol[g].keys)
            nmUb = V(nmU.ap.unsqueeze(1).to_broadcast([64, NC, 64]), nmU.keys)
            nmLb = V(nmLs.ap.unsqueeze(1).to_broadcast([64, NC, 64]), nmLs.keys)
            sUb = V(sU01.ap.unsqueeze(1).to_broadcast([64, NC, 64]), sU01.keys)
            P.tt(d3, g3[0:64], nmUb, ALU.add)
            P.tt(d3, d3, gcolb, ALU.subtract)
            P.act(M3[g][:, :, 2, :], d3, AF.Exp)
            P.tt(d3b, M3[g][:, :, 2, :], sUb, ALU.mult)
            P.stt(M3[g][:, :, 1, :], d3b, -1.0, v3(betab[g])[0:64], ALU.mult, ALU.mult)
            P.tt(d3, nmLb, g3[0:64], ALU.subtract)
            P.tt(d3, d3, gcolb, ALU.add)
            P.act(d3b, d3, AF.Exp)
            P.stt(M3[g][:, :, 0, :], d3b, -1.0, bcolb, ALU.mult, ALU.mult)
            for cc in range(NC):
                cs = slice(cc * 64, (cc + 1) * 64)
                ptr = ptb[0:64, (cc % 2) * 384:(cc % 2) * 384 + 256].re("p (s c) -> p s c", s=2)
                P.tr(ptr[:, 0, :], kn[g][:, cs], ident)
                P.tr(ptr[:, 1, :], vgb[g][:, cs], ident)
                P.ts(ktok[g][:, cc, :], ptr[:, 0, :], dcol[g][:, cc:cc + 1], ALU.mult)
                P.ts(bvf[g][:, cc, :], ptr[:, 1, :], bcol[g][:, cc:cc + 1], ALU.mult)
        for cc in range(NC):
            cs = slice(cc * 64, (cc + 1) * 64)
            R = []
            for hh in range(4):
                R.append((hh, hh // 2, (hh % 2) * 64))
            pas = {}
            for (i, g, off) in R:
                o_ = slice(off, off + 64)
                pa = bank()[0:64, 0:320].re("p (s q) -> p s q", s=5)
                pas[i] = pa
                P.mm(pa[:, 0, :], at[g][o_, cs], bt[g][o_, cs])
                P.mm(pa[:, 1, :], bt[g][o_, cs], at[g][o_, cs])
                P.mm(pa[:, 2, :], kt[g][o_, cs], at[g][o_, cs])
                P.mm(pa[:, 3, :], bt[g][o_, cs], rt[g][o_, cs])
                P.mm(pa[:, 4, :], kt[g][o_, cs], rt[g][o_, cs])
            for (i, g, off) in R:
                P.tt(AM[i], pas[i], m5, ALU.mult)
            for g in range(2):
                i = 4 + g
                pa = bank()[0:64, 0:192].re("p (s q) -> p s q", s=3)
                pas[i] = pa
                P.mm(pa[:, 0, :], kn[g][:, cs], kn[g][:, cs])
                P.mm(pa[:, 1, :], kn[g][:, cs], kn[g][:, cs])
                P.mm(pa[:, 2, :], kn[g][:, cs], qn[g][:, cs])
            for g in range(2):
                i = 4 + g
                P.tt(AM[i][:, 0:3, :], pas[i], M3[g][:, cc, :, :], ALU.mult)
            levels([0, 1, 2, 3, 4, 5])
            px = {}
            for (i, g, off) in R:
                o_ = slice(off, off + 64)
                p_ = bank()
                px[i] = p_
                X = p_[0:64, 0:64]
                tk = P.mm(X, AM[i][:, 2, :], tok[g][:, cc, 2, o_], start=True, stop=False)
                P.mm(X, at[g][o_, cs], Sb[g][o_, o_], start=False, stop=True, after=(tk if off else None))
                P.copy(Xs[i][:, 0:64], X, eng="act")
            for (i, g, off) in R:
                Z = px[i][0:64, 64:128]
                P.mm(Z, PQ[i][:, 1, :], Xs[i][:, 0:64])
                P.copy(Zs[i][:, off:off + 64], Z, eng="act")
            for (i, g, off) in R:
                o_ = slice(off, off + 64)
                Y = px[i][:, 128:192]
                tk = P.mm(Y, Sb[g][o_, :], rt[g][o_, cs], start=True, stop=False)
                P.mm(Y, Zs[i], AM[i][:, 3, :], start=False, stop=False, after=(tk if off else None))
                P.mm(Y, tok[g][:, cc, 2, :], AM[i][:, 4, :], start=False, stop=True)
                P.copy(yT[g][o_, cs], Y[o_, :], eng="dve")
                Sn = px[i][:, 192:256]
                P.mm(Sn, tok[g][:, cc, 0, :], Zs[i][:, off:off + 64], start=True, stop=False)
                P.mm(Sn, tok[g][:, cc, 1, :], tok[g][:, cc, 2, o_], start=False, stop=True)
                P.stt(Sf[g][o_, :], Sf[g][o_, :], WC[g][o_, cc:cc + 1], Sn[o_, :], ALU.mult, ALU.add)
                P.copy(Sb[g][o_, o_], Sf[g][o_, :], eng="pool")
            for g in range(2):
                i = 4 + g
                p_ = bank()
                px[i] = p_
                KS = p_[0:64, 0:128]
                P.mm(KS, kn[g][:, cs], Gb[g])
                P.stt(Xs[i], KS, nbw[g][:, cc:cc + 1], bvf[g][:, cc, :], ALU.mult, ALU.add)
            for g in range(2):
                i = 4 + g
                Z = px[i][0:64, 128:256]
                P.mm(Z, PQ[i][:, 1, :], Xs[i])
                P.copy(Zs[i], Z, eng="act")
            for g in range(2):
                i = 4 + g
                Y = px[i][:, 256:320]
                P.mm(Y, Gb[g], qd[g][:, cs], start=True, stop=False)
                P.mm(Y, Zs[i], AM[i][:, 2, :], start=False, stop=True)
                P.copy(yg[g][:, cs], Y, eng="dve")
                Sn = px[i][:, 320:448]
                P.mm(Sn, ktok[g][:, cc, :], Zs[i])
                P.stt(Gf[g], Gf[g], egl[g][:, cc:cc + 1], Sn, ALU.mult, ALU.add)
                P.copy(Gb[g], Gf[g], eng="pool")
        for g in range(2):
            P.copy(tb, yT[g], eng="pool")
            pm_ = bank()[:, 0:N]
            P.mm(pm_, blk, tb)
            P.ts(t1, pm_, 1.0 / 64, ALU.mult)
            P.tt(t2, yT[g], t1, ALU.subtract)
            P.tt(tb, t2, t2, ALU.mult, eng="pool")
            pv_ = bank()[:, 0:N]
            P.mm(pv_, blk, tb)
            rsq(t3, pv_, 1.0 / 64, 64e-5)
            P.tt(t2, t2, t3, ALU.mult)
            P.ts(t2, t2, C(g, 4), ALU.mult, C(g, 5), ALU.add)
            P.tt(t2, t2, bon[g], ALU.add)
            P.tt(ycat[:, g, :], t2, gg[g], ALU.mult)
            P.tt(tb, yg[g], yg[g], ALU.mult, eng="pool")
            pv_ = bank()[:, 0:N]
            P.mm(pv_, ones, tb)
            rsq(t3, pv_, 1.0 / 128, EPS)
            P.stt(t1, yg[g], gcs[:, 4 + g:5 + g], t3, ALU.mult, ALU.mult)
            P.tt(ycat[:, 2 + g, :], t1, sgate[g], ALU.mult)
        for dc in range(8):
            pm_ = bank()[:, 0:N]
            for q4 in range(4):
                P.mm(pm_, wob[:, q4, dc * 128:(dc + 1) * 128], ycat[:, q4, :], start=(q4 == 0), stop=(q4 == 3))
            P.copy(mo[:, dc, :], pm_, eng=("act" if dc % 2 else "dve"))
        P.dma("pool", m_at(t * N, N), mo)
        P.copy(hT[:, :, 0:3], hT[:, :, N:N + 3], eng="pool")


def even_inputs(inp, e, b, hh, xT_b, vfi=None):
    c = np.ascontiguousarray
    f = lambda k: np.asarray(inp[k][e], np.float32)
    W = f("even_w_in")
    o_bq = 1536; o_bg = o_bq + 1536; o_bb = o_bg + 512; o_ba = o_bb + 4
    a = slice(hh * 256, hh * 256 + 256)
    col2 = lambda v: c(np.asarray(v, np.float32)[a].reshape(2, 128).T)
    d = {"gm": c(np.asarray(inp["norm_mix_g"][2 * e], np.float32).reshape(8, 128).T)}
    d["wr"] = c(W[:, 0:512][:, a]); d["wk"] = c(W[:, 512:1024][:, a]); d["wv"] = c(W[:, 1024:1536][:, a])
    d["wbq"] = c(W[:, o_bq:o_bq + 512][:, a]); d["wbk"] = c(W[:, o_bq + 512:o_bq + 1024][:, a]); d["wbv"] = c(W[:, o_bq + 1024:o_bq + 1536][:, a])
    d["wgate"] = c(W[:, o_bg:o_bg + 512][:, a])
    d["wba"] = c(np.concatenate([W[:, o_bb + 2 * hh:o_bb + 2 * hh + 2], W[:, o_ba + 2 * hh:o_ba + 2 * hh + 2]], 1))
    mp = f("rwkv_mu_proj")
    d["mup"] = c(np.concatenate([col2(mp[j]) for j in range(3)], 1))
    ml = f("rwkv_mu_lora")
    mus = [ml[0], ml[1], ml[2], (np.asarray(inp["rwkv_v_mu"][e - 1], np.float32) if e > 0 else np.zeros(1024, np.float32))]
    d["mul"] = c(np.concatenate([m.reshape(8, 128).T for m in mus], 1))
    v0 = np.asarray(inp["rwkv_v0"][e - 1], np.float32) if e > 0 else np.zeros(512, np.float32)
    rk = f("rwkv_r_k").reshape(512)
    d["cols"] = c(np.concatenate([col2(v) for v in (f("rwkv_w0"), f("rwkv_a0"), f("rwkv_k_k"), f("rwkv_k_a"),
                                                     f("rwkv_ln_g"), f("rwkv_ln_b"), v0, rk)], 1))
    d["w1"] = f("rwkv_w1"); d["a1"] = f("rwkv_a1"); d["g1"] = f("rwkv_g1")
    d["w2"] = c(f("rwkv_w2")[:, a]); d["a2"] = c(f("rwkv_a2")[:, a]); d["g2"] = c(f("rwkv_g2")[:, a])
    if e > 0:
        d["v1"] = np.asarray(inp["rwkv_v1"][e - 1], np.float32); d["v2"] = c(np.asarray(inp["rwkv_v2"][e - 1], np.float32)[:, a])
    else:
        d["v1"] = np.zeros((1024, 32), np.float32); d["v2"] = np.zeros((32, 256), np.float32)
    cw = f("gdn_conv_w")
    cws = []
    for j in range(3):
        for g in range(2):
            ch = slice(j * 512 + hh * 256 + g * 128, j * 512 + hh * 256 + g * 128 + 128)
            cws.append(cw[:, ch].T)
    d["convw"] = c(np.concatenate(cws, 1))
    al = f("gdn_a_log")[2 * hh:2 * hh + 2]; dtb = f("gdn_dt_bias")[2 * hh:2 * hh + 2]; ng = f("gdn_norm_g")
    gcols = np.zeros((128, 8), np.float32)
    gcols[:, 0] = al[0]; gcols[:, 1] = al[1]; gcols[:, 2] = dtb[0]; gcols[:, 3] = dtb[1]; gcols[:, 4] = ng; gcols[:, 5] = ng
    d["gcols"] = gcols
    wo = f("even_w_out")
    d["wo"] = c(np.concatenate([wo[0:512][a], wo[512:1024][a]], 0))
    d["lvm"] = level_masks()
    if xT_b is not None:
        d["xT"] = xT_b
    if vfi is not None:
        d["vfi"] = vfi
    return d


def level_masks():
    t = np.arange(64)[:, None]
    j = np.arange(64)[None, :]
    out = np.zeros((64, 6, 2, 64), np.float32)
    for k in range(6):
        mq = ((t >> (k + 1)) == (j >> (k + 1))) & (((t >> k) & 1) == 1) & (((j >> k) & 1) == 0)
        out[:, k, 0, :] = mq
        out[:, k, 1, :] = mq.T
    return np.ascontiguousarray(out.reshape(64, 6 * 2 * 64))


PAIRS = [[0, 1], [2, 3], [4, 5], [6, 7]]


def build_fused(T=8192):
    H = T // 2
    nc = bass.Bass("TRN2", target_bir_lowering=False)
    P = Prog(nc)
    ext = lambda name, shape: P.dram(name, shape, F32, "ExternalInput")
    xT = ext("xT", [1024, T])
    xown = ext("xown", [1024, H])
    pTs = [ext("pT%d" % i, [256, H]) for i in range(4)]
    oT = P.dram("oT", [1024, H], F32, "ExternalOutput")
    CW = 512
    NCH = H // CW
    mk = lambda nm, rows: [[P.dram("%s%d_%d" % (nm, i, c), [rows, CW], F32, "Internal") for c in range(NCH)] for i in range(2)]
    xg, mp, mr, xo = mk("xg", 2048), mk("mp", 2048), mk("mr", 1024), mk("xo", 1024)
    vf = P.dram("vf", [256, T], F32, "Internal")

    def acc2(chunks):
        vs = [c.re("(h k p) t -> h p k t", h=2, p=128) for c in chunks]
        return lambda t0, n: vs[(t0 % H) // CW][t0 // H][:, :, (t0 % CW):(t0 % CW) + n]

    def acc1(chunks):
        vs = [c.re("(k p) t -> p k t", p=128) for c in chunks]
        return lambda t0, n: vs[t0 // CW][:, :, (t0 % CW):(t0 % CW) + n]

    P.prefix = "L0m_"
    xown_at = (lambda t0, n, _v=xown.re("(k p) t -> p k t", p=128): _v[:, :, t0:t0 + n])
    for i in range(4):
        if i == 0:
            x_at = (lambda t0, n, _v=xT.re("(k p) t -> p k t", p=128): _v[:, :, t0:t0 + n])
        else:
            x_at = acc2(xg[i % 2])
        pre = "L%dm_" % i
        if i % 2 == 0:
            io = {k: ext(pre + k, sh) for k, sh in EVEN_IN.items()}
            io["vfo" if i == 0 else "vfi"] = vf
            io["x_at"] = x_at
            io["m_at"] = acc2(mp[i % 2])
            phase_even(P, io, T, vres=(i > 0))
        else:
            io = {k: ext(pre + k, sh) for k, sh in ATTN_IN.items()}
            io["x_at"] = x_at
            io["m_at"] = acc2(mp[i % 2])
            phase_attn(P, io, T)
        for c in range(NCH):
            P.cc("ReduceScatter", ALU.add, PAIRS, mp[i % 2][c], mr[i % 2][c])
        P.end_phase("L%dp_" % i)
        pre = "L%dp_" % i
        io = {k: ext(pre + k, sh) for k, sh in POST_IN.items()}
        io.update(pT=pTs[i], x_at=xown_at, m0_at=acc1(mr[i % 2]))
        if i == 3:
            io["oT"] = oT
        else:
            io["o_at"] = acc1(xo[i % 2])
        phase_post(P, io, NT=H)
        if i < 3:
            for c in range(NCH):
                P.cc("AllGather", ALU.bypass, PAIRS, xo[i % 2][c], xg[(i + 1) % 2][c])
            xown_at = acc1(xo[i % 2])
        P.end_phase("L%dm_" % (i + 1))
    P.emit()
    return nc


def fused_inputs(inp, b, hh):
    c = np.ascontiguousarray
    f32 = lambda a: np.asarray(a, np.float32)
    S = inp["x"].shape[1]
    H = S // 2
    sl = slice(hh * H, (hh + 1) * H)
    xt = c(f32(inp["x"][b]).T)
    d = {"xT": xt, "xown": c(xt[:, sl])}
    for i in range(4):
        d["pT%d" % i] = c(f32(inp["p"][i, b, sl]).T)
        pre = "L%dm_" % i
        if i % 2 == 0:
            di = even_inputs(inp, i // 2, b, hh, None)
            for k in EVEN_IN:
                d[pre + k] = di[k]
        else:
            o = i // 2
            wqkv = f32(inp["attn_w_qkv"][o]); wo = f32(inp["attn_w_out"][o])
            bE, bO = attn_bias_layouts(f32(inp["attn_rel_bias"][o])[hh * 8:(hh + 1) * 8])
            da = {"gm": c(f32(inp["norm_mix_g"][i]).reshape(8, 128).T), "wq": c(wqkv[:, hh * 512:(hh + 1) * 512]),
                  "wk": c(wqkv[:, 1024 + hh * 512:1024 + (hh + 1) * 512]), "wv": c(wqkv[:, 2048 + hh * 512:2048 + (hh + 1) * 512]),
                  "wo": c(wo[hh * 512:(hh + 1) * 512]), "qg": c(np.tile(f32(inp["attn_q_g"][o]), 2)[:, None]),
                  "kg": c(np.tile(f32(inp["attn_k_g"][o]), 2)[:, None]), "bE": bE, "bO": bO}
            for k in ATTN_IN:
                d[pre + k] = da[k]
        pre = "L%dp_" % i
        dp = {"gf": c(f32(inp["norm_ffn_g"][i]).reshape(8, 128).T), "gp": c(f32(inp["ple_norm_g"][i]).reshape(8, 128).T),
              "w1": c(f32(inp["mlp_w1"][i])), "w2": c(f32(inp["mlp_w2"][i])), "wg": c(f32(inp["ple_w_gate"][i])), "wp": c(f32(inp["ple_w_proj"][i]))}
        for k in POST_IN:
            d[pre + k] = dp[k]
    return d


_NC = {}


def kernel(**inp):
    inp = {k: np.asarray(v) for k, v in inp.items()}
    B, S, D = inp["x"].shape
    if "fused" not in _NC:
        _NC["fused"] = build_fused(T=S)
    in_maps = [fused_inputs(inp, c // 2, c % 2) for c in range(2 * B)]
    res = run_bass_kernel_spmd(_NC["fused"], in_maps, core_ids=list(range(2 * B)))
    out = [np.concatenate([res.results[2 * b]["oT"], res.results[2 * b + 1]["oT"]], axis=1).T for b in range(B)]
    return np.ascontiguousarray(np.stack(out, 0)).astype(np.float32)
```

```python
import math
import contextlib
import numpy as np
import concourse.bass as bass
import concourse.mybir as mybir
from concourse.bass_utils import run_bass_kernel_spmd

F32 = mybir.dt.float32
BF16 = mybir.dt.bfloat16
AF = mybir.ActivationFunctionType
ALU = mybir.AluOpType
AX = mybir.AxisListType

SEM_LIMIT = 16000
N_DMA_SEMS = 10


class V:
    __slots__ = ("ap", "keys")

    def __init__(self, ap, keys):
        self.ap = ap
        self.keys = tuple(keys)

    def __getitem__(self, idx):
        return V(self.ap[idx], self.keys)

    def k(self, *sub):
        return V(self.ap, tuple((k0,) + tuple(sub) for k0 in self.keys))

    def re(self, pat, **kw):
        return V(self.ap.rearrange(pat, **kw), self.keys)

    def bc(self, shape):
        return V(self.ap.to_broadcast(shape), self.keys)


class Prog:
    ENGS = ("pe", "act", "dve", "pool", "sp")

    def __init__(self, nc):
        self.nc = nc
        self.es = contextlib.ExitStack()
        self.gs = contextlib.ExitStack()
        self.prefix = ""
        self.barrier = {}
        self.phase_dma = set()
        self.q = {e: [] for e in self.ENGS}
        self.cnt = {e: 0 for e in self.ENGS}
        self.waited = {e: {} for e in self.ENGS}
        self.last_w = {}
        self.readers = {}
        self.sems = {}
        self.dma_sems = {}
        self.dma_cnt = {}
        self.dma_rr = {e: 0 for e in self.ENGS}
        self.nbuf = 0
        self.out_tokens = []

    def sb(self, name, shape, dtype):
        name = self.prefix + name
        t = self.es.enter_context(self.nc.sbuf_tensor(name, list(shape), dtype))
        return V(t.ap() if hasattr(t, "ap") and callable(t.ap) else t[:], (name,))

    def ps(self, name, shape, dtype):
        name = self.prefix + name
        t = self.es.enter_context(self.nc.psum_tensor(name, list(shape), dtype))
        return V(t.ap() if hasattr(t, "ap") and callable(t.ap) else t[:], (name,))

    def dram(self, name, shape, dtype, kind):
        t = self.nc.dram_tensor(name, list(shape), dtype, kind=kind)
        return V(t.ap(), ("dram:" + name,))

    def _sem(self, name):
        if name not in self.sems:
            self.sems[name] = self.gs.enter_context(self.nc.semaphore(name))
        return self.sems[name]

    def _token(self, eng):
        self.cnt[eng] += 1
        i = self.cnt[eng]
        ep = (i - 1) // SEM_LIMIT
        return ("c", eng, ep, (i - 1) % SEM_LIMIT + 1)

    def _dma_token(self, eng):
        j = self.dma_rr[eng] % N_DMA_SEMS
        self.dma_rr[eng] += 1
        name = "d_%s_%d" % (eng, j)
        n = self.dma_cnt.get(name, 0) + 1
        self.dma_cnt[name] = n
        ep = (n - 1) // (SEM_LIMIT // 16)
        nn = (n - 1) % (SEM_LIMIT // 16) + 1
        prev = None
        if nn > 1:
            prev = ("d", name, ep, (nn - 1) * 16)
        return ("d", name, ep, nn * 16), prev

    def _deps(self, reads, writes, eng=None):
        deps = set()
        for k in reads:
            if k in self.last_w:
                deps.add(self.last_w[k])
        same = (lambda t: t[0] == "c" and t[1] == eng) if eng in ("act", "dve") else (lambda t: False)
        for k in writes:
            if k in self.last_w and not same(self.last_w[k]):
                deps.add(self.last_w[k])
            for r in self.readers.get(k, ()):
                if not same(r):
                    deps.add(r)
        return deps

    def _commit(self, tok, reads, writes):
        for k in writes:
            self.last_w[k] = tok
            self.readers[k] = []
        for k in reads:
            if k not in writes:
                self.readers.setdefault(k, []).append(tok)

    def _waits(self, eng, deps, force=()):
        out = []
        w = self.waited[eng]
        best = {}
        for d in deps:
            kind, nm, ep, val = d
            if kind == "c" and nm == eng and eng == "pe" and d not in force:
                continue
            key = (kind, nm, ep)
            if w.get(key, 0) >= val:
                continue
            if best.get(key, 0) < val:
                best[key] = val
        for key, val in best.items():
            w[key] = val
            out.append((key, val))
        return out

    def op(self, eng, fn, reads, writes, after=None):
        reads = [k for v in reads for k in (v.keys if isinstance(v, V) else (v,))]
        writes = [k for v in writes for k in (v.keys if isinstance(v, V) else (v,))]
        deps = self._deps(reads, writes, eng)
        force = ()
        if after is not None:
            deps.add(after)
            force = (after,)
        if eng in self.barrier:
            b = self.barrier.pop(eng)
            deps |= b
            force = tuple(force) + tuple(b)
        waits = self._waits(eng, deps, force)
        tok = self._token(eng)
        self.q[eng].append((waits, fn, tok))
        self._commit(tok, reads, writes)
        return tok

    def dma(self, eng, out, in_, **kw):
        reads = list(in_.keys)
        writes = list(out.keys)
        deps = self._deps(reads, writes)
        tok, prev = self._dma_token(eng)
        if prev is not None:
            deps.add(prev)
        if eng in self.barrier:
            deps |= self.barrier.pop(eng)
        waits = self._waits(eng, deps)
        o, i = out.ap, in_.ap
        self.q[eng].append((waits, lambda e: e.dma_start(out=o, in_=i, **kw), tok))
        self._commit(tok, reads, writes)
        self.phase_dma.add(tok)
        if any(k.startswith("dram:") for k in writes if isinstance(k, str)):
            self.out_tokens.append(tok)
        return tok

    def cc(self, kind, op, groups, in_, out, inc=1):
        reads = list(in_.keys)
        writes = list(out.keys)
        deps = self._deps(reads, writes)
        if inc == 16:
            tok, prev = self._dma_token("pool")
        else:
            self.ccn = getattr(self, "ccn", 0) + 1
            tok, prev = ("k", "cc", 0, self.ccn), (("k", "cc", 0, self.ccn - 1) if self.ccn > 1 else None)
        if prev is not None:
            deps.add(prev)
        waits = self._waits("pool", deps)
        o, i = out.ap, in_.ap
        self.q["pool"].append((waits, lambda e: e.collective_compute(kind, op, replica_groups=groups, ins=[i], outs=[o]), tok))
        self._commit(tok, reads, writes)
        self.phase_dma.add(tok)
        return tok

    def mm(self, out, lhsT, rhs, start=True, stop=True, after=None):
        o, l, r = out.ap, lhsT.ap, rhs.ap
        rd = [lhsT, rhs] + ([] if start else [out])
        return self.op("pe", lambda e: e.matmul(o, l, r, start=start, stop=stop), rd, [out], after=after)

    def tr(self, out, in_, ident):
        o, i, d = out.ap, in_.ap, ident.ap
        return self.op("pe", lambda e: e.transpose(o, i, d), [in_, ident], [out])

    def act(self, out, in_, func, bias=None, scale=1.0, accum=None, eng="act"):
        o, i = out.ap, in_.ap
        rd = [in_]
        kw = {}
        if bias is not None:
            if isinstance(bias, V):
                rd.append(bias)
                kw["bias"] = bias.ap
            else:
                kw["bias"] = bias
        if isinstance(scale, V):
            rd.append(scale)
            kw["scale"] = scale.ap
        else:
            kw["scale"] = scale
        wr = [out]
        if accum is not None:
            kw["accum_out"] = accum.ap
            wr.append(accum)
        return self.op("act", lambda e: e.activation(o, i, func, **kw), rd, wr)

    def tt(self, out, a, b, op, eng="dve"):
        o, x, y = out.ap, a.ap, b.ap
        return self.op(eng, lambda e: e.tensor_tensor(o, x, y, op), [a, b], [out])

    def ts(self, out, a, s1, op0, s2=None, op1=None, eng="dve", accum=None):
        o, x = out.ap, a.ap
        rd = [a]
        if isinstance(s1, V):
            rd.append(s1)
            s1 = s1.ap
        if isinstance(s2, V):
            rd.append(s2)
            s2 = s2.ap
        wr = [out]
        kw = {}
        if accum is not None:
            kw["accum_out"] = accum.ap
            wr.append(accum)
        if op1 is None:
            return self.op(eng, lambda e: e.tensor_scalar(o, x, s1, None, op0, **kw), rd, wr)
        return self.op(eng, lambda e: e.tensor_scalar(o, x, s1, s2, op0, op1, **kw), rd, wr)

    def stt(self, out, a, s, b, op0, op1, eng="dve"):
        o, x, y = out.ap, a.ap, b.ap
        rd = [a, b]
        if isinstance(s, V):
            rd.append(s)
            s = s.ap
        return self.op(eng, lambda e: e.scalar_tensor_tensor(o, x, s, y, op0, op1), rd, [out])

    def copy(self, out, in_, eng="dve"):
        o, i = out.ap, in_.ap
        if eng == "act":
            return self.op("act", lambda e: e.copy(o, i), [in_], [out])
        return self.op(eng, lambda e: e.tensor_copy(o, i), [in_], [out])

    def memset(self, out, val, eng="pool"):
        o = out.ap
        return self.op(eng, lambda e: e.memset(o, val), [], [out])

    def red(self, out, in_, op=None, eng="dve"):
        o, i = out.ap, in_.ap
        op = op or ALU.add
        return self.op(eng, lambda e: e.tensor_reduce(o, i, AX.X, op), [in_], [out])

    def _semh(self, key):
        kind, nm, ep = key
        return self._sem("%s_%s_%d" % (kind, nm, ep))

    def flush(self, final=False):
        nc = self.nc
        fin = self._waits("sp", set(self.out_tokens)) if final else []
        for e in self.ENGS:
            for waits, fn, tok in self.q[e]:
                self._semh(tok[:3])
                for key, val in waits:
                    self._semh(key)
        for key, val in fin:
            self._semh(key)
        qs = self.q
        semh = self._semh

        def run(eng_name):
            def body(e):
                for waits, fn, tok in qs[eng_name]:
                    for key, val in waits:
                        e.wait_ge(semh(key), val)
                    ins = fn(e)
                    ins.then_inc(semh(tok[:3]), 16 if tok[0] == "d" else 1)
                if eng_name == "sp":
                    for key, val in fin:
                        e.wait_ge(semh(key), val)
            return body

        with nc.Block() as block:
            block.tensor(run("pe"))
            block.scalar(run("act"))
            block.vector(run("dve"))
            block.gpsimd(run("pool"))
            block.sync(run("sp"))
        self.q = {e: [] for e in self.ENGS}

    def end_phase(self, next_prefix):
        toks = set(self.phase_dma)
        for e in ("pe", "act", "dve", "pool"):
            if self.cnt[e] > 0:
                i = self.cnt[e]
                toks.add(("c", e, (i - 1) // SEM_LIMIT, (i - 1) % SEM_LIMIT + 1))
        self.flush()
        self.es.close()
        self.es = contextlib.ExitStack()
        self.phase_dma = set()
        self.barrier = {e: set(toks) for e in self.ENGS}
        self.prefix = next_prefix

    def emit(self):
        self.flush(final=True)
        self.es.close()
        self.gs.close()


EPS = 1e-6


def load_w(P, dst, src, K, ncols, stage, ci):
    sv = src.re("(k p) n -> p k n", p=128)
    engs = ("dve", "pool", "act")
    step = stage[0].ap.shape[-1]
    for k in range(K):
        for c0 in range(0, ncols, step):
            c1 = min(ncols, c0 + step)
            st = stage[ci[0] % len(stage)]
            P.dma("sp", st[:, 0:c1 - c0], sv[:, k, c0:c1])
            P.copy(dst[:, k, c0:c1], st[:, 0:c1 - c0], eng=engs[ci[0] % 3])
            ci[0] += 1


def rms_feat(P, x1, sq, ssps, rstd, ones, K, N, scale_dim):
    P.act(sq, x1, AF.Square)
    for k in range(K):
        P.mm(ssps, ones, sq[:, k, :], start=(k == 0), stop=(k == K - 1))
    P.act(rstd, ssps, AF.Sqrt, bias=EPS, scale=1.0 / scale_dim)
    P.op("dve", lambda e: e.reciprocal(rstd.ap, rstd.ap), [rstd], [rstd])


POST_IN = {"gf": [128, 8], "gp": [128, 8], "w1": [1024, 4096], "w2": [4096, 1024], "wg": [1024, 1024], "wp": [256, 1024]}


def build_post(NT=4096, N=128):
    nc = bass.Bass("TRN2", target_bir_lowering=False)
    P = Prog(nc)
    D, PD = 1024, 256
    io = {k: P.dram(k, sh, F32, "ExternalInput") for k, sh in POST_IN.items()}
    for k in ("xT", "m0", "m1"):
        io[k] = P.dram(k, [D, NT], F32, "ExternalInput")
    io["pT"] = P.dram("pT", [PD, NT], F32, "ExternalInput")
    io["oT"] = P.dram("oT", [D, NT], F32, "ExternalOutput")
    phase_post(P, io, NT, N)
    P.emit()
    return nc


def phase_post(P, io, NT=4096, N=128):
    D, FF, PD = 1024, 4096, 256
    pT, gf, gp, w1, w2, wg, wp = (io[k] for k in ("pT", "gf", "gp", "w1", "w2", "wg", "wp"))
    xT, m0, oT = io.get("xT"), io.get("m0"), io.get("oT")
    m1 = io.get("m1")

    w1b = P.sb("w1b", [128, 8, FF], BF16)
    w2b = P.sb("w2b", [128, 32, D], BF16)
    wgb = P.sb("wgb", [128, 8, D], BF16)
    wpb = P.sb("wpb", [128, 2, D], BF16)
    stage = [P.sb("stg%d" % i, [128, 2048], F32) for i in range(2)]
    gfs = P.sb("gfs", [128, 8], F32)
    gps = P.sb("gps", [128, 8], F32)
    ones = P.sb("ones", [128, 128], BF16)
    x1 = P.sb("x1", [128, 8, N], F32)
    ma = P.sb("ma", [128, 8, N], F32)
    mb = P.sb("mb", [128, 8, N], F32)
    sq = P.sb("sq", [128, 8, N], BF16)
    h = P.sb("h", [128, 8, N], BF16)
    tmp = P.sb("tmp", [128, 8, N], F32)
    a1 = P.sb("a1", [128, 32, N], BF16)
    rl = [P.sb("rl%d" % i, [128, N], F32) for i in range(2)]
    x2b = P.sb("x2b", [128, 8, N], BF16)
    sg = P.sb("sg", [128, 8, N], F32)
    ee = P.sb("ee", [128, 8, N], F32)
    rstd = P.sb("rstd", [128, N], F32)
    rstd2 = P.sb("rstd2", [128, N], F32)
    pst = P.sb("pst", [128, 2, N], F32)
    pbb = P.sb("pbb", [128, 2, N], BF16)
    pb = [P.ps("pb%d" % i, [128, 512], F32) for i in range(8)]

    P.memset(ones, 1.0)
    P.dma("sp", gfs, gf)
    P.dma("sp", gps, gp)
    ci = [0]
    load_w(P, w1b, w1, 8, FF, stage, ci)
    load_w(P, w2b, w2, 32, D, stage, ci)
    load_w(P, wgb, wg, 8, D, stage, ci)
    load_w(P, wpb, wp, 2, D, stage, ci)

    _acc = lambda v: (lambda t0, n, _v=v.re("(k p) t -> p k t", p=128): _v[:, :, t0:t0 + n])
    x_at = io.get("x_at") or _acc(xT)
    m0_at = io.get("m0_at") or _acc(m0)
    o_at = io.get("o_at") or _acc(oT)
    m1v = m1.re("(k p) t -> p k t", p=128) if m1 is not None else None
    pv = pT.re("(k p) t -> p k t", p=128)
    gfb = V(gfs.ap.unsqueeze(2).to_broadcast([128, 8, N]), gfs.keys)
    gpb = V(gps.ap.unsqueeze(2).to_broadcast([128, 8, N]), gps.keys)

    def bcN(r):
        return V(r.ap.unsqueeze(1).to_broadcast([128, 8, N]), r.keys)

    for t in range(NT // N):
        ts_ = slice(t * N, (t + 1) * N)
        P.dma("sp", x1, x_at(t * N, N))
        P.dma("sp", ma, m0_at(t * N, N))
        if m1v is not None:
            P.dma("sp", mb, m1v[:, :, ts_])
        P.dma("sp", pst, pv[:, :, ts_])
        P.tt(x1, x1, ma, ALU.add)
        if m1v is not None:
            P.tt(x1, x1, mb, ALU.add, eng="pool")
        rms_feat(P, x1, sq, pb[0][:, 0:N], rstd, ones, 8, N, 1024.0)
        P.tt(tmp, x1, gfb, ALU.mult, eng="pool")
        P.tt(h, tmp, bcN(rstd), ALU.mult)
        for fc in range(32):
            pu = pb[1 + fc % 2][:, 0:N]
            for k in range(8):
                P.mm(pu, w1b[:, k, fc * 128:(fc + 1) * 128], h[:, k, :], start=(k == 0), stop=(k == 7))
            r = rl[fc % 2]
            P.act(r, pu, AF.Relu)
            P.tt(a1[:, fc, :], r, r, ALU.mult, eng=("pool" if fc % 2 else "dve"))
        for dc in range(8):
            pd = pb[3 + dc % 2][:, 0:N]
            for fk in range(32):
                P.mm(pd, w2b[:, fk, dc * 128:(dc + 1) * 128], a1[:, fk, :], start=(fk == 0), stop=(fk == 31))
            P.tt(x1[:, dc, :], x1[:, dc, :], pd, ALU.add)
        P.copy(x2b, x1, eng="pool")
        for dc in range(8):
            pg = pb[5 + dc % 2][:, 0:N]
            for k in range(8):
                P.mm(pg, wgb[:, k, dc * 128:(dc + 1) * 128], x2b[:, k, :], start=(k == 0), stop=(k == 7))
            P.act(sg[:, dc, :], pg, AF.Sigmoid)
        P.copy(pbb, pst, eng="pool")
        for dc in range(8):
            pp = pb[7 if dc % 2 else 0][:, 0:N]
            for k in range(2):
                P.mm(pp, wpb[:, k, dc * 128:(dc + 1) * 128], pbb[:, k, :], start=(k == 0), stop=(k == 1))
            P.copy(ee[:, dc, :], pp, eng="dve")
        rms_feat(P, ee, sq, pb[0][:, 0:N], rstd2, ones, 8, N, 1024.0)
        P.tt(tmp, ee, gpb, ALU.mult, eng="pool")
        P.tt(tmp, tmp, bcN(rstd2), ALU.mult)
        P.tt(tmp, tmp, sg, ALU.mult, eng="pool")
        P.tt(tmp, tmp, x1, ALU.add)
        P.dma("pool", o_at(t * N, N), tmp)


ATTN_IN = {"gm": [128, 8], "wq": [1024, 512], "wk": [1024, 512], "wv": [1024, 512], "wo": [512, 1024],
           "qg": [128, 1], "kg": [128, 1], "bE": [128, 8 * 5 * 64], "bO": [128, 8 * 5 * 64]}


def build_attn(T=8192):
    nc = bass.Bass("TRN2", target_bir_lowering=False)
    P = Prog(nc)
    io = {k: P.dram(k, sh, F32, "ExternalInput") for k, sh in ATTN_IN.items()}
    io["xT"] = P.dram("xT", [1024, T], F32, "ExternalInput")
    io["mT"] = P.dram("mT", [1024, T], F32, "ExternalOutput")
    phase_attn(P, io, T)
    P.emit()
    return nc


def phase_attn(P, io, T=8192):
    N = 128
    D = 1024
    gm, wq, wk, wv, wo, qg, kg, bE, bO = (io[k] for k in ("gm", "wq", "wk", "wv", "wo", "qg", "kg", "bE", "bO"))
    xT, mT = io.get("xT"), io.get("mT")

    wqb = P.sb("wqb", [128, 8, 512], BF16)
    wkb = P.sb("wkb", [128, 8, 512], BF16)
    wvb = P.sb("wvb", [128, 8, 512], BF16)
    wob = P.sb("wob", [128, 4, D], BF16)
    stage = [P.sb("stg%d" % i, [128, 512], F32) for i in range(2)]
    gms = P.sb("gms", [128, 8], F32)
    qgs = P.sb("qgs", [128, 1], F32)
    kgs = P.sb("kgs", [128, 1], F32)
    bEs = P.sb("bEs", [128, 8, 5, 64], F32)
    bOs = P.sb("bOs", [128, 8, 5, 64], F32)
    ones = P.sb("ones", [128, 128], BF16)
    blk = P.sb("blk", [128, 128], BF16)
    kTa = P.sb("kTa", [128, 4, T], BF16)
    Vt = P.sb("Vt", [128, T // 128, 512], BF16)
    qTt = P.sb("qTt", [128, 4, N], BF16)
    x1 = P.sb("x1", [128, 8, N], F32)
    sq = P.sb("sq", [128, 8, N], BF16)
    h = P.sb("h", [128, 8, N], BF16)
    tmp = P.sb("tmp", [128, 8, N], F32)
    rstd = P.sb("rstd", [128, N], F32)
    raw = [P.sb("raw%d" % i, [128, N], F32) for i in range(2)]
    sq1 = [P.sb("sq1%d" % i, [128, N], BF16) for i in range(2)]
    rs1 = [P.sb("rs1%d" % i, [128, N], F32) for i in range(2)]
    sc = [P.sb("sc%d" % i, [128, 5, 64], F32) for i in range(2)]
    pT = [P.sb("pT%d" % i, [128, 5, 64], BF16) for i in range(2)]
    rden = [P.sb("rden%d" % i, [128, 64], F32) for i in range(2)]
    ao = P.sb("ao", [128, 4, N], BF16)
    mo = tmp
    pb = [P.ps("pb%d" % i, [128, 512], F32) for i in range(8)]

    P.memset(ones, 1.0)
    P.memset(blk, 0.0)
    P.memset(blk[0:64, 0:64], 1.0)
    P.memset(blk[64:128, 64:128], 1.0)
    P.dma("sp", gms, gm)
    P.dma("sp", qgs, qg)
    P.dma("sp", kgs, kg)
    P.dma("sp", bEs.re("p a b c -> p (a b c)"), bE)
    P.dma("sp", bOs.re("p a b c -> p (a b c)"), bO)
    P.ts(qgs, qgs, 0.125, ALU.mult)
    ci = [0]
    load_w(P, wqb, wq, 8, 512, stage, ci)
    load_w(P, wkb, wk, 8, 512, stage, ci)
    load_w(P, wvb, wv, 8, 512, stage, ci)
    load_w(P, wob, wo, 4, D, stage, ci)

    x_at = io.get("x_at") or (lambda t0, n, _v=xT.re("(k p) t -> p k t", p=128): _v[:, :, t0:t0 + n])
    m_at = io.get("m_at") or (lambda t0, n, _v=mT.re("(k p) t -> p k t", p=128): _v[:, :, t0:t0 + n])
    gmb = V(gms.ap.unsqueeze(2).to_broadcast([128, 8, N]), gms.keys)
    cnt = [0]

    for t in range(T // N):
        ts_ = slice(t * N, (t + 1) * N)
        P.dma("sp", x1, x_at(t * N, N))
        rms_feat(P, x1, sq, pb[0][:, 0:N], rstd, ones, 8, N, 1024.0)
        P.tt(tmp, x1, gmb, ALU.mult, eng="pool")
        P.tt(h, tmp, V(rstd.ap.unsqueeze(1).to_broadcast([128, 8, N]), rstd.keys), ALU.mult)
        for which in range(2):
            wb = wqb if which == 0 else wkb
            gs = qgs if which == 0 else kgs
            for g in range(4):
                i = cnt[0] % 2
                cnt[0] += 1
                pq = pb[1 + i][:, 0:N]
                for k in range(8):
                    P.mm(pq, wb[:, k, g * 128:(g + 1) * 128], h[:, k, :], start=(k == 0), stop=(k == 7))
                P.copy(raw[i], pq, eng="act")
                P.tt(sq1[i], raw[i], raw[i], ALU.mult, eng="pool")
                ps2 = pb[3 + i][:, 0:N]
                P.mm(ps2, blk, sq1[i])
                P.act(rs1[i], ps2, AF.Sqrt, bias=EPS, scale=1.0 / 64)
                r_ = rs1[i]
                P.op("dve", lambda e, r_=r_: e.reciprocal(r_.ap, r_.ap), [r_], [r_])
                dst = qTt[:, g, :] if which == 0 else kTa[:, g, ts_]
                P.stt(dst, raw[i], gs[:, 0:1], rs1[i], ALU.mult, ALU.mult)
        pv_ = pb[5]
        for k in range(8):
            P.mm(pv_, h[:, k, :], wvb[:, k, :], start=(k == 0), stop=(k == 7))
        P.copy(Vt[:, t, :], pv_, eng="act")
        for cc in range(2):
            c = 2 * t + cc
            qs = slice(cc * 64, (cc + 1) * 64)
            slots = []
            if c % 2 == 0:
                for s in range(4):
                    slots.append(((c - 8) // 2 + s, 0, 128))
                slots.append((c // 2, 0, 64))
                bias = bEs
            else:
                slots.append(((c - 9) // 2, 64, 128))
                for s in range(1, 5):
                    slots.append(((c - 9) // 2 + s, 0, 128))
                bias = bOs
            valid = [(s, b, p0, p1) for s, (b, p0, p1) in enumerate(slots) if b >= 0]
            s_lo = valid[0][0]
            for hh in range(8):
                g, off = hh // 2, (hh % 2) * 64
                i = cnt[0] % 2
                cnt[0] += 1
                pS = pb[6][:, (i * 256):(i * 256) + 320] if False else pb[6 + i][:, 0:320]
                pS3 = pS.re("p (s q) -> p s q", s=5)
                for (s, b, p0, p1) in valid:
                    P.mm(pS3[:, s, :], kTa[off:off + 64, g, b * 128:(b + 1) * 128], qTt[off:off + 64, g, qs])
                P.tt(sc[i][:, s_lo:5, :], pS3[:, s_lo:5, :], bias[:, hh, s_lo:5, :], ALU.add)
                P.act(pT[i][:, s_lo:5, :], sc[i][:, s_lo:5, :], AF.Exp)
                pden = pb[3 + i][:, 128:192]
                po = pb[3 + i][:, 192:256]
                for j, (s, b, p0, p1) in enumerate(valid):
                    P.mm(pden, ones[p0:p1, :], pT[i][p0:p1, s, :], start=(j == 0), stop=(j == len(valid) - 1))
                for j, (s, b, p0, p1) in enumerate(valid):
                    P.mm(po, Vt[p0:p1, b, g * 128:(g + 1) * 128], pT[i][p0:p1, s, :], start=(j == 0), stop=(j == len(valid) - 1))
                rd = rden[i]
                P.op("dve", lambda e, rd=rd, pden=pden: e.reciprocal(rd.ap, pden.ap), [pden], [rd])
                P.tt(ao[off:off + 64, g, qs], po[off:off + 64, :], rd[off:off + 64, :], ALU.mult)
        for dc in range(8):
            pm = pb[1 + dc % 2][:, 256:256 + N]
            for g in range(4):
                P.mm(pm, wob[:, g, dc * 128:(dc + 1) * 128], ao[:, g, :], start=(g == 0), stop=(g == 3))
            P.copy(mo[:, dc, :], pm, eng=("act" if dc % 2 else "dve"))
        P.dma("pool", m_at(t * N, N), mo)


def attn_bias_layouts(rel_bias8):
    H = rel_bias8.shape[0]
    p = np.arange(128)[:, None, None]
    s = np.arange(5)[None, :, None]
    q = np.arange(64)[None, None, :]
    kk = p % 64
    hi = (p >= 64).astype(np.int64)
    outs = []
    for odd in (0, 1):
        if not odd:
            m = 2 * s + hi
            ok = (m <= 8)
        else:
            m = 2 * s - 1 + hi
            ok = (m >= 0)
        m = np.clip(m, 0, 8) + 0 * q
        rel = q - kk + 64 * (8 - m)
        idx = np.clip(rel, -256, 256) + 256
        ok = np.broadcast_to(ok, idx.shape)
        g = rel_bias8[:, idx]
        g = np.where(ok[None], g, np.float32(0))
        outs.append(np.ascontiguousarray(np.transpose(g, (1, 0, 2, 3)).reshape(128, H * 5 * 64)).astype(np.float32))
    return outs


NEG = -30000.0


EVEN_IN = {"gm": [128, 8], "wr": [1024, 256], "wk": [1024, 256], "wv": [1024, 256], "wbq": [1024, 256], "wbk": [1024, 256],
           "wbv": [1024, 256], "wgate": [1024, 256], "wba": [1024, 4], "mup": [128, 6], "mul": [128, 32], "cols": [128, 16],
           "w1": [1024, 64], "a1": [1024, 64], "g1": [1024, 128], "v1": [1024, 32], "w2": [64, 256], "a2": [64, 256],
           "g2": [128, 256], "v2": [32, 256], "convw": [128, 24], "gcols": [128, 8], "wo": [512, 1024], "lvm": [64, 6 * 2 * 64]}


def build_even(T=8192, vres=False, stop=99):
    nc = bass.Bass("TRN2", target_bir_lowering=False)
    P = Prog(nc)
    io = {k: P.dram(k, sh, F32, "ExternalInput") for k, sh in EVEN_IN.items()}
    io["xT"] = P.dram("xT", [1024, T], F32, "ExternalInput")
    if vres:
        io["vfi"] = P.dram("vfi", [256, T], F32, "ExternalInput")
    else:
        io["vfo"] = P.dram("vfo", [256, T], F32, "ExternalOutput")
    io["mT"] = P.dram("mT", [1024, T], F32, "ExternalOutput")
    phase_even(P, io, T, vres)
    P.emit()
    return nc


def phase_even(P, io, T=8192, vres=False):
    stop = 99
    N = 256
    NC = N // 64
    HW = N + 3
    D = 1024
    xT = io.get("xT"); gm = io["gm"]
    w_rkv = [io["wr"], io["wk"], io["wv"]]
    w_qkv = [io["wbq"], io["wbk"], io["wbv"]]
    wgate = io["wgate"]; wba = io["wba"]; mup = io["mup"]; mul = io["mul"]; cols = io["cols"]
    lw1 = [io["w1"], io["a1"], io["g1"], io["v1"]]
    lw2 = [io["w2"], io["a2"], io["g2"], io["v2"]]
    convw = io["convw"]; gcols = io["gcols"]; wo = io["wo"]; lvm = io["lvm"]
    vfi = io.get("vfi"); vfo = io.get("vfo"); mT = io.get("mT")

    S = lambda name, shape, dt=F32: P.sb(name, shape, dt)
    stage = [S("stg%d" % i, [128, 1024]) for i in range(2)]
    wrkvb = [S("wrkvb%d" % i, [128, 8, 256], BF16) for i in range(3)]
    wqkvb = [S("wqkvb%d" % i, [128, 8, 256], BF16) for i in range(3)]
    wgateb = S("wgateb", [128, 8, 256], BF16)
    wbab = S("wbab", [128, 8, 4], BF16)
    wbar = S("wbar", [128, 8, 4, 128], BF16)
    lcols = [64, 64, 128, 32]
    l1b = [S("l1b%d" % i, [128, 8, lcols[i]], BF16) for i in range(4)]
    l1A = [S("l1A%d" % i, [128, 8, lcols[i]], BF16) for i in range(4)]
    l1B = [S("l1B%d" % i, [128, 8, lcols[i]], BF16) for i in range(4)]
    l2b = [S("l2b%d" % i, [128, 1, 256], BF16) for i in range(4)]
    wob = S("wob", [128, 4, D], BF16)
    gms = S("gms", [128, 8]); mups = S("mups", [128, 6]); omups = S("omups", [128, 6])
    muls = S("muls", [128, 32]); omuls = S("omuls", [128, 32])
    colss = S("colss", [128, 16]); convs = S("convs", [128, 24]); gcs = S("gcs", [128, 8])
    ones = S("ones", [128, 128], BF16); blk = S("blk", [128, 128], BF16); ident = S("ident", [128, 128], BF16)
    identf = S("identf", [128, 128])
    m5 = S("m5", [64, 5, 64]); I2 = S("I2", [64, 2, 64])
    lvms = S("lvms", [64, 6, 2, 64])
    nmU = S("nmU", [64, 64]); nmLs = S("nmLs", [64, 64]); sU01 = S("sU01", [64, 64])
    pb = [P.ps("pb%d" % i, [128, 512], F32) for i in range(7)]
    ptb = P.ps("ptb", [128, 1024], BF16)

    P.memset(ones, 1.0); P.memset(blk, 0.0)
    P.memset(blk[0:64, 0:64], 1.0); P.memset(blk[64:128, 64:128], 1.0)
    P.memset(identf, 1.0)
    idf = identf
    P.op("pool", lambda e: e.affine_select(idf.ap, idf.ap, [[-1, 128]], ALU.is_ge, 0.0, base=0, channel_multiplier=1), [idf], [idf])
    P.op("pool", lambda e: e.affine_select(idf.ap, idf.ap, [[1, 128]], ALU.is_ge, 0.0, base=0, channel_multiplier=-1), [idf], [idf])
    P.copy(ident, identf)
    def tri(dst, cmp, fill, init):
        P.memset(dst, init)
        d = dst
        sgn = 1
        if cmp == ALU.is_lt:
            cmp, sgn = ALU.is_gt, -1
        elif cmp == ALU.is_le:
            cmp, sgn = ALU.is_ge, -1
        P.op("pool", lambda e: e.affine_select(d.ap, d.ap, [[-sgn, 64]], cmp, fill, base=0, channel_multiplier=sgn), [d], [d])
    tri(m5[:, 0, :], ALU.is_gt, 0.0, 1.0)
    tri(m5[:, 1, :], ALU.is_lt, 0.0, 1.0)
    tri(m5[:, 2, :], ALU.is_lt, 0.0, 1.0)
    tri(m5[:, 3, :], ALU.is_le, 0.0, 1.0)
    tri(m5[:, 4, :], ALU.is_le, 0.0, 1.0)
    P.copy(I2[:, 0, :], identf[0:64, 0:64]); P.copy(I2[:, 1, :], identf[0:64, 0:64])
    tri(nmU, ALU.is_le, NEG, 0.0)
    tri(nmLs, ALU.is_gt, NEG, 0.0)
    tri(sU01, ALU.is_lt, 0.0, 1.0)

    P.dma("sp", lvms.re("p a b c -> p (a b c)"), lvm)
    for dst, src in ((gms, gm), (mups, mup), (muls, mul), (colss, cols), (convs, convw), (gcs, gcols)):
        P.dma("sp", dst, src)
    P.ts(omups, mups, -1.0, ALU.mult, 1.0, ALU.add)
    P.ts(omuls, muls, -1.0, ALU.mult, 1.0, ALU.add)
    ci = [0]
    for i in range(3):
        load_w(P, wrkvb[i], w_rkv[i], 8, 256, stage, ci)
        load_w(P, wqkvb[i], w_qkv[i], 8, 256, stage, ci)
    load_w(P, wgateb, wgate, 8, 256, stage, ci)
    load_w(P, wbab, wba, 8, 4, stage, ci)
    P.copy(wbar, V(wbab.ap.unsqueeze(3).to_broadcast([128, 8, 4, 128]), wbab.keys))
    nl = 4 if vres else 3
    for i in range(nl):
        load_w(P, l1b[i], lw1[i], 8, lcols[i], stage, ci)
        mi = i
        for k in range(8):
            P.ts(l1A[i][:, k, :], l1b[i][:, k, :], omuls[:, mi * 8 + k:mi * 8 + k + 1], ALU.mult, eng=("pool" if k % 2 else "dve"))
            P.ts(l1B[i][:, k, :], l1b[i][:, k, :], muls[:, mi * 8 + k:mi * 8 + k + 1], ALU.mult, eng=("dve" if k % 2 else "pool"))
        st = stage[ci[0] % 2]; ci[0] += 1
        P.dma("sp", st[0:lcols[i], 0:256], lw2[i])
        P.copy(l2b[i][0:lcols[i], 0, :], st[0:lcols[i], 0:256])
    load_w(P, wob, wo, 4, D, stage, ci)
    nea = S("nea", [128, 2])
    P.act(nea, gcs[:, 0:2], AF.Exp)
    P.ts(nea, nea, -1.0, ALU.mult)

    hT = S("hT", [128, 8, HW], BF16)
    x1 = S("x1", [128, 8, N]); sq = S("sq", [128, 8, N], BF16); tmp = S("tmp", [128, 8, N]); rstd = S("rstd", [128, N])
    P.memset(hT[:, :, 0:3], 0.0)
    FM = lambda name, dt=F32: [S("%s%d" % (name, g), [128, N], dt) for g in range(2)]
    rr, kr, vr = FM("rr"), FM("kr"), FM("vr")
    lw, aa, gg, bon = FM("lw"), FM("aa"), FM("gg"), FM("bon")
    kkn, k2 = FM("kkn"), FM("k2")
    t1, t2, t3 = S("t1", [128, N]), S("t2", [128, N]), S("t3", [128, N])
    tb = S("tb", [128, N], BF16)
    clA, clB = S("clA", [128, N]), S("clB", [128, N])
    Wt, Wi, Wp, Wd = S("Wt", [128, N]), S("Wi", [128, N]), S("Wp", [128, N]), S("Wd", [128, N])
    WC = [S("WC%d" % g, [128, NC]) for g in range(2)]
    rt, kt, at, bt = FM("rt", BF16), FM("kt", BF16), FM("at", BF16), FM("bt", BF16)
    bd, kd, vb = FM("bd", BF16), FM("kd", BF16), FM("vb", BF16)
    tok = [S("tok%d" % g, [64, NC, 3, 128], BF16) for g in range(2)]
    dl = [S("dl%d" % i, [128, N], BF16) for i in range(4)]
    yT = FM("yT")
    ycat = S("ycat", [128, 4, N], BF16)
    mo = S("mo", [128, 8, N])
    Sf = [S("Sf%d" % g, [128, 64]) for g in range(2)]
    Sb = [S("Sb%d" % g, [128, 128], BF16) for g in range(2)]
    for g in range(2):
        P.memset(Sf[g], 0.0); P.memset(Sb[g], 0.0)
    NI = 6
    AM = [S("AM%d" % i, [64, 5, 64], BF16) for i in range(NI)]
    AB = [S("AB%d" % i, [64, 2, 64], BF16) for i in range(NI)]
    LL = [S("LL%d" % i, [64, 6, 2, 64], BF16) for i in range(NI)]
    PQ = [S("PQ%d" % i, [64, 2, 64], BF16) for i in range(NI)]
    Xs = [S("Xs%d" % i, [64, 128], BF16) for i in range(NI)]
    Zs = [S("Zs%d" % i, [64, 128], BF16) for i in range(NI)]
    for i in range(NI):
        P.memset(Zs[i], 0.0)
    qn, kn, qd = FM("qn", BF16), FM("kn", BF16), FM("qd", BF16)
    vg, sgate = FM("vg"), FM("sgate")
    vgb = FM("vgb", BF16)
    betab, gcb, egc = FM("betab"), FM("gcb"), FM("egc")
    gcol = [S("gcol%d" % g, [64, NC]) for g in range(2)]
    bcol = [S("bcol%d" % g, [64, NC]) for g in range(2)]
    nbw = [S("nbw%d" % g, [64, NC]) for g in range(2)]
    dcol = [S("dcol%d" % g, [64, NC]) for g in range(2)]
    egl = [S("egl%d" % g, [128, NC]) for g in range(2)]
    M3 = [S("M3%d" % g, [64, NC, 3, 64]) for g in range(2)]
    d3 = S("d3", [64, NC, 64]); d3b = S("d3b", [64, NC, 64])
    ktok = [S("ktok%d" % g, [64, NC, 128], BF16) for g in range(2)]
    bvf = [S("bvf%d" % g, [64, NC, 128]) for g in range(2)]
    Gf = [S("Gf%d" % g, [128, 128]) for g in range(2)]
    Gb = [S("Gb%d" % g, [128, 128], BF16) for g in range(2)]
    for g in range(2):
        P.memset(Gf[g], 0.0); P.memset(Gb[g], 0.0)
    yg = FM("yg")

    x_at = io.get("x_at") or (lambda t0, n, _v=xT.re("(k p) t -> p k t", p=128): _v[:, :, t0:t0 + n])
    m_at = io.get("m_at") or (lambda t0, n, _v=mT.re("(k p) t -> p k t", p=128): _v[:, :, t0:t0 + n])
    gmb = V(gms.ap.unsqueeze(2).to_broadcast([128, 8, N]), gms.keys)
    C = lambda g, j: colss[:, 2 * j + g:2 * j + g + 1]
    cur = slice(3, HW); prv = slice(2, HW - 1)
    rot = [0]

    def bank():
        rot[0] += 1
        return pb[rot[0] % 7]

    def v3(x):
        return x.re("p (c t) -> p c t", c=NC)

    def cumsum(src, dA, dB, np_=128):
        a = v3(src)
        bufs = [v3(dA), v3(dB)]
        i = 0
        for s in (1, 2, 4, 8, 16, 32):
            d = bufs[i % 2]
            P.tt(d[0:np_, :, s:], a[0:np_, :, s:], a[0:np_, :, :64 - s], ALU.add)
            P.copy(d[0:np_, :, :s], a[0:np_, :, :s], eng="pool")
            a = d
            i += 1
        return dB if i % 2 == 0 else dA

    def rsq(dst, ps, scale, eps):
        P.act(dst, ps, AF.Sqrt, bias=eps, scale=scale)
        P.op("dve", lambda e: e.reciprocal(dst.ap, dst.ap), [dst], [dst])

    def levels(insts):
        for i in insts:
            src = V(AM[i].ap[:, 0:2, :].unsqueeze(1).to_broadcast([64, 6, 2, 64]), AM[i].keys)
            P.tt(LL[i], src, lvms, ALU.mult, eng=("pool" if i % 2 else "dve"))
        for i in insts:
            P.tt(PQ[i], LL[i][:, 0, :, :], I2, ALU.add)
        for lvl in range(1, 6):
            pls = {}
            for i in insts:
                pl = bank()[0:64, 0:256].re("p (s q) -> p s q", s=4)
                pls[i] = pl
                P.mm(pl[:, 0, :], LL[i][:, lvl, 1, :], PQ[i][:, 0, :])
                P.mm(pl[:, 1, :], LL[i][:, lvl, 0, :], PQ[i][:, 1, :])
            for i in insts:
                P.copy(AB[i], pls[i][:, 0:2, :], eng="act")
            for i in insts:
                P.mm(pls[i][:, 2, :], PQ[i][:, 1, :], AB[i][:, 0, :])
                P.mm(pls[i][:, 3, :], PQ[i][:, 0, :], AB[i][:, 1, :])
            for i in insts:
                P.tt(PQ[i], PQ[i], pls[i][:, 2:4, :], ALU.add)

    for t in range(T // N):
        ts_ = slice(t * N, (t + 1) * N)
        P.dma("sp", x1, x_at(t * N, N))
        rms_feat(P, x1, sq, pb[0][:, 0:N], rstd, ones, 8, N, 1024.0)
        P.tt(tmp, x1, gmb, ALU.mult, eng="pool")
        P.tt(hT[:, :, cur], tmp, V(rstd.ap.unsqueeze(1).to_broadcast([128, 8, N]), rstd.keys), ALU.mult)
        for i in range(nl):
            pd_ = bank()[0:lcols[i], 0:N]
            for k in range(8):
                P.mm(pd_, l1A[i][:, k, :], hT[:, k, cur], start=(k == 0), stop=False)
                P.mm(pd_, l1B[i][:, k, :], hT[:, k, prv], start=False, stop=(k == 7))
            if i == 0:
                P.act(dl[i][0:lcols[i], :], pd_, AF.Tanh)
            elif i == 2:
                P.act(dl[i][0:lcols[i], :], pd_, AF.Sigmoid)
            else:
                P.copy(dl[i][0:lcols[i], :], pd_, eng="act")
        for g in range(2):
            gs = slice(g * 128, (g + 1) * 128)
            for j, dst in enumerate((rr[g], kr[g], vr[g])):
                pz = bank()[:, 0:HW]
                for k in range(8):
                    P.mm(pz, wrkvb[j][:, k, gs], hT[:, k, :], start=(k == 0), stop=(k == 7))
                P.ts(t1, pz[:, prv], mups[:, 2 * j + g:2 * j + g + 1], ALU.mult)
                P.stt(dst, pz[:, cur], omups[:, 2 * j + g:2 * j + g + 1], t1, ALU.mult, ALU.add)
            pu = bank()[:, 0:N]
            P.mm(pu, l2b[0][0:64, 0, gs], dl[0][0:64, :])
            P.act(lw[g], pu, AF.Sigmoid, bias=C(g, 0))
            P.ts(lw[g], lw[g], -math.exp(-0.5), ALU.mult, eng="pool")
            pu = bank()[:, 0:N]
            P.mm(pu, l2b[1][0:64, 0, gs], dl[1][0:64, :])
            P.act(aa[g], pu, AF.Sigmoid, bias=C(g, 1))
            pu = bank()[:, 0:N]
            P.mm(pu, l2b[2][:, 0, gs], dl[2])
            P.copy(gg[g], pu, eng="act")
            if vres:
                pu = bank()[:, 0:N]
                P.mm(pu, l2b[3][0:32, 0, gs], dl[3][0:32, :])
                P.act(t2, pu, AF.Sigmoid, bias=C(g, 6))
                P.dma("sp", t3, vfi[g * 128:(g + 1) * 128, ts_])
                P.tt(t3, t3, vr[g], ALU.subtract)
                P.tt(t3, t3, t2, ALU.mult)
                P.tt(vr[g], vr[g], t3, ALU.add)
            else:
                P.dma("pool", vfo[g * 128:(g + 1) * 128, ts_], vr[g])
            P.ts(t1, kr[g], C(g, 2), ALU.mult)
            P.tt(tb, t1, t1, ALU.mult, eng="pool")
            pk = bank()[:, 0:N]
            P.mm(pk, blk, tb)
            rsq(t2, pk, 1.0, 1e-6)
            P.tt(kkn[g], t1, t2, ALU.mult)
            P.ts(t1, aa[g], -1.0, ALU.add, C(g, 3), ALU.mult)
            P.stt(k2[g], t1, 1.0, kr[g], ALU.add, ALU.mult)
            P.tt(t1, rr[g], k2[g], ALU.mult, eng="pool")
            P.ts(tb, t1, C(g, 7), ALU.mult)
            pk = bank()[:, 0:N]
            P.mm(pk, blk, tb)
            P.tt(bon[g], pk, vr[g], ALU.mult)
            cl = cumsum(lw[g], clA, clB)
            P.act(Wt, cl, AF.Exp)
            P.act(Wi, cl, AF.Exp, scale=-1.0)
            P.tt(t1, cl, lw[g], ALU.subtract)
            P.act(Wp, t1, AF.Exp)
            cl3 = v3(cl)
            P.tt(v3(t1), V(cl3.ap[:, :, 63:64].to_broadcast([128, NC, 64]), cl.keys), cl3, ALU.subtract)
            P.act(Wd, t1, AF.Exp)
            P.act(WC[g], cl3[:, :, 63], AF.Exp)
            P.tt(rt[g], rr[g], Wt, ALU.mult)
            P.tt(kt[g], k2[g], Wi, ALU.mult, eng="pool")
            P.stt(at[g], kkn[g], -1.0, Wp, ALU.mult, ALU.mult)
            P.tt(t2, kkn[g], aa[g], ALU.mult, eng="pool")
            P.tt(bt[g], t2, Wi, ALU.mult)
            P.tt(bd[g], t2, Wd, ALU.mult, eng="pool")
            P.tt(kd[g], k2[g], Wd, ALU.mult)
            P.copy(vb[g], vr[g], eng="pool")
            for cc in range(NC):
                cs = slice(cc * 64, (cc + 1) * 64)
                ptr = ptb[0:64, (cc % 2) * 384:(cc % 2) * 384 + 384].re("p (s c) -> p s c", s=3)
                for j, src in enumerate((bd[g], kd[g], vb[g])):
                    P.tr(ptr[:, j, :], src[:, cs], ident)
                P.copy(tok[g][:, cc, :, :], ptr, eng=("act" if cc % 2 else "dve"))
        for g in range(2):
            gs = slice(g * 128, (g + 1) * 128)
            outs = []
            for j in range(3):
                pz = bank()[:, 0:HW]
                for k in range(8):
                    P.mm(pz, wqkvb[j][:, k, gs], hT[:, k, :], start=(k == 0), stop=(k == 7))
                cw = lambda tap: convs[:, (j * 2 + g) * 4 + tap:(j * 2 + g) * 4 + tap + 1]
                P.ts(t1, pz[:, 0:N], cw(0), ALU.mult)
                for tap in (1, 2, 3):
                    P.stt(t1, pz[:, tap:tap + N], cw(tap), t1, ALU.mult, ALU.add)
                dst = (t2, t3, vg[g])[j]
                P.act(dst, t1, AF.Silu)
            for src, dstb, sc_ in ((t2, qn[g], 128.0 ** -0.5), (t3, kn[g], 1.0)):
                P.tt(tb, src, src, ALU.mult, eng="pool")
                pk = bank()[:, 0:N]
                P.mm(pk, ones, tb)
                rsq(t1, pk, 1.0, 1e-6)
                P.stt(dstb, src, sc_, t1, ALU.mult, ALU.mult)
            P.copy(vgb[g], vg[g], eng="pool")
            pz = bank()[:, 0:N]
            for k in range(8):
                P.mm(pz, wgateb[:, k, gs], hT[:, k, cur], start=(k == 0), stop=(k == 7))
            P.act(sgate[g], pz, AF.Silu)
            pz = bank()[:, 0:N]
            for k in range(8):
                P.mm(pz, wbar[:, k, g, :], hT[:, k, cur], start=(k == 0), stop=(k == 7))
            P.act(betab[g], pz, AF.Sigmoid)
            pz = bank()[:, 0:N]
            for k in range(8):
                P.mm(pz, wbar[:, k, 2 + g, :], hT[:, k, cur], start=(k == 0), stop=(k == 7))
            P.act(t1, pz, AF.Exp, bias=gcs[:, 2 + g:3 + g])
            P.act(t1, t1, AF.Ln, bias=1.0)
            P.ts(t2, t1, nea[:, g:g + 1], ALU.mult)
            gc = cumsum(t2, clA, clB)
            P.copy(gcb[g], gc, eng="pool")
            g3 = v3(gcb[g])
            P.act(egc[g], gcb[g], AF.Exp)
            P.tt(qd[g], qn[g], egc[g], ALU.mult)
            P.act(egl[g], g3[:, :, 63], AF.Exp)
            idb = V(identf.ap[0:64, 0:64].unsqueeze(1).to_broadcast([64, NC, 64]), identf.keys)
            P.tt(d3, g3[0:64], idb, ALU.mult)
            P.red(gcol[g], d3)
            P.tt(d3, v3(betab[g])[0:64], idb, ALU.mult)
            P.red(bcol[g], d3)
            P.act(nbw[g], gcol[g], AF.Exp)
            P.stt(nbw[g], nbw[g], -1.0, bcol[g], ALU.mult, ALU.mult)
            P.tt(dcol[g], g3[0:64, :, 63], gcol[g], ALU.subtract)
            P.act(dcol[g], dcol[g], AF.Exp)
            gcolb = V(gcol[g].ap.unsqueeze(2).to_broadcast([64, NC, 64]), gcol[g].keys)
            bcolb = V(bcol[g].ap.unsqueeze(2).to_broadcast([64, NC, 64]), bcol[g].keys)
            nmUb = V(nmU.ap.unsqueeze(1).to_broadcast([64, NC, 64]), nmU.keys)
            nmLb = V(nmLs.ap.unsqueeze(1).to_broadcast([64, NC, 64]), nmLs.keys)
            sUb = V(sU01.ap.unsqueeze(1).to_broadcast([64, NC, 64]), sU01.keys)
            P.tt(d3, g3[0:64], nmUb, ALU.add)
            P.tt(d3, d3, gcolb, ALU.subtract)
            P.act(M3[g][:, :, 2, :], d3, AF.Exp)
            P.tt(d3b, M3[g][:, :, 2, :], sUb, ALU.mult)
            P.stt(M3[g][:, :, 1, :], d3b, -1.0, v3(betab[g])[0:64], ALU.mult, ALU.mult)
            P.tt(d3, nmLb, g3[0:64], ALU.subtract)
            P.tt(d3, d3, gcolb, ALU.add)
            P.act(d3b, d3, AF.Exp)
            P.stt(M3[g][:, :, 0, :], d3b, -1.0, bcolb, ALU.mult, ALU.mult)
            for cc in range(NC):
                cs = slice(cc * 64, (cc + 1) * 64)
                ptr = ptb[0:64, (cc % 2) * 384:(cc % 2) * 384 + 256].re("p (s c) -> p s c", s=2)
                P.tr(ptr[:, 0, :], kn[g][:, cs], ident)
                P.tr(ptr[:, 1, :], vgb[g][:, cs], ident)
                P.ts(ktok[g][:, cc, :], ptr[:, 0, :], dcol[g][:, cc:cc + 1], ALU.mult)
                P.ts(bvf[g][:, cc, :], ptr[:, 1, :], bcol[g][:, cc:cc + 1], ALU.mult)
        for cc in range(NC):
            cs = slice(cc * 64, (cc + 1) * 64)
            R = []
            for hh in range(4):
                R.append((hh, hh // 2, (hh % 2) * 64))
            pas = {}
            for (i, g, off) in R:
                o_ = slice(off, off + 64)
                pa = bank()[0:64, 0:320].re("p (s q) -> p s q", s=5)
                pas[i] = pa
                P.mm(pa[:, 0, :], at[g][o_, cs], bt[g][o_, cs])
                P.mm(pa[:, 1, :], bt[g][o_, cs], at[g][o_, cs])
                P.mm(pa[:, 2, :], kt[g][o_, cs], at[g][o_, cs])
                P.mm(pa[:, 3, :], bt[g][o_, cs], rt[g][o_, cs])
                P.mm(pa[:, 4, :], kt[g][o_, cs], rt[g][o_, cs])
            for (i, g, off) in R:
                P.tt(AM[i], pas[i], m5, ALU.mult)
            for g in range(2):
                i = 4 + g
                pa = bank()[0:64, 0:192].re("p (s q) -> p s q", s=3)
                pas[i] = pa
                P.mm(pa[:, 0, :], kn[g][:, cs], kn[g][:, cs])
                P.mm(pa[:, 1, :], kn[g][:, cs], kn[g][:, cs])
                P.mm(pa[:, 2, :], kn[g][:, cs], qn[g][:, cs])
            for g in range(2):
                i = 4 + g
                P.tt(AM[i][:, 0:3, :], pas[i], M3[g][:, cc, :, :], ALU.mult)
            levels([0, 1, 2, 3, 4, 5])
            px = {}
            for (i, g, off) in R:
                o_ = slice(off, off + 64)
                p_ = bank()
                px[i] = p_
                X = p_[0:64, 0:64]
                tk = P.mm(X, AM[i][:, 2, :], tok[g][:, cc, 2, o_], start=True, stop=False)
                P.mm(X, at[g][o_, cs], Sb[g][o_, o_], start=False, stop=True, after=(tk if off else None))
                P.copy(Xs[i][:, 0:64], X, eng="act")
            for (i, g, off) in R:
                Z = px[i][0:64, 64:128]
                P.mm(Z, PQ[i][:, 1, :], Xs[i][:, 0:64])
                P.copy(Zs[i][:, off:off + 64], Z, eng="act")
            for (i, g, off) in R:
                o_ = slice(off, off + 64)
                Y = px[i][:, 128:192]
                tk = P.mm(Y, Sb[g][o_, :], rt[g][o_, cs], start=True, stop=False)
                P.mm(Y, Zs[i], AM[i][:, 3, :], start=False, stop=False, after=(tk if off else None))
                P.mm(Y, tok[g][:, cc, 2, :], AM[i][:, 4, :], start=False, stop=True)
                P.copy(yT[g][o_, cs], Y[o_, :], eng="dve")
                Sn = px[i][:, 192:256]
                P.mm(Sn, tok[g][:, cc, 0, :], Zs[i][:, off:off + 64], start=True, stop=False)
                P.mm(Sn, tok[g][:, cc, 1, :], tok[g][:, cc, 2, o_], start=False, stop=True)
                P.stt(Sf[g][o_, :], Sf[g][o_, :], WC[g][o_, cc:cc + 1], Sn[o_, :], ALU.mult, ALU.add)
                P.copy(Sb[g][o_, o_], Sf[g][o_, :], eng="pool")
            for g in range(2):
                i = 4 + g
                p_ = bank()
                px[i] = p_
                KS = p_[0:64, 0:128]
                P.mm(KS, kn[g][:, cs], Gb[g])
                P.stt(Xs[i], KS, nbw[g][:, cc:cc + 1], bvf[g][:, cc, :], ALU.mult, ALU.add)
            for g in range(2):
                i = 4 + g
                Z = px[i][0:64, 128:256]
                P.mm(Z, PQ[i][:, 1, :], Xs[i])
                P.copy(Zs[i], Z, eng="act")
            for g in range(2):
                i = 4 + g
                Y = px[i][:, 256:320]
                P.mm(Y, Gb[g], qd[g][:, cs], start=True, stop=False)
                P.mm(Y, Zs[i], AM[i][:, 2, :], start=False, stop=True)
                P.copy(yg[g][:, cs], Y, eng="dve")
                Sn = px[i][:, 320:448]
                P.mm(Sn, ktok[g][:, cc, :], Zs[i])
                P.stt(Gf[g], Gf[g], egl[g][:, cc:cc + 1], Sn, ALU.mult, ALU.add)
                P.copy(Gb[g], Gf[g], eng="pool")
        for g in range(2):
            P.copy(tb, yT[g], eng="pool")
            pm_ = bank()[:, 0:N]
            P.mm(pm_, blk, tb)
            P.ts(t1, pm_, 1.0 / 64, ALU.mult)
            P.tt(t2, yT[g], t1, ALU.subtract)
            P.tt(tb, t2, t2, ALU.mult, eng="pool")
            pv_ = bank()[:, 0:N]
            P.mm(pv_, blk, tb)
            rsq(t3, pv_, 1.0 / 64, 64e-5)
            P.tt(t2, t2, t3, ALU.mult)
            P.ts(t2, t2, C(g, 4), ALU.mult, C(g, 5), ALU.add)
            P.tt(t2, t2, bon[g], ALU.add)
            P.tt(ycat[:, g, :], t2, gg[g], ALU.mult)
            P.tt(tb, yg[g], yg[g], ALU.mult, eng="pool")
            pv_ = bank()[:, 0:N]
            P.mm(pv_, ones, tb)
            rsq(t3, pv_, 1.0 / 128, EPS)
            P.stt(t1, yg[g], gcs[:, 4 + g:5 + g], t3, ALU.mult, ALU.mult)
            P.tt(ycat[:, 2 + g, :], t1, sgate[g], ALU.mult)
        for dc in range(8):
            pm_ = bank()[:, 0:N]
            for q4 in range(4):
                P.mm(pm_, wob[:, q4, dc * 128:(dc + 1) * 128], ycat[:, q4, :], start=(q4 == 0), stop=(q4 == 3))
            P.copy(mo[:, dc, :], pm_, eng=("act" if dc % 2 else "dve"))
        P.dma("pool", m_at(t * N, N), mo)
        P.copy(hT[:, :, 0:3], hT[:, :, N:N + 3], eng="pool")


def even_inputs(inp, e, b, hh, xT_b, vfi=None):
    c = np.ascontiguousarray
    f = lambda k: np.asarray(inp[k][e], np.float32)
    W = f("even_w_in")
    o_bq = 1536; o_bg = o_bq + 1536; o_bb = o_bg + 512; o_ba = o_bb + 4
    a = slice(hh * 256, hh * 256 + 256)
    col2 = lambda v: c(np.asarray(v, np.float32)[a].reshape(2, 128).T)
    d = {"gm": c(np.asarray(inp["norm_mix_g"][2 * e], np.float32).reshape(8, 128).T)}
    d["wr"] = c(W[:, 0:512][:, a]); d["wk"] = c(W[:, 512:1024][:, a]); d["wv"] = c(W[:, 1024:1536][:, a])
    d["wbq"] = c(W[:, o_bq:o_bq + 512][:, a]); d["wbk"] = c(W[:, o_bq + 512:o_bq + 1024][:, a]); d["wbv"] = c(W[:, o_bq + 1024:o_bq + 1536][:, a])
    d["wgate"] = c(W[:, o_bg:o_bg + 512][:, a])
    d["wba"] = c(np.concatenate([W[:, o_bb + 2 * hh:o_bb + 2 * hh + 2], W[:, o_ba + 2 * hh:o_ba + 2 * hh + 2]], 1))
    mp = f("rwkv_mu_proj")
    d["mup"] = c(np.concatenate([col2(mp[j]) for j in range(3)], 1))
    ml = f("rwkv_mu_lora")
    mus = [ml[0], ml[1], ml[2], (np.asarray(inp["rwkv_v_mu"][e - 1], np.float32) if e > 0 else np.zeros(1024, np.float32))]
    d["mul"] = c(np.concatenate([m.reshape(8, 128).T for m in mus], 1))
    v0 = np.asarray(inp["rwkv_v0"][e - 1], np.float32) if e > 0 else np.zeros(512, np.float32)
    rk = f("rwkv_r_k").reshape(512)
    d["cols"] = c(np.concatenate([col2(v) for v in (f("rwkv_w0"), f("rwkv_a0"), f("rwkv_k_k"), f("rwkv_k_a"),
                                                     f("rwkv_ln_g"), f("rwkv_ln_b"), v0, rk)], 1))
    d["w1"] = f("rwkv_w1"); d["a1"] = f("rwkv_a1"); d["g1"] = f("rwkv_g1")
    d["w2"] = c(f("rwkv_w2")[:, a]); d["a2"] = c(f("rwkv_a2")[:, a]); d["g2"] = c(f("rwkv_g2")[:, a])
    if e > 0:
        d["v1"] = np.asarray(inp["rwkv_v1"][e - 1], np.float32); d["v2"] = c(np.asarray(inp["rwkv_v2"][e - 1], np.float32)[:, a])
    else:
        d["v1"] = np.zeros((1024, 32), np.float32); d["v2"] = np.zeros((32, 256), np.float32)
    cw = f("gdn_conv_w")
    cws = []
    for j in range(3):
        for g in range(2):
            ch = slice(j * 512 + hh * 256 + g * 128, j * 512 + hh * 256 + g * 128 + 128)
            cws.append(cw[:, ch].T)
    d["convw"] = c(np.concatenate(cws, 1))
    al = f("gdn_a_log")[2 * hh:2 * hh + 2]; dtb = f("gdn_dt_bias")[2 * hh:2 * hh + 2]; ng = f("gdn_norm_g")
    gcols = np.zeros((128, 8), np.float32)
    gcols[:, 0] = al[0]; gcols[:, 1] = al[1]; gcols[:, 2] = dtb[0]; gcols[:, 3] = dtb[1]; gcols[:, 4] = ng; gcols[:, 5] = ng
    d["gcols"] = gcols
    wo = f("even_w_out")
    d["wo"] = c(np.concatenate([wo[0:512][a], wo[512:1024][a]], 0))
    d["lvm"] = level_masks()
    if xT_b is not None:
        d["xT"] = xT_b
    if vfi is not None:
        d["vfi"] = vfi
    return d


def level_masks():
    t = np.arange(64)[:, None]
    j = np.arange(64)[None, :]
    out = np.zeros((64, 6, 2, 64), np.float32)
    for k in range(6):
        mq = ((t >> (k + 1)) == (j >> (k + 1))) & (((t >> k) & 1) == 1) & (((j >> k) & 1) == 0)
        out[:, k, 0, :] = mq
        out[:, k, 1, :] = mq.T
    return np.ascontiguousarray(out.reshape(64, 6 * 2 * 64))


PAIRS = [[0, 1], [2, 3], [4, 5], [6, 7]]


def build_fused(T=8192):
    H = T // 2
    nc = bass.Bass("TRN2", target_bir_lowering=False)
    P = Prog(nc)
    ext = lambda name, shape: P.dram(name, shape, F32, "ExternalInput")
    xT = ext("xT", [1024, T])
    xown = ext("xown", [1024, H])
    pTs = [ext("pT%d" % i, [256, H]) for i in range(4)]
    oT = P.dram("oT", [1024, H], F32, "ExternalOutput")
    CW = 512
    NCH = H // CW
    mk = lambda nm, rows: [[P.dram("%s%d_%d" % (nm, i, c), [rows, CW], F32, "Internal") for c in range(NCH)] for i in range(2)]
    xg, mp, mr, xo = mk("xg", 2048), mk("mp", 2048), mk("mr", 1024), mk("xo", 1024)
    vf = P.dram("vf", [256, T], F32, "Internal")

    def acc2(chunks):
        vs = [c.re("(h k p) t -> h p k t", h=2, p=128) for c in chunks]
        return lambda t0, n: vs[(t0 % H) // CW][t0 // H][:, :, (t0 % CW):(t0 % CW) + n]

    def acc1(chunks):
        vs = [c.re("(k p) t -> p k t", p=128) for c in chunks]
        return lambda t0, n: vs[t0 // CW][:, :, (t0 % CW):(t0 % CW) + n]

    P.prefix = "L0m_"
    xown_at = (lambda t0, n, _v=xown.re("(k p) t -> p k t", p=128): _v[:, :, t0:t0 + n])
    for i in range(4):
        if i == 0:
            x_at = (lambda t0, n, _v=xT.re("(k p) t -> p k t", p=128): _v[:, :, t0:t0 + n])
        else:
            x_at = acc2(xg[i % 2])
        pre = "L%dm_" % i
        if i % 2 == 0:
            io = {k: ext(pre + k, sh) for k, sh in EVEN_IN.items()}
            io["vfo" if i == 0 else "vfi"] = vf
            io["x_at"] = x_at
            io["m_at"] = acc2(mp[i % 2])
            phase_even(P, io, T, vres=(i > 0))
        else:
            io = {k: ext(pre + k, sh) for k, sh in ATTN_IN.items()}
            io["x_at"] = x_at
            io["m_at"] = acc2(mp[i % 2])
            phase_attn(P, io, T)
        for c in range(NCH):
            P.cc("ReduceScatter", ALU.add, PAIRS, mp[i % 2][c], mr[i % 2][c])
        P.end_phase("L%dp_" % i)
        pre = "L%dp_" % i
        io = {k: ext(pre + k, sh) for k, sh in POST_IN.items()}
        io.update(pT=pTs[i], x_at=xown_at, m0_at=acc1(mr[i % 2]))
        if i == 3:
            io["oT"] = oT
        else:
            io["o_at"] = acc1(xo[i % 2])
        phase_post(P, io, NT=H)
        if i < 3:
            for c in range(NCH):
                P.cc("AllGather", ALU.bypass, PAIRS, xo[i % 2][c], xg[(i + 1) % 2][c])
            xown_at = acc1(xo[i % 2])
        P.end_phase("L%dm_" % (i + 1))
    P.emit()
    return nc


def fused_inputs(inp, b, hh):
    c = np.ascontiguousarray
    f32 = lambda a: np.asarray(a, np.float32)
    S = inp["x"].shape[1]
    H = S // 2
    sl = slice(hh * H, (hh + 1) * H)
    xt = c(f32(inp["x"][b]).T)
    d = {"xT": xt, "xown": c(xt[:, sl])}
    for i in range(4):
        d["pT%d" % i] = c(f32(inp["p"][i, b, sl]).T)
        pre = "L%dm_" % i
        if i % 2 == 0:
            di = even_inputs(inp, i // 2, b, hh, None)
            for k in EVEN_IN:
                d[pre + k] = di[k]
        else:
            o = i // 2
            wqkv = f32(inp["attn_w_qkv"][o]); wo = f32(inp["attn_w_out"][o])
            bE, bO = attn_bias_layouts(f32(inp["attn_rel_bias"][o])[hh * 8:(hh + 1) * 8])
            da = {"gm": c(f32(inp["norm_mix_g"][i]).reshape(8, 128).T), "wq": c(wqkv[:, hh * 512:(hh + 1) * 512]),
                  "wk": c(wqkv[:, 1024 + hh * 512:1024 + (hh + 1) * 512]), "wv": c(wqkv[:, 2048 + hh * 512:2048 + (hh + 1) * 512]),
                  "wo": c(wo[hh * 512:(hh + 1) * 512]), "qg": c(np.tile(f32(inp["attn_q_g"][o]), 2)[:, None]),
                  "kg": c(np.tile(f32(inp["attn_k_g"][o]), 2)[:, None]), "bE": bE, "bO": bO}
            for k in ATTN_IN:
                d[pre + k] = da[k]
        pre = "L%dp_" % i
        dp = {"gf": c(f32(inp["norm_ffn_g"][i]).reshape(8, 128).T), "gp": c(f32(inp["ple_norm_g"][i]).reshape(8, 128).T),
              "w1": c(f32(inp["mlp_w1"][i])), "w2": c(f32(inp["mlp_w2"][i])), "wg": c(f32(inp["ple_w_gate"][i])), "wp": c(f32(inp["ple_w_proj"][i]))}
        for k in POST_IN:
            d[pre + k] = dp[k]
    return d


_NC = {}


def kernel(**inp):
    inp = {k: np.asarray(v) for k, v in inp.items()}
    B, S, D = inp["x"].shape
    if "fused" not in _NC:
        _NC["fused"] = build_fused(T=S)
    in_maps = [fused_inputs(inp, c // 2, c % 2) for c in range(2 * B)]
    res = run_bass_kernel_spmd(_NC["fused"], in_maps, core_ids=list(range(2 * B)))
    out = [np.concatenate([res.results[2 * b]["oT"], res.results[2 * b + 1]["oT"]], axis=1).T for b in range(B)]
    return np.ascontiguousarray(np.stack(out, 0)).astype(np.float32)
```

```python
import math
import contextlib
import numpy as np
import concourse.bass as bass
import concourse.mybir as mybir
from concourse.bass_utils import run_bass_kernel_spmd

F32 = mybir.dt.float32
BF16 = mybir.dt.bfloat16
AF = mybir.ActivationFunctionType
ALU = mybir.AluOpType
AX = mybir.AxisListType

SEM_LIMIT = 16000
N_DMA_SEMS = 10


class V:
    __slots__ = ("ap", "keys")

    def __init__(self, ap, keys):
        self.ap = ap
        self.keys = tuple(keys)

    def __getitem__(self, idx):
        return V(self.ap[idx], self.keys)

    def k(self, *sub):
        return V(self.ap, tuple((k0,) + tuple(sub) for k0 in self.keys))

    def re(self, pat, **kw):
        return V(self.ap.rearrange(pat, **kw), self.keys)

    def bc(self, shape):
        return V(self.ap.to_broadcast(shape), self.keys)


class Prog:
    ENGS = ("pe", "act", "dve", "pool", "sp")

    def __init__(self, nc):
        self.nc = nc
        self.es = contextlib.ExitStack()
        self.gs = contextlib.ExitStack()
        self.prefix = ""
        self.barrier = {}
        self.phase_dma = set()
        self.q = {e: [] for e in self.ENGS}
        self.cnt = {e: 0 for e in self.ENGS}
        self.waited = {e: {} for e in self.ENGS}
        self.last_w = {}
        self.readers = {}
        self.sems = {}
        self.dma_sems = {}
        self.dma_cnt = {}
        self.dma_rr = {e: 0 for e in self.ENGS}
        self.nbuf = 0
        self.out_tokens = []

    def sb(self, name, shape, dtype):
        name = self.prefix + name
        t = self.es.enter_context(self.nc.sbuf_tensor(name, list(shape), dtype))
        return V(t.ap() if hasattr(t, "ap") and callable(t.ap) else t[:], (name,))

    def ps(self, name, shape, dtype):
        name = self.prefix + name
        t = self.es.enter_context(self.nc.psum_tensor(name, list(shape), dtype))
        return V(t.ap() if hasattr(t, "ap") and callable(t.ap) else t[:], (name,))

    def dram(self, name, shape, dtype, kind):
        t = self.nc.dram_tensor(name, list(shape), dtype, kind=kind)
        return V(t.ap(), ("dram:" + name,))

    def _sem(self, name):
        if name not in self.sems:
            self.sems[name] = self.gs.enter_context(self.nc.semaphore(name))
        return self.sems[name]

    def _token(self, eng):
        self.cnt[eng] += 1
        i = self.cnt[eng]
        ep = (i - 1) // SEM_LIMIT
        return ("c", eng, ep, (i - 1) % SEM_LIMIT + 1)

    def _dma_token(self, eng):
        j = self.dma_rr[eng] % N_DMA_SEMS
        self.dma_rr[eng] += 1
        name = "d_%s_%d" % (eng, j)
        n = self.dma_cnt.get(name, 0) + 1
        self.dma_cnt[name] = n
        ep = (n - 1) // (SEM_LIMIT // 16)
        nn = (n - 1) % (SEM_LIMIT // 16) + 1
        prev = None
        if nn > 1:
            prev = ("d", name, ep, (nn - 1) * 16)
        return ("d", name, ep, nn * 16), prev

    def _deps(self, reads, writes, eng=None):
        deps = set()
        for k in reads:
            if k in self.last_w:
                deps.add(self.last_w[k])
        same = (lambda t: t[0] == "c" and t[1] == eng) if eng in ("act", "dve") else (lambda t: False)
        for k in writes:
            if k in self.last_w and not same(self.last_w[k]):
                deps.add(self.last_w[k])
            for r in self.readers.get(k, ()):
                if not same(r):
                    deps.add(r)
        return deps

    def _commit(self, tok, reads, writes):
        for k in writes:
            self.last_w[k] = tok
            self.readers[k] = []
        for k in reads:
            if k not in writes:
                self.readers.setdefault(k, []).append(tok)

    def _waits(self, eng, deps, force=()):
        out = []
        w = self.waited[eng]
        best = {}
        for d in deps:
            kind, nm, ep, val = d
            if kind == "c" and nm == eng and eng == "pe" and d not in force:
                continue
            key = (kind, nm, ep)
            if w.get(key, 0) >= val:
                continue
            if best.get(key, 0) < val:
                best[key] = val
        for key, val in best.items():
            w[key] = val
            out.append((key, val))
        return out

    def op(self, eng, fn, reads, writes, after=None):
        reads = [k for v in reads for k in (v.keys if isinstance(v, V) else (v,))]
        writes = [k for v in writes for k in (v.keys if isinstance(v, V) else (v,))]
        deps = self._deps(reads, writes, eng)
        force = ()
        if after is not None:
            deps.add(after)
            force = (after,)
        if eng in self.barrier:
            b = self.barrier.pop(eng)
            deps |= b
            force = tuple(force) + tuple(b)
        waits = self._waits(eng, deps, force)
        tok = self._token(eng)
        self.q[eng].append((waits, fn, tok))
        self._commit(tok, reads, writes)
        return tok

    def dma(self, eng, out, in_, **kw):
        reads = list(in_.keys)
        writes = list(out.keys)
        deps = self._deps(reads, writes)
        tok, prev = self._dma_token(eng)
        if prev is not None:
            deps.add(prev)
        if eng in self.barrier:
            deps |= self.barrier.pop(eng)
        waits = self._waits(eng, deps)
        o, i = out.ap, in_.ap
        self.q[eng].append((waits, lambda e: e.dma_start(out=o, in_=i, **kw), tok))
        self._commit(tok, reads, writes)
        self.phase_dma.add(tok)
        if any(k.startswith("dram:") for k in writes if isinstance(k, str)):
            self.out_tokens.append(tok)
        return tok

    def cc(self, kind, op, groups, in_, out, inc=1):
        reads = list(in_.keys)
        writes = list(out.keys)
        deps = self._deps(reads, writes)
        if inc == 16:
            tok, prev = self._dma_token("pool")
        else:
            self.ccn = getattr(self, "ccn", 0) + 1
            tok, prev = ("k", "cc", 0, self.ccn), (("k", "cc", 0, self.ccn - 1) if self.ccn > 1 else None)
        if prev is not None:
            deps.add(prev)
        waits = self._waits("pool", deps)
        o, i = out.ap, in_.ap
        self.q["pool"].append((waits, lambda e: e.collective_compute(kind, op, replica_groups=groups, ins=[i], outs=[o]), tok))
        self._commit(tok, reads, writes)
        self.phase_dma.add(tok)
        return tok

    def mm(self, out, lhsT, rhs, start=True, stop=True, after=None):
        o, l, r = out.ap, lhsT.ap, rhs.ap
        rd = [lhsT, rhs] + ([] if start else [out])
        return self.op("pe", lambda e: e.matmul(o, l, r, start=start, stop=stop), rd, [out], after=after)

    def tr(self, out, in_, ident):
        o, i, d = out.ap, in_.ap, ident.ap
        return self.op("pe", lambda e: e.transpose(o, i, d), [in_, ident], [out])

    def act(self, out, in_, func, bias=None, scale=1.0, accum=None, eng="act"):
        o, i = out.ap, in_.ap
        rd = [in_]
        kw = {}
        if bias is not None:
            if isinstance(bias, V):
                rd.append(bias)
                kw["bias"] = bias.ap
            else:
                kw["bias"] = bias
        if isinstance(scale, V):
            rd.append(scale)
            kw["scale"] = scale.ap
        else:
            kw["scale"] = scale
        wr = [out]
        if accum is not None:
            kw["accum_out"] = accum.ap
            wr.append(accum)
        return self.op("act", lambda e: e.activation(o, i, func, **kw), rd, wr)

    def tt(self, out, a, b, op, eng="dve"):
        o, x, y = out.ap, a.ap, b.ap
        return self.op(eng, lambda e: e.tensor_tensor(o, x, y, op), [a, b], [out])

    def ts(self, out, a, s1, op0, s2=None, op1=None, eng="dve", accum=None):
        o, x = out.ap, a.ap
        rd = [a]
        if isinstance(s1, V):
            rd.append(s1)
            s1 = s1.ap
        if isinstance(s2, V):
            rd.append(s2)
            s2 = s2.ap
        wr = [out]
        kw = {}
        if accum is not None:
            kw["accum_out"] = accum.ap
            wr.append(accum)
        if op1 is None:
            return self.op(eng, lambda e: e.tensor_scalar(o, x, s1, None, op0, **kw), rd, wr)
        return self.op(eng, lambda e: e.tensor_scalar(o, x, s1, s2, op0, op1, **kw), rd, wr)

    def stt(self, out, a, s, b, op0, op1, eng="dve"):
        o, x, y = out.ap, a.ap, b.ap
        rd = [a, b]
        if isinstance(s, V):
            rd.append(s)
            s = s.ap
        return self.op(eng, lambda e: e.scalar_tensor_tensor(o, x, s, y, op0, op1), rd, [out])

    def copy(self, out, in_, eng="dve"):
        o, i = out.ap, in_.ap
        if eng == "act":
            return self.op("act", lambda e: e.copy(o, i), [in_], [out])
        return self.op(eng, lambda e: e.tensor_copy(o, i), [in_], [out])

    def memset(self, out, val, eng="pool"):
        o = out.ap
        return self.op(eng, lambda e: e.memset(o, val), [], [out])

    def red(self, out, in_, op=None, eng="dve"):
        o, i = out.ap, in_.ap
        op = op or ALU.add
        return self.op(eng, lambda e: e.tensor_reduce(o, i, AX.X, op), [in_], [out])

    def _semh(self, key):
        kind, nm, ep = key
        return self._sem("%s_%s_%d" % (kind, nm, ep))

    def flush(self, final=False):
        nc = self.nc
        fin = self._waits("sp", set(self.out_tokens)) if final else []
        for e in self.ENGS:
            for waits, fn, tok in self.q[e]:
                self._semh(tok[:3])
                for key, val in waits:
                    self._semh(key)
        for key, val in fin:
            self._semh(key)
        qs = self.q
        semh = self._semh

        def run(eng_name):
            def body(e):
                for waits, fn, tok in qs[eng_name]:
                    for key, val in waits:
                        e.wait_ge(semh(key), val)
                    ins = fn(e)
                    ins.then_inc(semh(tok[:3]), 16 if tok[0] == "d" else 1)
                if eng_name == "sp":
                    for key, val in fin:
                        e.wait_ge(semh(key), val)
            return body

        with nc.Block() as block:
            block.tensor(run("pe"))
            block.scalar(run("act"))
            block.vector(run("dve"))
            block.gpsimd(run("pool"))
            block.sync(run("sp"))
        self.q = {e: [] for e in self.ENGS}

    def end_phase(self, next_prefix):
        toks = set(self.phase_dma)
        for e in ("pe", "act", "dve", "pool"):
            if self.cnt[e] > 0:
                i = self.cnt[e]
                toks.add(("c", e, (i - 1) // SEM_LIMIT, (i - 1) % SEM_LIMIT + 1))
        self.flush()
        self.es.close()
        self.es = contextlib.ExitStack()
        self.phase_dma = set()
        self.barrier = {e: set(toks) for e in self.ENGS}
        self.prefix = next_prefix

    def emit(self):
        self.flush(final=True)
        self.es.close()
        self.gs.close()


EPS = 1e-6


def load_w(P, dst, src, K, ncols, stage, ci):
    sv = src.re("(k p) n -> p k n", p=128)
    engs = ("dve", "pool", "act")
    step = stage[0].ap.shape[-1]
    for k in range(K):
        for c0 in range(0, ncols, step):
            c1 = min(ncols, c0 + step)
            st = stage[ci[0] % len(stage)]
            P.dma("sp", st[:, 0:c1 - c0], sv[:, k, c0:c1])
            P.copy(dst[:, k, c0:c1], st[:, 0:c1 - c0], eng=engs[ci[0] % 3])
            ci[0] += 1


def rms_feat(P, x1, sq, ssps, rstd, ones, K, N, scale_dim):
    P.act(sq, x1, AF.Square)
    for k in range(K):
        P.mm(ssps, ones, sq[:, k, :], start=(k == 0), stop=(k == K - 1))
    P.act(rstd, ssps, AF.Sqrt, bias=EPS, scale=1.0 / scale_dim)
    P.op("dve", lambda e: e.reciprocal(rstd.ap, rstd.ap), [rstd], [rstd])


POST_IN = {"gf": [128, 8], "gp": [128, 8], "w1": [1024, 4096], "w2": [4096, 1024], "wg": [1024, 1024], "wp": [256, 1024]}


def build_post(NT=4096, N=128):
    nc = bass.Bass("TRN2", target_bir_lowering=False)
    P = Prog(nc)
    D, PD = 1024, 256
    io = {k: P.dram(k, sh, F32, "ExternalInput") for k, sh in POST_IN.items()}
    for k in ("xT", "m0", "m1"):
        io[k] = P.dram(k, [D, NT], F32, "ExternalInput")
    io["pT"] = P.dram("pT", [PD, NT], F32, "ExternalInput")
    io["oT"] = P.dram("oT", [D, NT], F32, "ExternalOutput")
    phase_post(P, io, NT, N)
    P.emit()
    return nc


def phase_post(P, io, NT=4096, N=128):
    D, FF, PD = 1024, 4096, 256
    pT, gf, gp, w1, w2, wg, wp = (io[k] for k in ("pT", "gf", "gp", "w1", "w2", "wg", "wp"))
    xT, m0, oT = io.get("xT"), io.get("m0"), io.get("oT")
    m1 = io.get("m1")

    w1b = P.sb("w1b", [128, 8, FF], BF16)
    w2b = P.sb("w2b", [128, 32, D], BF16)
    wgb = P.sb("wgb", [128, 8, D], BF16)
    wpb = P.sb("wpb", [128, 2, D], BF16)
    stage = [P.sb("stg%d" % i, [128, 2048], F32) for i in range(2)]
    gfs = P.sb("gfs", [128, 8], F32)
    gps = P.sb("gps", [128, 8], F32)
    ones = P.sb("ones", [128, 128], BF16)
    x1 = P.sb("x1", [128, 8, N], F32)
    ma = P.sb("ma", [128, 8, N], F32)
    mb = P.sb("mb", [128, 8, N], F32)
    sq = P.sb("sq", [128, 8, N], BF16)
    h = P.sb("h", [128, 8, N], BF16)
    tmp = P.sb("tmp", [128, 8, N], F32)
    a1 = P.sb("a1", [128, 32, N], BF16)
    rl = [P.sb("rl%d" % i, [128, N], F32) for i in range(2)]
    x2b = P.sb("x2b", [128, 8, N], BF16)
    sg = P.sb("sg", [128, 8, N], F32)
    ee = P.sb("ee", [128, 8, N], F32)
    rstd = P.sb("rstd", [128, N], F32)
    rstd2 = P.sb("rstd2", [128, N], F32)
    pst = P.sb("pst", [128, 2, N], F32)
    pbb = P.sb("pbb", [128, 2, N], BF16)
    pb = [P.ps("pb%d" % i, [128, 512], F32) for i in range(8)]

    P.memset(ones, 1.0)
    P.dma("sp", gfs, gf)
    P.dma("sp", gps, gp)
    ci = [0]
    load_w(P, w1b, w1, 8, FF, stage, ci)
    load_w(P, w2b, w2, 32, D, stage, ci)
    load_w(P, wgb, wg, 8, D, stage, ci)
    load_w(P, wpb, wp, 2, D, stage, ci)

    _acc = lambda v: (lambda t0, n, _v=v.re("(k p) t -> p k t", p=128): _v[:, :, t0:t0 + n])
    x_at = io.get("x_at") or _acc(xT)
    m0_at = io.get("m0_at") or _acc(m0)
    o_at = io.get("o_at") or _acc(oT)
    m1v = m1.re("(k p) t -> p k t", p=128) if m1 is not None else None
    pv = pT.re("(k p) t -> p k t", p=128)
    gfb = V(gfs.ap.unsqueeze(2).to_broadcast([128, 8, N]), gfs.keys)
    gpb = V(gps.ap.unsqueeze(2).to_broadcast([128, 8, N]), gps.keys)

    def bcN(r):
        return V(r.ap.unsqueeze(1).to_broadcast([128, 8, N]), r.keys)

    for t in range(NT // N):
        ts_ = slice(t * N, (t + 1) * N)
        P.dma("sp", x1, x_at(t * N, N))
        P.dma("sp", ma, m0_at(t * N, N))
        if m1v is not None:
            P.dma("sp", mb, m1v[:, :, ts_])
        P.dma("sp", pst, pv[:, :, ts_])
        P.tt(x1, x1, ma, ALU.add)
        if m1v is not None:
            P.tt(x1, x1, mb, ALU.add, eng="pool")
        rms_feat(P, x1, sq, pb[0][:, 0:N], rstd, ones, 8, N, 1024.0)
        P.tt(tmp, x1, gfb, ALU.mult, eng="pool")
        P.tt(h, tmp, bcN(rstd), ALU.mult)
        for fc in range(32):
            pu = pb[1 + fc % 2][:, 0:N]
            for k in range(8):
                P.mm(pu, w1b[:, k, fc * 128:(fc + 1) * 128], h[:, k, :], start=(k == 0), stop=(k == 7))
            r = rl[fc % 2]
            P.act(r, pu, AF.Relu)
            P.tt(a1[:, fc, :], r, r, ALU.mult, eng=("pool" if fc % 2 else "dve"))
        for dc in range(8):
            pd = pb[3 + dc % 2][:, 0:N]
            for fk in range(32):
                P.mm(pd, w2b[:, fk, dc * 128:(dc + 1) * 128], a1[:, fk, :], start=(fk == 0), stop=(fk == 31))
            P.tt(x1[:, dc, :], x1[:, dc, :], pd, ALU.add)
        P.copy(x2b, x1, eng="pool")
        for dc in range(8):
            pg = pb[5 + dc % 2][:, 0:N]
            for k in range(8):
                P.mm(pg, wgb[:, k, dc * 128:(dc + 1) * 128], x2b[:, k, :], start=(k == 0), stop=(k == 7))
            P.act(sg[:, dc, :], pg, AF.Sigmoid)
        P.copy(pbb, pst, eng="pool")
        for dc in range(8):
            pp = pb[7 if dc % 2 else 0][:, 0:N]
            for k in range(2):
                P.mm(pp, wpb[:, k, dc * 128:(dc + 1) * 128], pbb[:, k, :], start=(k == 0), stop=(k == 1))
            P.copy(ee[:, dc, :], pp, eng="dve")
        rms_feat(P, ee, sq, pb[0][:, 0:N], rstd2, ones, 8, N, 1024.0)
        P.tt(tmp, ee, gpb, ALU.mult, eng="pool")
        P.tt(tmp, tmp, bcN(rstd2), ALU.mult)
        P.tt(tmp, tmp, sg, ALU.mult, eng="pool")
        P.tt(tmp, tmp, x1, ALU.add)
        P.dma("pool", o_at(t * N, N), tmp)
        if io.get("after_tile"):
            io["after_tile"]((t + 1) * N)


ATTN_IN = {"gm": [128, 8], "wq": [1024, 512], "wk": [1024, 512], "wv": [1024, 512], "wo": [512, 1024],
           "qg": [128, 1], "kg": [128, 1], "bE": [128, 8 * 5 * 64], "bO": [128, 8 * 5 * 64]}


def build_attn(T=8192):
    nc = bass.Bass("TRN2", target_bir_lowering=False)
    P = Prog(nc)
    io = {k: P.dram(k, sh, F32, "ExternalInput") for k, sh in ATTN_IN.items()}
    io["xT"] = P.dram("xT", [1024, T], F32, "ExternalInput")
    io["mT"] = P.dram("mT", [1024, T], F32, "ExternalOutput")
    phase_attn(P, io, T)
    P.emit()
    return nc


def phase_attn(P, io, T=8192):
    N = 128
    D = 1024
    gm, wq, wk, wv, wo, qg, kg, bE, bO = (io[k] for k in ("gm", "wq", "wk", "wv", "wo", "qg", "kg", "bE", "bO"))
    xT, mT = io.get("xT"), io.get("mT")

    wqb = P.sb("wqb", [128, 8, 512], BF16)
    wkb = P.sb("wkb", [128, 8, 512], BF16)
    wvb = P.sb("wvb", [128, 8, 512], BF16)
    wob = P.sb("wob", [128, 4, D], BF16)
    stage = [P.sb("stg%d" % i, [128, 512], F32) for i in range(2)]
    gms = P.sb("gms", [128, 8], F32)
    qgs = P.sb("qgs", [128, 1], F32)
    kgs = P.sb("kgs", [128, 1], F32)
    bEs = P.sb("bEs", [128, 8, 5, 64], F32)
    bOs = P.sb("bOs", [128, 8, 5, 64], F32)
    ones = P.sb("ones", [128, 128], BF16)
    blk = P.sb("blk", [128, 128], BF16)
    kTa = P.sb("kTa", [128, 4, T], BF16)
    Vt = P.sb("Vt", [128, T // 128, 512], BF16)
    qTt = P.sb("qTt", [128, 4, N], BF16)
    x1 = P.sb("x1", [128, 8, N], F32)
    sq = P.sb("sq", [128, 8, N], BF16)
    h = P.sb("h", [128, 8, N], BF16)
    tmp = P.sb("tmp", [128, 8, N], F32)
    rstd = P.sb("rstd", [128, N], F32)
    raw = [P.sb("raw%d" % i, [128, N], F32) for i in range(2)]
    sq1 = [P.sb("sq1%d" % i, [128, N], BF16) for i in range(2)]
    rs1 = [P.sb("rs1%d" % i, [128, N], F32) for i in range(2)]
    sc = [P.sb("sc%d" % i, [128, 5, 64], F32) for i in range(2)]
    pT = [P.sb("pT%d" % i, [128, 5, 64], BF16) for i in range(2)]
    rden = [P.sb("rden%d" % i, [128, 64], F32) for i in range(2)]
    ao = P.sb("ao", [128, 4, N], BF16)
    mo = tmp
    pb = [P.ps("pb%d" % i, [128, 512], F32) for i in range(8)]

    P.memset(ones, 1.0)
    P.memset(blk, 0.0)
    P.memset(blk[0:64, 0:64], 1.0)
    P.memset(blk[64:128, 64:128], 1.0)
    P.dma("sp", gms, gm)
    P.dma("sp", qgs, qg)
    P.dma("sp", kgs, kg)
    P.dma("sp", bEs.re("p a b c -> p (a b c)"), bE)
    P.dma("sp", bOs.re("p a b c -> p (a b c)"), bO)
    P.ts(qgs, qgs, 0.125, ALU.mult)
    ci = [0]
    load_w(P, wqb, wq, 8, 512, stage, ci)
    load_w(P, wkb, wk, 8, 512, stage, ci)
    load_w(P, wvb, wv, 8, 512, stage, ci)
    load_w(P, wob, wo, 4, D, stage, ci)

    x_at = io.get("x_at") or (lambda t0, n, _v=xT.re("(k p) t -> p k t", p=128): _v[:, :, t0:t0 + n])
    m_at = io.get("m_at") or (lambda t0, n, _v=mT.re("(k p) t -> p k t", p=128): _v[:, :, t0:t0 + n])
    gmb = V(gms.ap.unsqueeze(2).to_broadcast([128, 8, N]), gms.keys)
    cnt = [0]

    for t in range(T // N):
        ts_ = slice(t * N, (t + 1) * N)
        P.dma("sp", x1, x_at(t * N, N))
        rms_feat(P, x1, sq, pb[0][:, 0:N], rstd, ones, 8, N, 1024.0)
        P.tt(tmp, x1, gmb, ALU.mult, eng="pool")
        P.tt(h, tmp, V(rstd.ap.unsqueeze(1).to_broadcast([128, 8, N]), rstd.keys), ALU.mult)
        for which in range(2):
            wb = wqb if which == 0 else wkb
            gs = qgs if which == 0 else kgs
            for g in range(4):
                i = cnt[0] % 2
                cnt[0] += 1
                pq = pb[1 + i][:, 0:N]
                for k in range(8):
                    P.mm(pq, wb[:, k, g * 128:(g + 1) * 128], h[:, k, :], start=(k == 0), stop=(k == 7))
                P.copy(raw[i], pq, eng="act")
                P.tt(sq1[i], raw[i], raw[i], ALU.mult, eng="pool")
                ps2 = pb[3 + i][:, 0:N]
                P.mm(ps2, blk, sq1[i])
                P.act(rs1[i], ps2, AF.Sqrt, bias=EPS, scale=1.0 / 64)
                r_ = rs1[i]
                P.op("dve", lambda e, r_=r_: e.reciprocal(r_.ap, r_.ap), [r_], [r_])
                dst = qTt[:, g, :] if which == 0 else kTa[:, g, ts_]
                P.stt(dst, raw[i], gs[:, 0:1], rs1[i], ALU.mult, ALU.mult)
        pv_ = pb[5]
        for k in range(8):
            P.mm(pv_, h[:, k, :], wvb[:, k, :], start=(k == 0), stop=(k == 7))
        P.copy(Vt[:, t, :], pv_, eng="act")
        for cc in range(2):
            c = 2 * t + cc
            qs = slice(cc * 64, (cc + 1) * 64)
            slots = []
            if c % 2 == 0:
                for s in range(4):
                    slots.append(((c - 8) // 2 + s, 0, 128))
                slots.append((c // 2, 0, 64))
                bias = bEs
            else:
                slots.append(((c - 9) // 2, 64, 128))
                for s in range(1, 5):
                    slots.append(((c - 9) // 2 + s, 0, 128))
                bias = bOs
            valid = [(s, b, p0, p1) for s, (b, p0, p1) in enumerate(slots) if b >= 0]
            s_lo = valid[0][0]
            for hh in range(8):
                g, off = hh // 2, (hh % 2) * 64
                i = cnt[0] % 2
                cnt[0] += 1
                pS = pb[6][:, (i * 256):(i * 256) + 320] if False else pb[6 + i][:, 0:320]
                pS3 = pS.re("p (s q) -> p s q", s=5)
                for (s, b, p0, p1) in valid:
                    P.mm(pS3[:, s, :], kTa[off:off + 64, g, b * 128:(b + 1) * 128], qTt[off:off + 64, g, qs])
                P.tt(sc[i][:, s_lo:5, :], pS3[:, s_lo:5, :], bias[:, hh, s_lo:5, :], ALU.add)
                P.act(pT[i][:, s_lo:5, :], sc[i][:, s_lo:5, :], AF.Exp)
                pden = pb[3 + i][:, 128:192]
                po = pb[3 + i][:, 192:256]
                for j, (s, b, p0, p1) in enumerate(valid):
                    P.mm(pden, ones[p0:p1, :], pT[i][p0:p1, s, :], start=(j == 0), stop=(j == len(valid) - 1))
                for j, (s, b, p0, p1) in enumerate(valid):
                    P.mm(po, Vt[p0:p1, b, g * 128:(g + 1) * 128], pT[i][p0:p1, s, :], start=(j == 0), stop=(j == len(valid) - 1))
                rd = rden[i]
                P.op("dve", lambda e, rd=rd, pden=pden: e.reciprocal(rd.ap, pden.ap), [pden], [rd])
                P.tt(ao[off:off + 64, g, qs], po[off:off + 64, :], rd[off:off + 64, :], ALU.mult)
        for dc in range(8):
            pm = pb[1 + dc % 2][:, 256:256 + N]
            for g in range(4):
                P.mm(pm, wob[:, g, dc * 128:(dc + 1) * 128], ao[:, g, :], start=(g == 0), stop=(g == 3))
            P.copy(mo[:, dc, :], pm, eng=("act" if dc % 2 else "dve"))
        P.dma("pool", m_at(t * N, N), mo)
        if io.get("after_tile"):
            io["after_tile"]((t + 1) * N)


def attn_bias_layouts(rel_bias8):
    H = rel_bias8.shape[0]
    p = np.arange(128)[:, None, None]
    s = np.arange(5)[None, :, None]
    q = np.arange(64)[None, None, :]
    kk = p % 64
    hi = (p >= 64).astype(np.int64)
    outs = []
    for odd in (0, 1):
        if not odd:
            m = 2 * s + hi
            ok = (m <= 8)
        else:
            m = 2 * s - 1 + hi
            ok = (m >= 0)
        m = np.clip(m, 0, 8) + 0 * q
        rel = q - kk + 64 * (8 - m)
        idx = np.clip(rel, -256, 256) + 256
        ok = np.broadcast_to(ok, idx.shape)
        g = rel_bias8[:, idx]
        g = np.where(ok[None], g, np.float32(0))
        outs.append(np.ascontiguousarray(np.transpose(g, (1, 0, 2, 3)).reshape(128, H * 5 * 64)).astype(np.float32))
    return outs


NEG = -30000.0


EVEN_IN = {"gm": [128, 8], "wr": [1024, 256], "wk": [1024, 256], "wv": [1024, 256], "wbq": [1024, 256], "wbk": [1024, 256],
           "wbv": [1024, 256], "wgate": [1024, 256], "wba": [1024, 4], "mup": [128, 6], "mul": [128, 32], "cols": [128, 16],
           "w1": [1024, 64], "a1": [1024, 64], "g1": [1024, 128], "v1": [1024, 32], "w2": [64, 256], "a2": [64, 256],
           "g2": [128, 256], "v2": [32, 256], "convw": [128, 24], "gcols": [128, 8], "wo": [512, 1024], "lvm": [64, 6 * 2 * 64]}


def build_even(T=8192, vres=False, stop=99):
    nc = bass.Bass("TRN2", target_bir_lowering=False)
    P = Prog(nc)
    io = {k: P.dram(k, sh, F32, "ExternalInput") for k, sh in EVEN_IN.items()}
    io["xT"] = P.dram("xT", [1024, T], F32, "ExternalInput")
    if vres:
        io["vfi"] = P.dram("vfi", [256, T], F32, "ExternalInput")
    else:
        io["vfo"] = P.dram("vfo", [256, T], F32, "ExternalOutput")
    io["mT"] = P.dram("mT", [1024, T], F32, "ExternalOutput")
    phase_even(P, io, T, vres)
    P.emit()
    return nc


def phase_even(P, io, T=8192, vres=False):
    stop = 99
    N = 256
    NC = N // 64
    HW = N + 3
    D = 1024
    xT = io.get("xT"); gm = io["gm"]
    w_rkv = [io["wr"], io["wk"], io["wv"]]
    w_qkv = [io["wbq"], io["wbk"], io["wbv"]]
    wgate = io["wgate"]; wba = io["wba"]; mup = io["mup"]; mul = io["mul"]; cols = io["cols"]
    lw1 = [io["w1"], io["a1"], io["g1"], io["v1"]]
    lw2 = [io["w2"], io["a2"], io["g2"], io["v2"]]
    convw = io["convw"]; gcols = io["gcols"]; wo = io["wo"]; lvm = io["lvm"]
    vfi = io.get("vfi"); vfo = io.get("vfo"); mT = io.get("mT")

    S = lambda name, shape, dt=F32: P.sb(name, shape, dt)
    stage = [S("stg%d" % i, [128, 1024]) for i in range(2)]
    wrkvb = [S("wrkvb%d" % i, [128, 8, 256], BF16) for i in range(3)]
    wqkvb = [S("wqkvb%d" % i, [128, 8, 256], BF16) for i in range(3)]
    wgateb = S("wgateb", [128, 8, 256], BF16)
    wbab = S("wbab", [128, 8, 4], BF16)
    wbar = S("wbar", [128, 8, 4, 128], BF16)
    lcols = [64, 64, 128, 32]
    l1b = [S("l1b%d" % i, [128, 8, lcols[i]], BF16) for i in range(4)]
    l1A = [S("l1A%d" % i, [128, 8, lcols[i]], BF16) for i in range(4)]
    l1B = [S("l1B%d" % i, [128, 8, lcols[i]], BF16) for i in range(4)]
    l2b = [S("l2b%d" % i, [128, 1, 256], BF16) for i in range(4)]
    wob = S("wob", [128, 4, D], BF16)
    gms = S("gms", [128, 8]); mups = S("mups", [128, 6]); omups = S("omups", [128, 6])
    muls = S("muls", [128, 32]); omuls = S("omuls", [128, 32])
    colss = S("colss", [128, 16]); convs = S("convs", [128, 24]); gcs = S("gcs", [128, 8])
    ones = S("ones", [128, 128], BF16); blk = S("blk", [128, 128], BF16); ident = S("ident", [128, 128], BF16)
    identf = S("identf", [128, 128])
    m5 = S("m5", [64, 5, 64]); I2 = S("I2", [64, 2, 64])
    lvms = S("lvms", [64, 6, 2, 64])
    nmU = S("nmU", [64, 64]); nmLs = S("nmLs", [64, 64]); sU01 = S("sU01", [64, 64])
    pb = [P.ps("pb%d" % i, [128, 512], F32) for i in range(7)]
    ptb = P.ps("ptb", [128, 1024], BF16)

    P.memset(ones, 1.0); P.memset(blk, 0.0)
    P.memset(blk[0:64, 0:64], 1.0); P.memset(blk[64:128, 64:128], 1.0)
    P.memset(identf, 1.0)
    idf = identf
    P.op("pool", lambda e: e.affine_select(idf.ap, idf.ap, [[-1, 128]], ALU.is_ge, 0.0, base=0, channel_multiplier=1), [idf], [idf])
    P.op("pool", lambda e: e.affine_select(idf.ap, idf.ap, [[1, 128]], ALU.is_ge, 0.0, base=0, channel_multiplier=-1), [idf], [idf])
    P.copy(ident, identf)
    def tri(dst, cmp, fill, init):
        P.memset(dst, init)
        d = dst
        sgn = 1
        if cmp == ALU.is_lt:
            cmp, sgn = ALU.is_gt, -1
        elif cmp == ALU.is_le:
            cmp, sgn = ALU.is_ge, -1
        P.op("pool", lambda e: e.affine_select(d.ap, d.ap, [[-sgn, 64]], cmp, fill, base=0, channel_multiplier=sgn), [d], [d])
    tri(m5[:, 0, :], ALU.is_gt, 0.0, 1.0)
    tri(m5[:, 1, :], ALU.is_lt, 0.0, 1.0)
    tri(m5[:, 2, :], ALU.is_lt, 0.0, 1.0)
    tri(m5[:, 3, :], ALU.is_le, 0.0, 1.0)
    tri(m5[:, 4, :], ALU.is_le, 0.0, 1.0)
    P.copy(I2[:, 0, :], identf[0:64, 0:64]); P.copy(I2[:, 1, :], identf[0:64, 0:64])
    tri(nmU, ALU.is_le, NEG, 0.0)
    tri(nmLs, ALU.is_gt, NEG, 0.0)
    tri(sU01, ALU.is_lt, 0.0, 1.0)

    P.dma("sp", lvms.re("p a b c -> p (a b c)"), lvm)
    for dst, src in ((gms, gm), (mups, mup), (muls, mul), (colss, cols), (convs, convw), (gcs, gcols)):
        P.dma("sp", dst, src)
    P.ts(omups, mups, -1.0, ALU.mult, 1.0, ALU.add)
    P.ts(omuls, muls, -1.0, ALU.mult, 1.0, ALU.add)
    ci = [0]
    for i in range(3):
        load_w(P, wrkvb[i], w_rkv[i], 8, 256, stage, ci)
        load_w(P, wqkvb[i], w_qkv[i], 8, 256, stage, ci)
    load_w(P, wgateb, wgate, 8, 256, stage, ci)
    load_w(P, wbab, wba, 8, 4, stage, ci)
    P.copy(wbar, V(wbab.ap.unsqueeze(3).to_broadcast([128, 8, 4, 128]), wbab.keys))
    nl = 4 if vres else 3
    for i in range(nl):
        load_w(P, l1b[i], lw1[i], 8, lcols[i], stage, ci)
        mi = i
        for k in range(8):
            P.ts(l1A[i][:, k, :], l1b[i][:, k, :], omuls[:, mi * 8 + k:mi * 8 + k + 1], ALU.mult, eng=("pool" if k % 2 else "dve"))
            P.ts(l1B[i][:, k, :], l1b[i][:, k, :], muls[:, mi * 8 + k:mi * 8 + k + 1], ALU.mult, eng=("dve" if k % 2 else "pool"))
        st = stage[ci[0] % 2]; ci[0] += 1
        P.dma("sp", st[0:lcols[i], 0:256], lw2[i])
        P.copy(l2b[i][0:lcols[i], 0, :], st[0:lcols[i], 0:256])
    load_w(P, wob, wo, 4, D, stage, ci)
    nea = S("nea", [128, 2])
    P.act(nea, gcs[:, 0:2], AF.Exp)
    P.ts(nea, nea, -1.0, ALU.mult)

    hT = S("hT", [128, 8, HW], BF16)
    x1 = S("x1", [128, 8, N]); sq = S("sq", [128, 8, N], BF16); tmp = S("tmp", [128, 8, N]); rstd = S("rstd", [128, N])
    P.memset(hT[:, :, 0:3], 0.0)
    FM = lambda name, dt=F32: [S("%s%d" % (name, g), [128, N], dt) for g in range(2)]
    rr, kr, vr = FM("rr"), FM("kr"), FM("vr")
    lw, aa, gg, bon = FM("lw"), FM("aa"), FM("gg"), FM("bon")
    kkn, k2 = FM("kkn"), FM("k2")
    t1, t2, t3 = S("t1", [128, N]), S("t2", [128, N]), S("t3", [128, N])
    tb = S("tb", [128, N], BF16)
    clA, clB = S("clA", [128, N]), S("clB", [128, N])
    Wt, Wi, Wp, Wd = S("Wt", [128, N]), S("Wi", [128, N]), S("Wp", [128, N]), S("Wd", [128, N])
    WC = [S("WC%d" % g, [128, NC]) for g in range(2)]
    rt, kt, at, bt = FM("rt", BF16), FM("kt", BF16), FM("at", BF16), FM("bt", BF16)
    bd, kd, vb = FM("bd", BF16), FM("kd", BF16), FM("vb", BF16)
    tok = [S("tok%d" % g, [64, NC, 3, 128], BF16) for g in range(2)]
    dl = [S("dl%d" % i, [128, N], BF16) for i in range(4)]
    yT = FM("yT")
    ycat = S("ycat", [128, 4, N], BF16)
    mo = S("mo", [128, 8, N])
    Sf = [S("Sf%d" % g, [128, 64]) for g in range(2)]
    Sb = [S("Sb%d" % g, [128, 128], BF16) for g in range(2)]
    for g in range(2):
        P.memset(Sf[g], 0.0); P.memset(Sb[g], 0.0)
    NI = 6
    AM = [S("AM%d" % i, [64, 5, 64], BF16) for i in range(NI)]
    AB = [S("AB%d" % i, [64, 2, 64], BF16) for i in range(NI)]
    LL = [S("LL%d" % i, [64, 6, 2, 64], BF16) for i in range(NI)]
    PQ = [S("PQ%d" % i, [64, 2, 64], BF16) for i in range(NI)]
    Xs = [S("Xs%d" % i, [64, 128], BF16) for i in range(NI)]
    Zs = [S("Zs%d" % i, [64, 128], BF16) for i in range(NI)]
    for i in range(NI):
        P.memset(Zs[i], 0.0)
    qn, kn, qd = FM("qn", BF16), FM("kn", BF16), FM("qd", BF16)
    vg, sgate = FM("vg"), FM("sgate")
    vgb = FM("vgb", BF16)
    betab, gcb, egc = FM("betab"), FM("gcb"), FM("egc")
    gcol = [S("gcol%d" % g, [64, NC]) for g in range(2)]
    bcol = [S("bcol%d" % g, [64, NC]) for g in range(2)]
    nbw = [S("nbw%d" % g, [64, NC]) for g in range(2)]
    dcol = [S("dcol%d" % g, [64, NC]) for g in range(2)]
    egl = [S("egl%d" % g, [128, NC]) for g in range(2)]
    M3 = [S("M3%d" % g, [64, NC, 3, 64]) for g in range(2)]
    d3 = S("d3", [64, NC, 64]); d3b = S("d3b", [64, NC, 64])
    ktok = [S("ktok%d" % g, [64, NC, 128], BF16) for g in range(2)]
    bvf = [S("bvf%d" % g, [64, NC, 128]) for g in range(2)]
    Gf = [S("Gf%d" % g, [128, 128]) for g in range(2)]
    Gb = [S("Gb%d" % g, [128, 128], BF16) for g in range(2)]
    for g in range(2):
        P.memset(Gf[g], 0.0); P.memset(Gb[g], 0.0)
    yg = FM("yg")

    x_at = io.get("x_at") or (lambda t0, n, _v=xT.re("(k p) t -> p k t", p=128): _v[:, :, t0:t0 + n])
    m_at = io.get("m_at") or (lambda t0, n, _v=mT.re("(k p) t -> p k t", p=128): _v[:, :, t0:t0 + n])
    gmb = V(gms.ap.unsqueeze(2).to_broadcast([128, 8, N]), gms.keys)
    C = lambda g, j: colss[:, 2 * j + g:2 * j + g + 1]
    cur = slice(3, HW); prv = slice(2, HW - 1)
    rot = [0]

    def bank():
        rot[0] += 1
        return pb[rot[0] % 7]

    def v3(x):
        return x.re("p (c t) -> p c t", c=NC)

    def cumsum(src, dA, dB, np_=128):
        a = v3(src)
        bufs = [v3(dA), v3(dB)]
        i = 0
        for s in (1, 2, 4, 8, 16, 32):
            d = bufs[i % 2]
            P.tt(d[0:np_, :, s:], a[0:np_, :, s:], a[0:np_, :, :64 - s], ALU.add)
            P.copy(d[0:np_, :, :s], a[0:np_, :, :s], eng="pool")
            a = d
            i += 1
        return dB if i % 2 == 0 else dA

    def rsq(dst, ps, scale, eps):
        P.act(dst, ps, AF.Sqrt, bias=eps, scale=scale)
        P.op("dve", lambda e: e.reciprocal(dst.ap, dst.ap), [dst], [dst])

    def levels(insts):
        for i in insts:
            src = V(AM[i].ap[:, 0:2, :].unsqueeze(1).to_broadcast([64, 6, 2, 64]), AM[i].keys)
            P.tt(LL[i], src, lvms, ALU.mult, eng=("pool" if i % 2 else "dve"))
        for i in insts:
            P.tt(PQ[i], LL[i][:, 0, :, :], I2, ALU.add)
        for lvl in range(1, 6):
            pls = {}
            for i in insts:
                pl = bank()[0:64, 0:256].re("p (s q) -> p s q", s=4)
                pls[i] = pl
                P.mm(pl[:, 0, :], LL[i][:, lvl, 1, :], PQ[i][:, 0, :])
                P.mm(pl[:, 1, :], LL[i][:, lvl, 0, :], PQ[i][:, 1, :])
            for i in insts:
                P.copy(AB[i], pls[i][:, 0:2, :], eng="act")
            for i in insts:
                P.mm(pls[i][:, 2, :], PQ[i][:, 1, :], AB[i][:, 0, :])
                P.mm(pls[i][:, 3, :], PQ[i][:, 0, :], AB[i][:, 1, :])
            for i in insts:
                P.tt(PQ[i], PQ[i], pls[i][:, 2:4, :], ALU.add)

    for t in range(T // N):
        ts_ = slice(t * N, (t + 1) * N)
        P.dma("sp", x1, x_at(t * N, N))
        rms_feat(P, x1, sq, pb[0][:, 0:N], rstd, ones, 8, N, 1024.0)
        P.tt(tmp, x1, gmb, ALU.mult, eng="pool")
        P.tt(hT[:, :, cur], tmp, V(rstd.ap.unsqueeze(1).to_broadcast([128, 8, N]), rstd.keys), ALU.mult)
        for i in range(nl):
            pd_ = bank()[0:lcols[i], 0:N]
            for k in range(8):
                P.mm(pd_, l1A[i][:, k, :], hT[:, k, cur], start=(k == 0), stop=False)
                P.mm(pd_, l1B[i][:, k, :], hT[:, k, prv], start=False, stop=(k == 7))
            if i == 0:
                P.act(dl[i][0:lcols[i], :], pd_, AF.Tanh)
            elif i == 2:
                P.act(dl[i][0:lcols[i], :], pd_, AF.Sigmoid)
            else:
                P.copy(dl[i][0:lcols[i], :], pd_, eng="act")
        for g in range(2):
            gs = slice(g * 128, (g + 1) * 128)
            for j, dst in enumerate((rr[g], kr[g], vr[g])):
                pz = bank()[:, 0:HW]
                for k in range(8):
                    P.mm(pz, wrkvb[j][:, k, gs], hT[:, k, :], start=(k == 0), stop=(k == 7))
                P.ts(t1, pz[:, prv], mups[:, 2 * j + g:2 * j + g + 1], ALU.mult)
                P.stt(dst, pz[:, cur], omups[:, 2 * j + g:2 * j + g + 1], t1, ALU.mult, ALU.add)
            pu = bank()[:, 0:N]
            P.mm(pu, l2b[0][0:64, 0, gs], dl[0][0:64, :])
            P.act(lw[g], pu, AF.Sigmoid, bias=C(g, 0))
            P.ts(lw[g], lw[g], -math.exp(-0.5), ALU.mult, eng="pool")
            pu = bank()[:, 0:N]
            P.mm(pu, l2b[1][0:64, 0, gs], dl[1][0:64, :])
            P.act(aa[g], pu, AF.Sigmoid, bias=C(g, 1))
            pu = bank()[:, 0:N]
            P.mm(pu, l2b[2][:, 0, gs], dl[2])
            P.copy(gg[g], pu, eng="act")
            if vres:
                pu = bank()[:, 0:N]
                P.mm(pu, l2b[3][0:32, 0, gs], dl[3][0:32, :])
                P.act(t2, pu, AF.Sigmoid, bias=C(g, 6))
                P.dma("sp", t3, vfi[g * 128:(g + 1) * 128, ts_])
                P.tt(t3, t3, vr[g], ALU.subtract)
                P.tt(t3, t3, t2, ALU.mult)
                P.tt(vr[g], vr[g], t3, ALU.add)
            else:
                P.dma("pool", vfo[g * 128:(g + 1) * 128, ts_], vr[g])
            P.ts(t1, kr[g], C(g, 2), ALU.mult)
            P.tt(tb, t1, t1, ALU.mult, eng="pool")
            pk = bank()[:, 0:N]
            P.mm(pk, blk, tb)
            rsq(t2, pk, 1.0, 1e-6)
            P.tt(kkn[g], t1, t2, ALU.mult)
            P.ts(t1, aa[g], -1.0, ALU.add, C(g, 3), ALU.mult)
            P.stt(k2[g], t1, 1.0, kr[g], ALU.add, ALU.mult)
            P.tt(t1, rr[g], k2[g], ALU.mult, eng="pool")
            P.ts(tb, t1, C(g, 7), ALU.mult)
            pk = bank()[:, 0:N]
            P.mm(pk, blk, tb)
            P.tt(bon[g], pk, vr[g], ALU.mult)
            cl = cumsum(lw[g], clA, clB)
            P.act(Wt, cl, AF.Exp)
            P.act(Wi, cl, AF.Exp, scale=-1.0)
            P.tt(t1, cl, lw[g], ALU.subtract)
            P.act(Wp, t1, AF.Exp)
            cl3 = v3(cl)
            P.tt(v3(t1), V(cl3.ap[:, :, 63:64].to_broadcast([128, NC, 64]), cl.keys), cl3, ALU.subtract)
            P.act(Wd, t1, AF.Exp)
            P.act(WC[g], cl3[:, :, 63], AF.Exp)
            P.tt(rt[g], rr[g], Wt, ALU.mult)
            P.tt(kt[g], k2[g], Wi, ALU.mult, eng="pool")
            P.stt(at[g], kkn[g], -1.0, Wp, ALU.mult, ALU.mult)
            P.tt(t2, kkn[g], aa[g], ALU.mult, eng="pool")
            P.tt(bt[g], t2, Wi, ALU.mult)
            P.tt(bd[g], t2, Wd, ALU.mult, eng="pool")
            P.tt(kd[g], k2[g], Wd, ALU.mult)
            P.copy(vb[g], vr[g], eng="pool")
            for cc in range(NC):
                cs = slice(cc * 64, (cc + 1) * 64)
                ptr = ptb[0:64, (cc % 2) * 384:(cc % 2) * 384 + 384].re("p (s c) -> p s c", s=3)
                for j, src in enumerate((bd[g], kd[g], vb[g])):
                    P.tr(ptr[:, j, :], src[:, cs], ident)
                P.copy(tok[g][:, cc, :, :], ptr, eng=("act" if cc % 2 else "dve"))
        for g in range(2):
            gs = slice(g * 128, (g + 1) * 128)
            outs = []
            for j in range(3):
                pz = bank()[:, 0:HW]
                for k in range(8):
                    P.mm(pz, wqkvb[j][:, k, gs], hT[:, k, :], start=(k == 0), stop=(k == 7))
                cw = lambda tap: convs[:, (j * 2 + g) * 4 + tap:(j * 2 + g) * 4 + tap + 1]
                P.ts(t1, pz[:, 0:N], cw(0), ALU.mult)
                for tap in (1, 2, 3):
                    P.stt(t1, pz[:, tap:tap + N], cw(tap), t1, ALU.mult, ALU.add)
                dst = (t2, t3, vg[g])[j]
                P.act(dst, t1, AF.Silu)
            for src, dstb, sc_ in ((t2, qn[g], 128.0 ** -0.5), (t3, kn[g], 1.0)):
                P.tt(tb, src, src, ALU.mult, eng="pool")
                pk = bank()[:, 0:N]
                P.mm(pk, ones, tb)
                rsq(t1, pk, 1.0, 1e-6)
                P.stt(dstb, src, sc_, t1, ALU.mult, ALU.mult)
            P.copy(vgb[g], vg[g], eng="pool")
            pz = bank()[:, 0:N]
            for k in range(8):
                P.mm(pz, wgateb[:, k, gs], hT[:, k, cur], start=(k == 0), stop=(k == 7))
            P.act(sgate[g], pz, AF.Silu)
            pz = bank()[:, 0:N]
            for k in range(8):
                P.mm(pz, wbar[:, k, g, :], hT[:, k, cur], start=(k == 0), stop=(k == 7))
            P.act(betab[g], pz, AF.Sigmoid)
            pz = bank()[:, 0:N]
            for k in range(8):
                P.mm(pz, wbar[:, k, 2 + g, :], hT[:, k, cur], start=(k == 0), stop=(k == 7))
            P.act(t1, pz, AF.Exp, bias=gcs[:, 2 + g:3 + g])
            P.act(t1, t1, AF.Ln, bias=1.0)
            P.ts(t2, t1, nea[:, g:g + 1], ALU.mult)
            gc = cumsum(t2, clA, clB)
            P.copy(gcb[g], gc, eng="pool")
            g3 = v3(gcb[g])
            P.act(egc[g], gcb[g], AF.Exp)
            P.tt(qd[g], qn[g], egc[g], ALU.mult)
            P.act(egl[g], g3[:, :, 63], AF.Exp)
            idb = V(identf.ap[0:64, 0:64].unsqueeze(1).to_broadcast([64, NC, 64]), identf.keys)
            P.tt(d3, g3[0:64], idb, ALU.mult)
            P.red(gcol[g], d3)
            P.tt(d3, v3(betab[g])[0:64], idb, ALU.mult)
            P.red(bcol[g], d3)
            P.act(nbw[g], gcol[g], AF.Exp)
            P.stt(nbw[g], nbw[g], -1.0, bcol[g], ALU.mult, ALU.mult)
            P.tt(dcol[g], g3[0:64, :, 63], gcol[g], ALU.subtract)
            P.act(dcol[g], dcol[g], AF.Exp)
            gcolb = V(gcol[g].ap.unsqueeze(2).to_broadcast([64, NC, 64]), gcol[g].keys)
            bcolb = V(bcol[g].ap.unsqueeze(2).to_broadcast([64, NC, 64]), bcol[g].keys)
            nmUb = V(nmU.ap.unsqueeze(1).to_broadcast([64, NC, 64]), nmU.keys)
            nmLb = V(nmLs.ap.unsqueeze(1).to_broadcast([64, NC, 64]), nmLs.keys)
            sUb = V(sU01.ap.unsqueeze(1).to_broadcast([64, NC, 64]), sU01.keys)
            P.tt(d3, g3[0:64], nmUb, ALU.add)
            P.tt(d3, d3, gcolb, ALU.subtract)
            P.act(M3[g][:, :, 2, :], d3, AF.Exp)
            P.tt(d3b, M3[g][:, :, 2, :], sUb, ALU.mult)
            P.stt(M3[g][:, :, 1, :], d3b, -1.0, v3(betab[g])[0:64], ALU.mult, ALU.mult)
            P.tt(d3, nmLb, g3[0:64], ALU.subtract)
            P.tt(d3, d3, gcolb, ALU.add)
            P.act(d3b, d3, AF.Exp)
            P.stt(M3[g][:, :, 0, :], d3b, -1.0, bcolb, ALU.mult, ALU.mult)
            for cc in range(NC):
                cs = slice(cc * 64, (cc + 1) * 64)
                ptr = ptb[0:64, (cc % 2) * 384:(cc % 2) * 384 + 256].re("p (s c) -> p s c", s=2)
                P.tr(ptr[:, 0, :], kn[g][:, cs], ident)
                P.tr(ptr[:, 1, :], vgb[g][:, cs], ident)
                P.ts(ktok[g][:, cc, :], ptr[:, 0, :], dcol[g][:, cc:cc + 1], ALU.mult)
                P.ts(bvf[g][:, cc, :], ptr[:, 1, :], bcol[g][:, cc:cc + 1], ALU.mult)
        for cc in range(NC):
            cs = slice(cc * 64, (cc + 1) * 64)
            R = []
            for hh in range(4):
                R.append((hh, hh // 2, (hh % 2) * 64))
            pas = {}
            for (i, g, off) in R:
                o_ = slice(off, off + 64)
                pa = bank()[0:64, 0:320].re("p (s q) -> p s q", s=5)
                pas[i] = pa
                P.mm(pa[:, 0, :], at[g][o_, cs], bt[g][o_, cs])
                P.mm(pa[:, 1, :], bt[g][o_, cs], at[g][o_, cs])
                P.mm(pa[:, 2, :], kt[g][o_, cs], at[g][o_, cs])
                P.mm(pa[:, 3, :], bt[g][o_, cs], rt[g][o_, cs])
                P.mm(pa[:, 4, :], kt[g][o_, cs], rt[g][o_, cs])
            for (i, g, off) in R:
                P.tt(AM[i], pas[i], m5, ALU.mult)
            for g in range(2):
                i = 4 + g
                pa = bank()[0:64, 0:192].re("p (s q) -> p s q", s=3)
                pas[i] = pa
                P.mm(pa[:, 0, :], kn[g][:, cs], kn[g][:, cs])
                P.mm(pa[:, 1, :], kn[g][:, cs], kn[g][:, cs])
                P.mm(pa[:, 2, :], kn[g][:, cs], qn[g][:, cs])
            for g in range(2):
                i = 4 + g
                P.tt(AM[i][:, 0:3, :], pas[i], M3[g][:, cc, :, :], ALU.mult)
            levels([0, 1, 2, 3, 4, 5])
            px = {}
            for (i, g, off) in R:
                o_ = slice(off, off + 64)
                p_ = bank()
                px[i] = p_
                X = p_[0:64, 0:64]
                tk = P.mm(X, AM[i][:, 2, :], tok[g][:, cc, 2, o_], start=True, stop=False)
                P.mm(X, at[g][o_, cs], Sb[g][o_, o_], start=False, stop=True, after=(tk if off else None))
                P.copy(Xs[i][:, 0:64], X, eng="act")
            for (i, g, off) in R:
                Z = px[i][0:64, 64:128]
                P.mm(Z, PQ[i][:, 1, :], Xs[i][:, 0:64])
                P.copy(Zs[i][:, off:off + 64], Z, eng="act")
            for (i, g, off) in R:
                o_ = slice(off, off + 64)
                Y = px[i][:, 128:192]
                tk = P.mm(Y, Sb[g][o_, :], rt[g][o_, cs], start=True, stop=False)
                P.mm(Y, Zs[i], AM[i][:, 3, :], start=False, stop=False, after=(tk if off else None))
                P.mm(Y, tok[g][:, cc, 2, :], AM[i][:, 4, :], start=False, stop=True)
                P.copy(yT[g][o_, cs], Y[o_, :], eng="dve")
                Sn = px[i][:, 192:256]
                P.mm(Sn, tok[g][:, cc, 0, :], Zs[i][:, off:off + 64], start=True, stop=False)
                P.mm(Sn, tok[g][:, cc, 1, :], tok[g][:, cc, 2, o_], start=False, stop=True)
                P.stt(Sf[g][o_, :], Sf[g][o_, :], WC[g][o_, cc:cc + 1], Sn[o_, :], ALU.mult, ALU.add)
                P.copy(Sb[g][o_, o_], Sf[g][o_, :], eng="pool")
            for g in range(2):
                i = 4 + g
                p_ = bank()
                px[i] = p_
                KS = p_[0:64, 0:128]
                P.mm(KS, kn[g][:, cs], Gb[g])
                P.stt(Xs[i], KS, nbw[g][:, cc:cc + 1], bvf[g][:, cc, :], ALU.mult, ALU.add)
            for g in range(2):
                i = 4 + g
                Z = px[i][0:64, 128:256]
                P.mm(Z, PQ[i][:, 1, :], Xs[i])
                P.copy(Zs[i], Z, eng="act")
            for g in range(2):
                i = 4 + g
                Y = px[i][:, 256:320]
                P.mm(Y, Gb[g], qd[g][:, cs], start=True, stop=False)
                P.mm(Y, Zs[i], AM[i][:, 2, :], start=False, stop=True)
                P.copy(yg[g][:, cs], Y, eng="dve")
                Sn = px[i][:, 320:448]
                P.mm(Sn, ktok[g][:, cc, :], Zs[i])
                P.stt(Gf[g], Gf[g], egl[g][:, cc:cc + 1], Sn, ALU.mult, ALU.add)
                P.copy(Gb[g], Gf[g], eng="pool")
        for g in range(2):
            P.copy(tb, yT[g], eng="pool")
            pm_ = bank()[:, 0:N]
            P.mm(pm_, blk, tb)
            P.ts(t1, pm_, 1.0 / 64, ALU.mult)
            P.tt(t2, yT[g], t1, ALU.subtract)
            P.tt(tb, t2, t2, ALU.mult, eng="pool")
            pv_ = bank()[:, 0:N]
            P.mm(pv_, blk, tb)
            rsq(t3, pv_, 1.0 / 64, 64e-5)
            P.tt(t2, t2, t3, ALU.mult)
            P.ts(t2, t2, C(g, 4), ALU.mult, C(g, 5), ALU.add)
            P.tt(t2, t2, bon[g], ALU.add)
            P.tt(ycat[:, g, :], t2, gg[g], ALU.mult)
            P.tt(tb, yg[g], yg[g], ALU.mult, eng="pool")
            pv_ = bank()[:, 0:N]
            P.mm(pv_, ones, tb)
            rsq(t3, pv_, 1.0 / 128, EPS)
            P.stt(t1, yg[g], gcs[:, 4 + g:5 + g], t3, ALU.mult, ALU.mult)
            P.tt(ycat[:, 2 + g, :], t1, sgate[g], ALU.mult)
        for dc in range(8):
            pm_ = bank()[:, 0:N]
            for q4 in range(4):
                P.mm(pm_, wob[:, q4, dc * 128:(dc + 1) * 128], ycat[:, q4, :], start=(q4 == 0), stop=(q4 == 3))
            P.copy(mo[:, dc, :], pm_, eng=("act" if dc % 2 else "dve"))
        P.dma("pool", m_at(t * N, N), mo)
        if io.get("after_tile"):
            io["after_tile"]((t + 1) * N)
        P.copy(hT[:, :, 0:3], hT[:, :, N:N + 3], eng="pool")


def even_inputs(inp, e, b, hh, xT_b, vfi=None):
    c = np.ascontiguousarray
    f = lambda k: np.asarray(inp[k][e], np.float32)
    W = f("even_w_in")
    o_bq = 1536; o_bg = o_bq + 1536; o_bb = o_bg + 512; o_ba = o_bb + 4
    a = slice(hh * 256, hh * 256 + 256)
    col2 = lambda v: c(np.asarray(v, np.float32)[a].reshape(2, 128).T)
    d = {"gm": c(np.asarray(inp["norm_mix_g"][2 * e], np.float32).reshape(8, 128).T)}
    d["wr"] = c(W[:, 0:512][:, a]); d["wk"] = c(W[:, 512:1024][:, a]); d["wv"] = c(W[:, 1024:1536][:, a])
    d["wbq"] = c(W[:, o_bq:o_bq + 512][:, a]); d["wbk"] = c(W[:, o_bq + 512:o_bq + 1024][:, a]); d["wbv"] = c(W[:, o_bq + 1024:o_bq + 1536][:, a])
    d["wgate"] = c(W[:, o_bg:o_bg + 512][:, a])
    d["wba"] = c(np.concatenate([W[:, o_bb + 2 * hh:o_bb + 2 * hh + 2], W[:, o_ba + 2 * hh:o_ba + 2 * hh + 2]], 1))
    mp = f("rwkv_mu_proj")
    d["mup"] = c(np.concatenate([col2(mp[j]) for j in range(3)], 1))
    ml = f("rwkv_mu_lora")
    mus = [ml[0], ml[1], ml[2], (np.asarray(inp["rwkv_v_mu"][e - 1], np.float32) if e > 0 else np.zeros(1024, np.float32))]
    d["mul"] = c(np.concatenate([m.reshape(8, 128).T for m in mus], 1))
    v0 = np.asarray(inp["rwkv_v0"][e - 1], np.float32) if e > 0 else np.zeros(512, np.float32)
    rk = f("rwkv_r_k").reshape(512)
    d["cols"] = c(np.concatenate([col2(v) for v in (f("rwkv_w0"), f("rwkv_a0"), f("rwkv_k_k"), f("rwkv_k_a"),
                                                     f("rwkv_ln_g"), f("rwkv_ln_b"), v0, rk)], 1))
    d["w1"] = f("rwkv_w1"); d["a1"] = f("rwkv_a1"); d["g1"] = f("rwkv_g1")
    d["w2"] = c(f("rwkv_w2")[:, a]); d["a2"] = c(f("rwkv_a2")[:, a]); d["g2"] = c(f("rwkv_g2")[:, a])
    if e > 0:
        d["v1"] = np.asarray(inp["rwkv_v1"][e - 1], np.float32); d["v2"] = c(np.asarray(inp["rwkv_v2"][e - 1], np.float32)[:, a])
    else:
        d["v1"] = np.zeros((1024, 32), np.float32); d["v2"] = np.zeros((32, 256), np.float32)
    cw = f("gdn_conv_w")
    cws = []
    for j in range(3):
        for g in range(2):
            ch = slice(j * 512 + hh * 256 + g * 128, j * 512 + hh * 256 + g * 128 + 128)
            cws.append(cw[:, ch].T)
    d["convw"] = c(np.concatenate(cws, 1))
    al = f("gdn_a_log")[2 * hh:2 * hh + 2]; dtb = f("gdn_dt_bias")[2 * hh:2 * hh + 2]; ng = f("gdn_norm_g")
    gcols = np.zeros((128, 8), np.float32)
    gcols[:, 0] = al[0]; gcols[:, 1] = al[1]; gcols[:, 2] = dtb[0]; gcols[:, 3] = dtb[1]; gcols[:, 4] = ng; gcols[:, 5] = ng
    d["gcols"] = gcols
    wo = f("even_w_out")
    d["wo"] = c(np.concatenate([wo[0:512][a], wo[512:1024][a]], 0))
    d["lvm"] = level_masks()
    if xT_b is not None:
        d["xT"] = xT_b
    if vfi is not None:
        d["vfi"] = vfi
    return d


def level_masks():
    t = np.arange(64)[:, None]
    j = np.arange(64)[None, :]
    out = np.zeros((64, 6, 2, 64), np.float32)
    for k in range(6):
        mq = ((t >> (k + 1)) == (j >> (k + 1))) & (((t >> k) & 1) == 1) & (((j >> k) & 1) == 0)
        out[:, k, 0, :] = mq
        out[:, k, 1, :] = mq.T
    return np.ascontiguousarray(out.reshape(64, 6 * 2 * 64))


PAIRS = [[0, 1], [2, 3], [4, 5], [6, 7]]


def build_fused(T=8192):
    H = T // 2
    nc = bass.Bass("TRN2", target_bir_lowering=False)
    P = Prog(nc)
    ext = lambda name, shape: P.dram(name, shape, F32, "ExternalInput")
    xT = ext("xT", [1024, T])
    xown = ext("xown", [1024, H])
    pTs = [ext("pT%d" % i, [256, H]) for i in range(4)]
    oT = P.dram("oT", [1024, H], F32, "ExternalOutput")
    CW = 512
    NCH = H // CW
    mk = lambda nm, rows: [[P.dram("%s%d_%d" % (nm, i, c), [rows, CW], F32, "Internal") for c in range(NCH)] for i in range(2)]
    xg, mp, mr, xo = mk("xg", 2048), mk("mp", 2048), mk("mr", 1024), mk("xo", 1024)
    vf = P.dram("vf", [256, T], F32, "Internal")

    def acc2(chunks):
        vs = [c.re("(h k p) t -> h p k t", h=2, p=128) for c in chunks]
        return lambda t0, n: vs[(t0 % H) // CW][t0 // H][:, :, (t0 % CW):(t0 % CW) + n]

    def acc1(chunks):
        vs = [c.re("(k p) t -> p k t", p=128) for c in chunks]
        return lambda t0, n: vs[t0 // CW][:, :, (t0 % CW):(t0 % CW) + n]

    def rs_hook(i):
        def hook(tok_end):
            if tok_end > H and (tok_end - H) % CW == 0:
                c = (tok_end - H) // CW - 1
                P.cc("ReduceScatter", ALU.add, PAIRS, mp[i % 2][c], mr[i % 2][c])
        return hook

    def ag_hook(i):
        def hook(tok_end):
            if tok_end % CW == 0:
                c = tok_end // CW - 1
                P.cc("AllGather", ALU.bypass, PAIRS, xo[i % 2][c], xg[(i + 1) % 2][c])
        return hook

    P.prefix = "L0m_"
    xown_at = (lambda t0, n, _v=xown.re("(k p) t -> p k t", p=128): _v[:, :, t0:t0 + n])
    for i in range(4):
        if i == 0:
            x_at = (lambda t0, n, _v=xT.re("(k p) t -> p k t", p=128): _v[:, :, t0:t0 + n])
        else:
            x_at = acc2(xg[i % 2])
        pre = "L%dm_" % i
        if i % 2 == 0:
            io = {k: ext(pre + k, sh) for k, sh in EVEN_IN.items()}
            io["vfo" if i == 0 else "vfi"] = vf
            io["x_at"] = x_at
            io["m_at"] = acc2(mp[i % 2])
            io["after_tile"] = rs_hook(i)
            phase_even(P, io, T, vres=(i > 0))
        else:
            io = {k: ext(pre + k, sh) for k, sh in ATTN_IN.items()}
            io["x_at"] = x_at
            io["m_at"] = acc2(mp[i % 2])
            io["after_tile"] = rs_hook(i)
            phase_attn(P, io, T)
        P.end_phase("L%dp_" % i)
        pre = "L%dp_" % i
        io = {k: ext(pre + k, sh) for k, sh in POST_IN.items()}
        io.update(pT=pTs[i], x_at=xown_at, m0_at=acc1(mr[i % 2]))
        if i == 3:
            io["oT"] = oT
        else:
            io["o_at"] = acc1(xo[i % 2])
        if i < 3:
            io["after_tile"] = ag_hook(i)
        phase_post(P, io, NT=H)
        if i < 3:
            xown_at = acc1(xo[i % 2])
        P.end_phase("L%dm_" % (i + 1))
    P.emit()
    return nc


def fused_inputs(inp, b, hh):
    c = np.ascontiguousarray
    f32 = lambda a: np.asarray(a, np.float32)
    S = inp["x"].shape[1]
    H = S // 2
    sl = slice(hh * H, (hh + 1) * H)
    xt = c(f32(inp["x"][b]).T)
    d = {"xT": xt, "xown": c(xt[:, sl])}
    for i in range(4):
        d["pT%d" % i] = c(f32(inp["p"][i, b, sl]).T)
        pre = "L%dm_" % i
        if i % 2 == 0:
            di = even_inputs(inp, i // 2, b, hh, None)
            for k in EVEN_IN:
                d[pre + k] = di[k]
        else:
            o = i // 2
            wqkv = f32(inp["attn_w_qkv"][o]); wo = f32(inp["attn_w_out"][o])
            bE, bO = attn_bias_layouts(f32(inp["attn_rel_bias"][o])[hh * 8:(hh + 1) * 8])
            da = {"gm": c(f32(inp["norm_mix_g"][i]).reshape(8, 128).T), "wq": c(wqkv[:, hh * 512:(hh + 1) * 512]),
                  "wk": c(wqkv[:, 1024 + hh * 512:1024 + (hh + 1) * 512]), "wv": c(wqkv[:, 2048 + hh * 512:2048 + (hh + 1) * 512]),
                  "wo": c(wo[hh * 512:(hh + 1) * 512]), "qg": c(np.tile(f32(inp["attn_q_g"][o]), 2)[:, None]),
                  "kg": c(np.tile(f32(inp["attn_k_g"][o]), 2)[:, None]), "bE": bE, "bO": bO}
            for k in ATTN_IN:
                d[pre + k] = da[k]
        pre = "L%dp_" % i
        dp = {"gf": c(f32(inp["norm_ffn_g"][i]).reshape(8, 128).T), "gp": c(f32(inp["ple_norm_g"][i]).reshape(8, 128).T),
              "w1": c(f32(inp["mlp_w1"][i])), "w2": c(f32(inp["mlp_w2"][i])), "wg": c(f32(inp["ple_w_gate"][i])), "wp": c(f32(inp["ple_w_proj"][i]))}
        for k in POST_IN:
            d[pre + k] = dp[k]
    return d


_NC = {}


def kernel(**inp):
    inp = {k: np.asarray(v) for k, v in inp.items()}
    B, S, D = inp["x"].shape
    if "fused" not in _NC:
        _NC["fused"] = build_fused(T=S)
    in_maps = [fused_inputs(inp, c // 2, c % 2) for c in range(2 * B)]
    res = run_bass_kernel_spmd(_NC["fused"], in_maps, core_ids=list(range(2 * B)))
    out = [np.concatenate([res.results[2 * b]["oT"], res.results[2 * b + 1]["oT"]], axis=1).T for b in range(B)]
    return np.ascontiguousarray(np.stack(out, 0)).astype(np.float32)
```

```python
import math
import contextlib
import numpy as np
import concourse.bass as bass
import concourse.mybir as mybir
from concourse.bass_utils import run_bass_kernel_spmd

F32 = mybir.dt.float32
BF16 = mybir.dt.bfloat16
AF = mybir.ActivationFunctionType
ALU = mybir.AluOpType
AX = mybir.AxisListType

SEM_LIMIT = 16000
N_DMA_SEMS = 10


class V:
    __slots__ = ("ap", "keys")

    def __init__(self, ap, keys):
        self.ap = ap
        self.keys = tuple(keys)

    def __getitem__(self, idx):
        return V(self.ap[idx], self.keys)

    def k(self, *sub):
        return V(self.ap, tuple((k0,) + tuple(sub) for k0 in self.keys))

    def re(self, pat, **kw):
        return V(self.ap.rearrange(pat, **kw), self.keys)

    def bc(self, shape):
        return V(self.ap.to_broadcast(shape), self.keys)


class Prog:
    ENGS = ("pe", "act", "dve", "pool", "sp")

    def __init__(self, nc):
        self.nc = nc
        self.es = contextlib.ExitStack()
        self.gs = contextlib.ExitStack()
        self.prefix = ""
        self.barrier = {}
        self.phase_dma = set()
        self.q = {e: [] for e in self.ENGS}
        self.cnt = {e: 0 for e in self.ENGS}
        self.waited = {e: {} for e in self.ENGS}
        self.last_w = {}
        self.readers = {}
        self.sems = {}
        self.dma_sems = {}
        self.dma_cnt = {}
        self.dma_rr = {e: 0 for e in self.ENGS}
        self.nbuf = 0
        self.out_tokens = []

    def sb(self, name, shape, dtype):
        name = self.prefix + name
        t = self.es.enter_context(self.nc.sbuf_tensor(name, list(shape), dtype))
        return V(t.ap() if hasattr(t, "ap") and callable(t.ap) else t[:], (name,))

    def ps(self, name, shape, dtype):
        name = self.prefix + name
        t = self.es.enter_context(self.nc.psum_tensor(name, list(shape), dtype))
        return V(t.ap() if hasattr(t, "ap") and callable(t.ap) else t[:], (name,))

    def dram(self, name, shape, dtype, kind):
        t = self.nc.dram_tensor(name, list(shape), dtype, kind=kind)
        return V(t.ap(), ("dram:" + name,))

    def _sem(self, name):
        if name not in self.sems:
            self.sems[name] = self.gs.enter_context(self.nc.semaphore(name))
        return self.sems[name]

    def _token(self, eng):
        self.cnt[eng] += 1
        i = self.cnt[eng]
        ep = (i - 1) // SEM_LIMIT
        return ("c", eng, ep, (i - 1) % SEM_LIMIT + 1)

    def _dma_token(self, eng):
        j = self.dma_rr[eng] % N_DMA_SEMS
        self.dma_rr[eng] += 1
        name = "d_%s_%d" % (eng, j)
        n = self.dma_cnt.get(name, 0) + 1
        self.dma_cnt[name] = n
        ep = (n - 1) // (SEM_LIMIT // 16)
        nn = (n - 1) % (SEM_LIMIT // 16) + 1
        prev = None
        if nn > 1:
            prev = ("d", name, ep, (nn - 1) * 16)
        return ("d", name, ep, nn * 16), prev

    def _deps(self, reads, writes, eng=None):
        deps = set()
        for k in reads:
            if k in self.last_w:
                deps.add(self.last_w[k])
        same = (lambda t: t[0] == "c" and t[1] == eng) if eng in ("act", "dve") else (lambda t: False)
        for k in writes:
            if k in self.last_w and not same(self.last_w[k]):
                deps.add(self.last_w[k])
            for r in self.readers.get(k, ()):
                if not same(r):
                    deps.add(r)
        return deps

    def _commit(self, tok, reads, writes):
        for k in writes:
            self.last_w[k] = tok
            self.readers[k] = []
        for k in reads:
            if k not in writes:
                self.readers.setdefault(k, []).append(tok)

    def _waits(self, eng, deps, force=()):
        out = []
        w = self.waited[eng]
        best = {}
        for d in deps:
            kind, nm, ep, val = d
            if kind == "c" and nm == eng and eng == "pe" and d not in force:
                continue
            key = (kind, nm, ep)
            if w.get(key, 0) >= val:
                continue
            if best.get(key, 0) < val:
                best[key] = val
        for key, val in best.items():
            w[key] = val
            out.append((key, val))
        return out

    def op(self, eng, fn, reads, writes, after=None):
        reads = [k for v in reads for k in (v.keys if isinstance(v, V) else (v,))]
        writes = [k for v in writes for k in (v.keys if isinstance(v, V) else (v,))]
        deps = self._deps(reads, writes, eng)
        force = ()
        if after is not None:
            deps.add(after)
            force = (after,)
        if eng in self.barrier:
            b = self.barrier.pop(eng)
            deps |= b
            force = tuple(force) + tuple(b)
        waits = self._waits(eng, deps, force)
        tok = self._token(eng)
        self.q[eng].append((waits, fn, tok))
        self._commit(tok, reads, writes)
        return tok

    def dma(self, eng, out, in_, **kw):
        reads = list(in_.keys)
        writes = list(out.keys)
        deps = self._deps(reads, writes)
        tok, prev = self._dma_token(eng)
        if prev is not None:
            deps.add(prev)
        if eng in self.barrier:
            deps |= self.barrier.pop(eng)
        waits = self._waits(eng, deps)
        o, i = out.ap, in_.ap
        self.q[eng].append((waits, lambda e: e.dma_start(out=o, in_=i, **kw), tok))
        self._commit(tok, reads, writes)
        self.phase_dma.add(tok)
        if any(k.startswith("dram:") for k in writes if isinstance(k, str)):
            self.out_tokens.append(tok)
        return tok

    def cc(self, kind, op, groups, in_, out, inc=1):
        reads = list(in_.keys)
        writes = list(out.keys)
        deps = self._deps(reads, writes)
        if inc == 16:
            tok, prev = self._dma_token("pool")
        else:
            self.ccn = getattr(self, "ccn", 0) + 1
            tok, prev = ("k", "cc", 0, self.ccn), (("k", "cc", 0, self.ccn - 1) if self.ccn > 1 else None)
        if prev is not None:
            deps.add(prev)
        waits = self._waits("pool", deps)
        o, i = out.ap, in_.ap
        self.q["pool"].append((waits, lambda e: e.collective_compute(kind, op, replica_groups=groups, ins=[i], outs=[o]), tok))
        self._commit(tok, reads, writes)
        self.phase_dma.add(tok)
        return tok

    def mm(self, out, lhsT, rhs, start=True, stop=True, after=None):
        o, l, r = out.ap, lhsT.ap, rhs.ap
        rd = [lhsT, rhs] + ([] if start else [out])
        return self.op("pe", lambda e: e.matmul(o, l, r, start=start, stop=stop), rd, [out], after=after)

    def tr(self, out, in_, ident):
        o, i, d = out.ap, in_.ap, ident.ap
        return self.op("pe", lambda e: e.transpose(o, i, d), [in_, ident], [out])

    def act(self, out, in_, func, bias=None, scale=1.0, accum=None, eng="act"):
        o, i = out.ap, in_.ap
        rd = [in_]
        kw = {}
        if bias is not None:
            if isinstance(bias, V):
                rd.append(bias)
                kw["bias"] = bias.ap
            else:
                kw["bias"] = bias
        if isinstance(scale, V):
            rd.append(scale)
            kw["scale"] = scale.ap
        else:
            kw["scale"] = scale
        wr = [out]
        if accum is not None:
            kw["accum_out"] = accum.ap
            wr.append(accum)
        return self.op("act", lambda e: e.activation(o, i, func, **kw), rd, wr)

    def tt(self, out, a, b, op, eng="dve"):
        o, x, y = out.ap, a.ap, b.ap
        return self.op(eng, lambda e: e.tensor_tensor(o, x, y, op), [a, b], [out])

    def ts(self, out, a, s1, op0, s2=None, op1=None, eng="dve", accum=None):
        o, x = out.ap, a.ap
        rd = [a]
        if isinstance(s1, V):
            rd.append(s1)
            s1 = s1.ap
        if isinstance(s2, V):
            rd.append(s2)
            s2 = s2.ap
        wr = [out]
        kw = {}
        if accum is not None:
            kw["accum_out"] = accum.ap
            wr.append(accum)
        if op1 is None:
            return self.op(eng, lambda e: e.tensor_scalar(o, x, s1, None, op0, **kw), rd, wr)
        return self.op(eng, lambda e: e.tensor_scalar(o, x, s1, s2, op0, op1, **kw), rd, wr)

    def stt(self, out, a, s, b, op0, op1, eng="dve"):
        o, x, y = out.ap, a.ap, b.ap
        rd = [a, b]
        if isinstance(s, V):
            rd.append(s)
            s = s.ap
        return self.op(eng, lambda e: e.scalar_tensor_tensor(o, x, s, y, op0, op1), rd, [out])

    def copy(self, out, in_, eng="dve"):
        o, i = out.ap, in_.ap
        if eng == "act":
            return self.op("act", lambda e: e.copy(o, i), [in_], [out])
        return self.op(eng, lambda e: e.tensor_copy(o, i), [in_], [out])

    def memset(self, out, val, eng="pool"):
        o = out.ap
        return self.op(eng, lambda e: e.memset(o, val), [], [out])

    def red(self, out, in_, op=None, eng="dve"):
        o, i = out.ap, in_.ap
        op = op or ALU.add
        return self.op(eng, lambda e: e.tensor_reduce(o, i, AX.X, op), [in_], [out])

    def _semh(self, key):
        kind, nm, ep = key
        return self._sem("%s_%s_%d" % (kind, nm, ep))

    def flush(self, final=False):
        nc = self.nc
        fin = self._waits("sp", set(self.out_tokens)) if final else []
        for e in self.ENGS:
            for waits, fn, tok in self.q[e]:
                self._semh(tok[:3])
                for key, val in waits:
                    self._semh(key)
        for key, val in fin:
            self._semh(key)
        qs = self.q
        semh = self._semh

        def run(eng_name):
            def body(e):
                for waits, fn, tok in qs[eng_name]:
                    for key, val in waits:
                        e.wait_ge(semh(key), val)
                    ins = fn(e)
                    ins.then_inc(semh(tok[:3]), 16 if tok[0] == "d" else 1)
                if eng_name == "sp":
                    for key, val in fin:
                        e.wait_ge(semh(key), val)
            return body

        with nc.Block() as block:
            block.tensor(run("pe"))
            block.scalar(run("act"))
            block.vector(run("dve"))
            block.gpsimd(run("pool"))
            block.sync(run("sp"))
        self.q = {e: [] for e in self.ENGS}

    def end_phase(self, next_prefix):
        toks = set(self.phase_dma)
        for e in ("pe", "act", "dve", "pool"):
            if self.cnt[e] > 0:
                i = self.cnt[e]
                toks.add(("c", e, (i - 1) // SEM_LIMIT, (i - 1) % SEM_LIMIT + 1))
        self.flush()
        self.es.close()
        self.es = contextlib.ExitStack()
        self.phase_dma = set()
        self.barrier = {e: set(toks) for e in self.ENGS}
        self.prefix = next_prefix

    def emit(self):
        self.flush(final=True)
        self.es.close()
        self.gs.close()


EPS = 1e-6


def load_w(P, dst, src, K, ncols, stage, ci):
    sv = src.re("(k p) n -> p k n", p=128)
    engs = ("dve", "pool", "act")
    step = stage[0].ap.shape[-1]
    for k in range(K):
        for c0 in range(0, ncols, step):
            c1 = min(ncols, c0 + step)
            st = stage[ci[0] % len(stage)]
            P.dma("sp", st[:, 0:c1 - c0], sv[:, k, c0:c1])
            P.copy(dst[:, k, c0:c1], st[:, 0:c1 - c0], eng=engs[ci[0] % 3])
            ci[0] += 1


def rms_feat(P, x1, sq, ssps, rstd, ones, K, N, scale_dim):
    P.act(sq, x1, AF.Square)
    for k in range(K):
        P.mm(ssps, ones, sq[:, k, :], start=(k == 0), stop=(k == K - 1))
    P.act(rstd, ssps, AF.Sqrt, bias=EPS, scale=1.0 / scale_dim)
    P.op("dve", lambda e: e.reciprocal(rstd.ap, rstd.ap), [rstd], [rstd])


POST_IN = {"gf": [128, 8], "gp": [128, 8], "w1": [1024, 4096], "w2": [4096, 1024], "wg": [1024, 1024], "wp": [256, 1024]}


def build_post(NT=4096, N=128):
    nc = bass.Bass("TRN2", target_bir_lowering=False)
    P = Prog(nc)
    D, PD = 1024, 256
    io = {k: P.dram(k, sh, F32, "ExternalInput") for k, sh in POST_IN.items()}
    for k in ("xT", "m0", "m1"):
        io[k] = P.dram(k, [D, NT], F32, "ExternalInput")
    io["pT"] = P.dram("pT", [PD, NT], F32, "ExternalInput")
    io["oT"] = P.dram("oT", [D, NT], F32, "ExternalOutput")
    phase_post(P, io, NT, N)
    P.emit()
    return nc


def phase_post(P, io, NT=4096, N=128):
    D, FF, PD = 1024, 4096, 256
    pT, gf, gp, w1, w2, wg, wp = (io[k] for k in ("pT", "gf", "gp", "w1", "w2", "wg", "wp"))
    xT, m0, oT = io.get("xT"), io.get("m0"), io.get("oT")
    m1 = io.get("m1")

    w1b = P.sb("w1b", [128, 8, FF], BF16)
    w2b = P.sb("w2b", [128, 32, D], BF16)
    wgb = P.sb("wgb", [128, 8, D], BF16)
    wpb = P.sb("wpb", [128, 2, D], BF16)
    stage = [P.sb("stg%d" % i, [128, 512], F32) for i in range(2)]
    gfs = P.sb("gfs", [128, 8], F32)
    gps = P.sb("gps", [128, 8], F32)
    ones = P.sb("ones", [128, 128], BF16)
    x1 = P.sb("x1", [128, 8, N], F32)
    sg = P.sb("sg", [128, 8, N], F32)
    ee = P.sb("ee", [128, 8, N], F32)
    ma = sg
    mb = ee if m1 is not None else None
    h = P.sb("h", [128, 8, N], BF16)
    a1 = P.sb("a1", [128, 32, N], BF16)
    sq = a1[:, 0:8, :]
    rl = [P.sb("rl%d" % i, [128, N], F32) for i in range(2)]
    x2b = h
    rstd = P.sb("rstd", [128, N], F32)
    rstd2 = P.sb("rstd2", [128, N], F32)
    pst = P.sb("pst", [128, 2, N], F32)
    pbb = P.sb("pbb", [128, 2, N], BF16)
    pb = [P.ps("pb%d" % i, [128, 512], F32) for i in range(8)]

    P.memset(ones, 1.0)
    P.dma("sp", gfs, gf)
    P.dma("sp", gps, gp)
    ci = [0]
    load_w(P, w1b, w1, 8, FF, stage, ci)
    load_w(P, w2b, w2, 32, D, stage, ci)
    load_w(P, wgb, wg, 8, D, stage, ci)
    load_w(P, wpb, wp, 2, D, stage, ci)

    _acc = lambda v: (lambda t0, n, _v=v.re("(k p) t -> p k t", p=128): _v[:, :, t0:t0 + n])
    x_at = io.get("x_at") or _acc(xT)
    m0_at = io.get("m0_at") or _acc(m0)
    o_at = io.get("o_at") or _acc(oT)
    m1v = m1.re("(k p) t -> p k t", p=128) if m1 is not None else None
    pv = pT.re("(k p) t -> p k t", p=128)
    gfb = V(gfs.ap.unsqueeze(2).to_broadcast([128, 8, N]), gfs.keys)
    gpb = V(gps.ap.unsqueeze(2).to_broadcast([128, 8, N]), gps.keys)

    def bcN(r):
        return V(r.ap.unsqueeze(1).to_broadcast([128, 8, N]), r.keys)

    for t in range(NT // N):
        ts_ = slice(t * N, (t + 1) * N)
        P.dma("sp", x1, x_at(t * N, N))
        P.dma("sp", ma, m0_at(t * N, N))
        if m1v is not None:
            P.dma("sp", mb, m1v[:, :, ts_])
        P.dma("sp", pst, pv[:, :, ts_])
        P.tt(x1, x1, ma, ALU.add)
        if m1v is not None:
            P.tt(x1, x1, mb, ALU.add, eng="pool")
        rms_feat(P, x1, sq, pb[0][:, 0:N], rstd, ones, 8, N, 1024.0)
        P.tt(sg, x1, gfb, ALU.mult, eng="pool")
        P.tt(h, sg, bcN(rstd), ALU.mult)
        for fc in range(32):
            pu = pb[1 + fc % 2][:, 0:N]
            for k in range(8):
                P.mm(pu, w1b[:, k, fc * 128:(fc + 1) * 128], h[:, k, :], start=(k == 0), stop=(k == 7))
            r = rl[fc % 2]
            P.act(r, pu, AF.Relu)
            P.tt(a1[:, fc, :], r, r, ALU.mult, eng=("pool" if fc % 2 else "dve"))
        for dc in range(8):
            pd = pb[3 + dc % 2][:, 0:N]
            for fk in range(32):
                P.mm(pd, w2b[:, fk, dc * 128:(dc + 1) * 128], a1[:, fk, :], start=(fk == 0), stop=(fk == 31))
            P.tt(x1[:, dc, :], x1[:, dc, :], pd, ALU.add)
        P.copy(x2b, x1, eng="pool")
        for dc in range(8):
            pg = pb[5 + dc % 2][:, 0:N]
            for k in range(8):
                P.mm(pg, wgb[:, k, dc * 128:(dc + 1) * 128], x2b[:, k, :], start=(k == 0), stop=(k == 7))
            P.act(sg[:, dc, :], pg, AF.Sigmoid)
        P.copy(pbb, pst, eng="pool")
        for dc in range(8):
            pp = pb[7 if dc % 2 else 0][:, 0:N]
            for k in range(2):
                P.mm(pp, wpb[:, k, dc * 128:(dc + 1) * 128], pbb[:, k, :], start=(k == 0), stop=(k == 1))
            P.copy(ee[:, dc, :], pp, eng="dve")
        rms_feat(P, ee, sq, pb[0][:, 0:N], rstd2, ones, 8, N, 1024.0)
        P.tt(ee, ee, gpb, ALU.mult, eng="pool")
        P.tt(ee, ee, bcN(rstd2), ALU.mult)
        P.tt(ee, ee, sg, ALU.mult, eng="pool")
        P.tt(ee, ee, x1, ALU.add)
        P.dma("pool", o_at(t * N, N), ee)
        if io.get("after_tile"):
            io["after_tile"]((t + 1) * N)


ATTN_IN = {"gm": [128, 8], "wq": [1024, 512], "wk": [1024, 512], "wv": [1024, 512], "wo": [512, 1024],
           "qg": [128, 1], "kg": [128, 1], "bE": [128, 8 * 5 * 64], "bO": [128, 8 * 5 * 64]}


def build_attn(T=8192):
    nc = bass.Bass("TRN2", target_bir_lowering=False)
    P = Prog(nc)
    io = {k: P.dram(k, sh, F32, "ExternalInput") for k, sh in ATTN_IN.items()}
    io["xT"] = P.dram("xT", [1024, T], F32, "ExternalInput")
    io["mT"] = P.dram("mT", [1024, T], F32, "ExternalOutput")
    phase_attn(P, io, T)
    P.emit()
    return nc


def phase_attn(P, io, T=8192):
    N = 128
    D = 1024
    gm, wq, wk, wv, wo, qg, kg, bE, bO = (io[k] for k in ("gm", "wq", "wk", "wv", "wo", "qg", "kg", "bE", "bO"))
    xT, mT = io.get("xT"), io.get("mT")

    wqb = P.sb("wqb", [128, 8, 512], BF16)
    wkb = P.sb("wkb", [128, 8, 512], BF16)
    wvb = P.sb("wvb", [128, 8, 512], BF16)
    wob = P.sb("wob", [128, 4, D], BF16)
    stage = [P.sb("stg%d" % i, [128, 512], F32) for i in range(2)]
    gms = P.sb("gms", [128, 8], F32)
    qgs = P.sb("qgs", [128, 1], F32)
    kgs = P.sb("kgs", [128, 1], F32)
    bEs = P.sb("bEs", [128, 8, 5, 64], F32)
    bOs = P.sb("bOs", [128, 8, 5, 64], F32)
    ones = P.sb("ones", [128, 128], BF16)
    blk = P.sb("blk", [128, 128], BF16)
    kTa = P.sb("kTa", [128, 4, T], BF16)
    Vt = P.sb("Vt", [128, T // 128, 512], BF16)
    qTt = P.sb("qTt", [128, 4, N], BF16)
    x1 = P.sb("x1", [128, 8, N], F32)
    sq = P.sb("sq", [128, 8, N], BF16)
    h = P.sb("h", [128, 8, N], BF16)
    tmp = P.sb("tmp", [128, 8, N], F32)
    rstd = P.sb("rstd", [128, N], F32)
    raw = [P.sb("raw%d" % i, [128, N], F32) for i in range(2)]
    sq1 = [P.sb("sq1%d" % i, [128, N], BF16) for i in range(2)]
    rs1 = [P.sb("rs1%d" % i, [128, N], F32) for i in range(2)]
    sc = [P.sb("sc%d" % i, [128, 5, 64], F32) for i in range(2)]
    pT = [P.sb("pT%d" % i, [128, 5, 64], BF16) for i in range(2)]
    rden = [P.sb("rden%d" % i, [128, 64], F32) for i in range(2)]
    ao = P.sb("ao", [128, 4, N], BF16)
    mo = tmp
    pb = [P.ps("pb%d" % i, [128, 512], F32) for i in range(8)]

    P.memset(ones, 1.0)
    P.memset(blk, 0.0)
    P.memset(blk[0:64, 0:64], 1.0)
    P.memset(blk[64:128, 64:128], 1.0)
    P.dma("sp", gms, gm)
    P.dma("sp", qgs, qg)
    P.dma("sp", kgs, kg)
    P.dma("sp", bEs.re("p a b c -> p (a b c)"), bE)
    P.dma("sp", bOs.re("p a b c -> p (a b c)"), bO)
    P.ts(qgs, qgs, 0.125, ALU.mult)
    ci = [0]
    load_w(P, wqb, wq, 8, 512, stage, ci)
    load_w(P, wkb, wk, 8, 512, stage, ci)
    load_w(P, wvb, wv, 8, 512, stage, ci)
    load_w(P, wob, wo, 4, D, stage, ci)

    x_at = io.get("x_at") or (lambda t0, n, _v=xT.re("(k p) t -> p k t", p=128): _v[:, :, t0:t0 + n])
    m_at = io.get("m_at") or (lambda t0, n, _v=mT.re("(k p) t -> p k t", p=128): _v[:, :, t0:t0 + n])
    gmb = V(gms.ap.unsqueeze(2).to_broadcast([128, 8, N]), gms.keys)
    cnt = [0]

    for t in range(T // N):
        ts_ = slice(t * N, (t + 1) * N)
        P.dma("sp", x1, x_at(t * N, N))
        rms_feat(P, x1, sq, pb[0][:, 0:N], rstd, ones, 8, N, 1024.0)
        P.tt(tmp, x1, gmb, ALU.mult, eng="pool")
        P.tt(h, tmp, V(rstd.ap.unsqueeze(1).to_broadcast([128, 8, N]), rstd.keys), ALU.mult)
        for which in range(2):
            wb = wqb if which == 0 else wkb
            gs = qgs if which == 0 else kgs
            for g in range(4):
                i = cnt[0] % 2
                cnt[0] += 1
                pq = pb[1 + i][:, 0:N]
                for k in range(8):
                    P.mm(pq, wb[:, k, g * 128:(g + 1) * 128], h[:, k, :], start=(k == 0), stop=(k == 7))
                P.copy(raw[i], pq, eng="act")
                P.tt(sq1[i], raw[i], raw[i], ALU.mult, eng="pool")
                ps2 = pb[3 + i][:, 0:N]
                P.mm(ps2, blk, sq1[i])
                P.act(rs1[i], ps2, AF.Sqrt, bias=EPS, scale=1.0 / 64)
                r_ = rs1[i]
                P.op("dve", lambda e, r_=r_: e.reciprocal(r_.ap, r_.ap), [r_], [r_])
                dst = qTt[:, g, :] if which == 0 else kTa[:, g, ts_]
                P.stt(dst, raw[i], gs[:, 0:1], rs1[i], ALU.mult, ALU.mult)
        pv_ = pb[5]
        for k in range(8):
            P.mm(pv_, h[:, k, :], wvb[:, k, :], start=(k == 0), stop=(k == 7))
        P.copy(Vt[:, t, :], pv_, eng="act")
        for cc in range(2):
            c = 2 * t + cc
            qs = slice(cc * 64, (cc + 1) * 64)
            slots = []
            if c % 2 == 0:
                for s in range(4):
                    slots.append(((c - 8) // 2 + s, 0, 128))
                slots.append((c // 2, 0, 64))
                bias = bEs
            else:
                slots.append(((c - 9) // 2, 64, 128))
                for s in range(1, 5):
                    slots.append(((c - 9) // 2 + s, 0, 128))
                bias = bOs
            valid = [(s, b, p0, p1) for s, (b, p0, p1) in enumerate(slots) if b >= 0]
            s_lo = valid[0][0]
            for hh in range(8):
                g, off = hh // 2, (hh % 2) * 64
                i = cnt[0] % 2
                cnt[0] += 1
                pS = pb[6][:, (i * 256):(i * 256) + 320] if False else pb[6 + i][:, 0:320]
                pS3 = pS.re("p (s q) -> p s q", s=5)
                for (s, b, p0, p1) in valid:
                    P.mm(pS3[:, s, :], kTa[off:off + 64, g, b * 128:(b + 1) * 128], qTt[off:off + 64, g, qs])
                P.tt(sc[i][:, s_lo:5, :], pS3[:, s_lo:5, :], bias[:, hh, s_lo:5, :], ALU.add)
                P.act(pT[i][:, s_lo:5, :], sc[i][:, s_lo:5, :], AF.Exp)
                pden = pb[3 + i][:, 128:192]
                po = pb[3 + i][:, 192:256]
                for j, (s, b, p0, p1) in enumerate(valid):
                    P.mm(pden, ones[p0:p1, :], pT[i][p0:p1, s, :], start=(j == 0), stop=(j == len(valid) - 1))
                for j, (s, b, p0, p1) in enumerate(valid):
                    P.mm(po, Vt[p0:p1, b, g * 128:(g + 1) * 128], pT[i][p0:p1, s, :], start=(j == 0), stop=(j == len(valid) - 1))
                rd = rden[i]
                P.op("dve", lambda e, rd=rd, pden=pden: e.reciprocal(rd.ap, pden.ap), [pden], [rd])
                P.tt(ao[off:off + 64, g, qs], po[off:off + 64, :], rd[off:off + 64, :], ALU.mult)
        for dc in range(8):
            pm = pb[1 + dc % 2][:, 256:256 + N]
            for g in range(4):
                P.mm(pm, wob[:, g, dc * 128:(dc + 1) * 128], ao[:, g, :], start=(g == 0), stop=(g == 3))
            P.copy(mo[:, dc, :], pm, eng=("act" if dc % 2 else "dve"))
        P.dma("pool", m_at(t * N, N), mo)
        if io.get("after_tile"):
            io["after_tile"]((t + 1) * N)


def attn_bias_layouts(rel_bias8):
    H = rel_bias8.shape[0]
    p = np.arange(128)[:, None, None]
    s = np.arange(5)[None, :, None]
    q = np.arange(64)[None, None, :]
    kk = p % 64
    hi = (p >= 64).astype(np.int64)
    outs = []
    for odd in (0, 1):
        if not odd:
            m = 2 * s + hi
            ok = (m <= 8)
        else:
            m = 2 * s - 1 + hi
            ok = (m >= 0)
        m = np.clip(m, 0, 8) + 0 * q
        rel = q - kk + 64 * (8 - m)
        idx = np.clip(rel, -256, 256) + 256
        ok = np.broadcast_to(ok, idx.shape)
        g = rel_bias8[:, idx]
        g = np.where(ok[None], g, np.float32(0))
        outs.append(np.ascontiguousarray(np.transpose(g, (1, 0, 2, 3)).reshape(128, H * 5 * 64)).astype(np.float32))
    return outs


NEG = -30000.0


EVEN_IN = {"gm": [128, 8], "wr": [1024, 256], "wk": [1024, 256], "wv": [1024, 256], "wbq": [1024, 256], "wbk": [1024, 256],
           "wbv": [1024, 256], "wgate": [1024, 256], "wba": [1024, 4], "mup": [128, 6], "mul": [128, 32], "cols": [128, 16],
           "w1": [1024, 64], "a1": [1024, 64], "g1": [1024, 128], "v1": [1024, 32], "w2": [64, 256], "a2": [64, 256],
           "g2": [128, 256], "v2": [32, 256], "convw": [128, 24], "gcols": [128, 8], "wo": [512, 1024], "lvm": [64, 6 * 2 * 64]}


def build_even(T=8192, vres=False, stop=99):
    nc = bass.Bass("TRN2", target_bir_lowering=False)
    P = Prog(nc)
    io = {k: P.dram(k, sh, F32, "ExternalInput") for k, sh in EVEN_IN.items()}
    io["xT"] = P.dram("xT", [1024, T], F32, "ExternalInput")
    if vres:
        io["vfi"] = P.dram("vfi", [256, T], F32, "ExternalInput")
    else:
        io["vfo"] = P.dram("vfo", [256, T], F32, "ExternalOutput")
    io["mT"] = P.dram("mT", [1024, T], F32, "ExternalOutput")
    phase_even(P, io, T, vres)
    P.emit()
    return nc


def phase_even(P, io, T=8192, vres=False):
    stop = 99
    N = 256
    NC = N // 64
    HW = N + 3
    D = 1024
    xT = io.get("xT"); gm = io["gm"]
    w_rkv = [io["wr"], io["wk"], io["wv"]]
    w_qkv = [io["wbq"], io["wbk"], io["wbv"]]
    wgate = io["wgate"]; wba = io["wba"]; mup = io["mup"]; mul = io["mul"]; cols = io["cols"]
    lw1 = [io["w1"], io["a1"], io["g1"], io["v1"]]
    lw2 = [io["w2"], io["a2"], io["g2"], io["v2"]]
    convw = io["convw"]; gcols = io["gcols"]; wo = io["wo"]; lvm = io["lvm"]
    vfi = io.get("vfi"); vfo = io.get("vfo"); mT = io.get("mT")

    S = lambda name, shape, dt=F32: P.sb(name, shape, dt)
    stage = [S("stg%d" % i, [128, 1024]) for i in range(2)]
    wrkvb = [S("wrkvb%d" % i, [128, 8, 256], BF16) for i in range(3)]
    wqkvb = [S("wqkvb%d" % i, [128, 8, 256], BF16) for i in range(3)]
    wgateb = S("wgateb", [128, 8, 256], BF16)
    wbab = S("wbab", [128, 8, 4], BF16)
    wbar = S("wbar", [128, 8, 4, 128], BF16)
    lcols = [64, 64, 128, 32]
    l1b = [S("l1b%d" % i, [128, 8, lcols[i]], BF16) for i in range(4)]
    l1A = [S("l1A%d" % i, [128, 8, lcols[i]], BF16) for i in range(4)]
    l1B = [S("l1B%d" % i, [128, 8, lcols[i]], BF16) for i in range(4)]
    l2b = [S("l2b%d" % i, [128, 1, 256], BF16) for i in range(4)]
    wob = S("wob", [128, 4, D], BF16)
    gms = S("gms", [128, 8]); mups = S("mups", [128, 6]); omups = S("omups", [128, 6])
    muls = S("muls", [128, 32]); omuls = S("omuls", [128, 32])
    colss = S("colss", [128, 16]); convs = S("convs", [128, 24]); gcs = S("gcs", [128, 8])
    ones = S("ones", [128, 128], BF16); blk = S("blk", [128, 128], BF16); ident = S("ident", [128, 128], BF16)
    identf = S("identf", [128, 128])
    m5 = S("m5", [64, 5, 64]); I2 = S("I2", [64, 2, 64])
    lvms = S("lvms", [64, 6, 2, 64])
    nmU = S("nmU", [64, 64]); nmLs = S("nmLs", [64, 64]); sU01 = S("sU01", [64, 64])
    pb = [P.ps("pb%d" % i, [128, 512], F32) for i in range(7)]
    ptb = P.ps("ptb", [128, 1024], BF16)

    P.memset(ones, 1.0); P.memset(blk, 0.0)
    P.memset(blk[0:64, 0:64], 1.0); P.memset(blk[64:128, 64:128], 1.0)
    P.memset(identf, 1.0)
    idf = identf
    P.op("pool", lambda e: e.affine_select(idf.ap, idf.ap, [[-1, 128]], ALU.is_ge, 0.0, base=0, channel_multiplier=1), [idf], [idf])
    P.op("pool", lambda e: e.affine_select(idf.ap, idf.ap, [[1, 128]], ALU.is_ge, 0.0, base=0, channel_multiplier=-1), [idf], [idf])
    P.copy(ident, identf)
    def tri(dst, cmp, fill, init):
        P.memset(dst, init)
        d = dst
        sgn = 1
        if cmp == ALU.is_lt:
            cmp, sgn = ALU.is_gt, -1
        elif cmp == ALU.is_le:
            cmp, sgn = ALU.is_ge, -1
        P.op("pool", lambda e: e.affine_select(d.ap, d.ap, [[-sgn, 64]], cmp, fill, base=0, channel_multiplier=sgn), [d], [d])
    tri(m5[:, 0, :], ALU.is_gt, 0.0, 1.0)
    tri(m5[:, 1, :], ALU.is_lt, 0.0, 1.0)
    tri(m5[:, 2, :], ALU.is_lt, 0.0, 1.0)
    tri(m5[:, 3, :], ALU.is_le, 0.0, 1.0)
    tri(m5[:, 4, :], ALU.is_le, 0.0, 1.0)
    P.copy(I2[:, 0, :], identf[0:64, 0:64]); P.copy(I2[:, 1, :], identf[0:64, 0:64])
    tri(nmU, ALU.is_le, NEG, 0.0)
    tri(nmLs, ALU.is_gt, NEG, 0.0)
    tri(sU01, ALU.is_lt, 0.0, 1.0)

    P.dma("sp", lvms.re("p a b c -> p (a b c)"), lvm)
    for dst, src in ((gms, gm), (mups, mup), (muls, mul), (colss, cols), (convs, convw), (gcs, gcols)):
        P.dma("sp", dst, src)
    P.ts(omups, mups, -1.0, ALU.mult, 1.0, ALU.add)
    P.ts(omuls, muls, -1.0, ALU.mult, 1.0, ALU.add)
    ci = [0]
    for i in range(3):
        load_w(P, wrkvb[i], w_rkv[i], 8, 256, stage, ci)
        load_w(P, wqkvb[i], w_qkv[i], 8, 256, stage, ci)
    load_w(P, wgateb, wgate, 8, 256, stage, ci)
    load_w(P, wbab, wba, 8, 4, stage, ci)
    P.copy(wbar, V(wbab.ap.unsqueeze(3).to_broadcast([128, 8, 4, 128]), wbab.keys))
    nl = 4 if vres else 3
    for i in range(nl):
        load_w(P, l1b[i], lw1[i], 8, lcols[i], stage, ci)
        mi = i
        for k in range(8):
            P.ts(l1A[i][:, k, :], l1b[i][:, k, :], omuls[:, mi * 8 + k:mi * 8 + k + 1], ALU.mult, eng=("pool" if k % 2 else "dve"))
            P.ts(l1B[i][:, k, :], l1b[i][:, k, :], muls[:, mi * 8 + k:mi * 8 + k + 1], ALU.mult, eng=("dve" if k % 2 else "pool"))
        st = stage[ci[0] % 2]; ci[0] += 1
        P.dma("sp", st[0:lcols[i], 0:256], lw2[i])
        P.copy(l2b[i][0:lcols[i], 0, :], st[0:lcols[i], 0:256])
    load_w(P, wob, wo, 4, D, stage, ci)
    nea = S("nea", [128, 2])
    P.act(nea, gcs[:, 0:2], AF.Exp)
    P.ts(nea, nea, -1.0, ALU.mult)

    hT = S("hT", [128, 8, HW], BF16)
    x1 = S("x1", [128, 8, N]); sq = S("sq", [128, 8, N], BF16); tmp = S("tmp", [128, 8, N]); rstd = S("rstd", [128, N])
    P.memset(hT[:, :, 0:3], 0.0)
    FM = lambda name, dt=F32: [S("%s%d" % (name, g), [128, N], dt) for g in range(2)]
    rr, kr, vr = FM("rr"), FM("kr"), FM("vr")
    lw, aa, gg, bon = FM("lw"), FM("aa"), FM("gg"), FM("bon")
    kkn, k2 = FM("kkn"), FM("k2")
    t1, t2, t3 = S("t1", [128, N]), S("t2", [128, N]), S("t3", [128, N])
    tb = S("tb", [128, N], BF16)
    clA, clB = S("clA", [128, N]), S("clB", [128, N])
    Wt, Wi, Wp, Wd = S("Wt", [128, N]), S("Wi", [128, N]), S("Wp", [128, N]), S("Wd", [128, N])
    WC = [S("WC%d" % g, [128, NC]) for g in range(2)]
    rt, kt, at, bt = FM("rt", BF16), FM("kt", BF16), FM("at", BF16), FM("bt", BF16)
    bd, kd, vb = FM("bd", BF16), FM("kd", BF16), FM("vb", BF16)
    tok = [S("tok%d" % g, [64, NC, 3, 128], BF16) for g in range(2)]
    dl = [S("dl%d" % i, [128, N], BF16) for i in range(4)]
    yT = FM("yT")
    ycat = S("ycat", [128, 4, N], BF16)
    mo = S("mo", [128, 8, N])
    Sf = [S("Sf%d" % g, [128, 64]) for g in range(2)]
    Sb = [S("Sb%d" % g, [128, 128], BF16) for g in range(2)]
    for g in range(2):
        P.memset(Sf[g], 0.0); P.memset(Sb[g], 0.0)
    NI = 6
    AM = [S("AM%d" % i, [64, 5, 64], BF16) for i in range(NI)]
    AB = [S("AB%d" % i, [64, 2, 64], BF16) for i in range(NI)]
    LL = [S("LL%d" % i, [64, 6, 2, 64], BF16) for i in range(NI)]
    PQ = [S("PQ%d" % i, [64, 2, 64], BF16) for i in range(NI)]
    Xs = [S("Xs%d" % i, [64, 128], BF16) for i in range(NI)]
    Zs = [S("Zs%d" % i, [64, 128], BF16) for i in range(NI)]
    for i in range(NI):
        P.memset(Zs[i], 0.0)
    qn, kn, qd = FM("qn", BF16), FM("kn", BF16), FM("qd", BF16)
    vg, sgate = FM("vg"), FM("sgate")
    vgb = FM("vgb", BF16)
    betab, gcb, egc = FM("betab"), FM("gcb"), FM("egc")
    gcol = [S("gcol%d" % g, [64, NC]) for g in range(2)]
    bcol = [S("bcol%d" % g, [64, NC]) for g in range(2)]
    nbw = [S("nbw%d" % g, [64, NC]) for g in range(2)]
    dcol = [S("dcol%d" % g, [64, NC]) for g in range(2)]
    egl = [S("egl%d" % g, [128, NC]) for g in range(2)]
    M3 = [S("M3%d" % g, [64, NC, 3, 64]) for g in range(2)]
    d3 = S("d3", [64, NC, 64]); d3b = S("d3b", [64, NC, 64])
    ktok = [S("ktok%d" % g, [64, NC, 128], BF16) for g in range(2)]
    bvf = [S("bvf%d" % g, [64, NC, 128]) for g in range(2)]
    Gf = [S("Gf%d" % g, [128, 128]) for g in range(2)]
    Gb = [S("Gb%d" % g, [128, 128], BF16) for g in range(2)]
    for g in range(2):
        P.memset(Gf[g], 0.0); P.memset(Gb[g], 0.0)
    yg = FM("yg")

    x_at = io.get("x_at") or (lambda t0, n, _v=xT.re("(k p) t -> p k t", p=128): _v[:, :, t0:t0 + n])
    m_at = io.get("m_at") or (lambda t0, n, _v=mT.re("(k p) t -> p k t", p=128): _v[:, :, t0:t0 + n])
    gmb = V(gms.ap.unsqueeze(2).to_broadcast([128, 8, N]), gms.keys)
    C = lambda g, j: colss[:, 2 * j + g:2 * j + g + 1]
    cur = slice(3, HW); prv = slice(2, HW - 1)
    rot = [0]

    def bank():
        rot[0] += 1
        return pb[rot[0] % 7]

    def v3(x):
        return x.re("p (c t) -> p c t", c=NC)

    def cumsum(src, dA, dB, np_=128):
        a = v3(src)
        bufs = [v3(dA), v3(dB)]
        i = 0
        for s in (1, 2, 4, 8, 16, 32):
            d = bufs[i % 2]
            P.tt(d[0:np_, :, s:], a[0:np_, :, s:], a[0:np_, :, :64 - s], ALU.add)
            P.copy(d[0:np_, :, :s], a[0:np_, :, :s], eng="pool")
            a = d
            i += 1
        return dB if i % 2 == 0 else dA

    def rsq(dst, ps, scale, eps):
        P.act(dst, ps, AF.Sqrt, bias=eps, scale=scale)
        P.op("dve", lambda e: e.reciprocal(dst.ap, dst.ap), [dst], [dst])

    def levels(insts):
        for i in insts:
            src = V(AM[i].ap[:, 0:2, :].unsqueeze(1).to_broadcast([64, 6, 2, 64]), AM[i].keys)
            P.tt(LL[i], src, lvms, ALU.mult, eng=("pool" if i % 2 else "dve"))
        for i in insts:
            P.tt(PQ[i], LL[i][:, 0, :, :], I2, ALU.add)
        for lvl in range(1, 6):
            pls = {}
            for i in insts:
                pl = bank()[0:64, 0:256].re("p (s q) -> p s q", s=4)
                pls[i] = pl
                P.mm(pl[:, 0, :], LL[i][:, lvl, 1, :], PQ[i][:, 0, :])
                P.mm(pl[:, 1, :], LL[i][:, lvl, 0, :], PQ[i][:, 1, :])
            for i in insts:
                P.copy(AB[i], pls[i][:, 0:2, :], eng="act")
            for i in insts:
                P.mm(pls[i][:, 2, :], PQ[i][:, 1, :], AB[i][:, 0, :])
                P.mm(pls[i][:, 3, :], PQ[i][:, 0, :], AB[i][:, 1, :])
            for i in insts:
                P.tt(PQ[i], PQ[i], pls[i][:, 2:4, :], ALU.add)

    for t in range(T // N):
        ts_ = slice(t * N, (t + 1) * N)
        P.dma("sp", x1, x_at(t * N, N))
        rms_feat(P, x1, sq, pb[0][:, 0:N], rstd, ones, 8, N, 1024.0)
        P.tt(tmp, x1, gmb, ALU.mult, eng="pool")
        P.tt(hT[:, :, cur], tmp, V(rstd.ap.unsqueeze(1).to_broadcast([128, 8, N]), rstd.keys), ALU.mult)
        for i in range(nl):
            pd_ = bank()[0:lcols[i], 0:N]
            for k in range(8):
                P.mm(pd_, l1A[i][:, k, :], hT[:, k, cur], start=(k == 0), stop=False)
                P.mm(pd_, l1B[i][:, k, :], hT[:, k, prv], start=False, stop=(k == 7))
            if i == 0:
                P.act(dl[i][0:lcols[i], :], pd_, AF.Tanh)
            elif i == 2:
                P.act(dl[i][0:lcols[i], :], pd_, AF.Sigmoid)
            else:
                P.copy(dl[i][0:lcols[i], :], pd_, eng="act")
        for g in range(2):
            gs = slice(g * 128, (g + 1) * 128)
            for j, dst in enumerate((rr[g], kr[g], vr[g])):
                pz = bank()[:, 0:HW]
                for k in range(8):
                    P.mm(pz, wrkvb[j][:, k, gs], hT[:, k, :], start=(k == 0), stop=(k == 7))
                P.ts(t1, pz[:, prv], mups[:, 2 * j + g:2 * j + g + 1], ALU.mult)
                P.stt(dst, pz[:, cur], omups[:, 2 * j + g:2 * j + g + 1], t1, ALU.mult, ALU.add)
            pu = bank()[:, 0:N]
            P.mm(pu, l2b[0][0:64, 0, gs], dl[0][0:64, :])
            P.act(lw[g], pu, AF.Sigmoid, bias=C(g, 0))
            P.ts(lw[g], lw[g], -math.exp(-0.5), ALU.mult, eng="pool")
            pu = bank()[:, 0:N]
            P.mm(pu, l2b[1][0:64, 0, gs], dl[1][0:64, :])
            P.act(aa[g], pu, AF.Sigmoid, bias=C(g, 1))
            pu = bank()[:, 0:N]
            P.mm(pu, l2b[2][:, 0, gs], dl[2])
            P.copy(gg[g], pu, eng="act")
            if vres:
                pu = bank()[:, 0:N]
                P.mm(pu, l2b[3][0:32, 0, gs], dl[3][0:32, :])
                P.act(t2, pu, AF.Sigmoid, bias=C(g, 6))
                P.dma("sp", t3, vfi[g * 128:(g + 1) * 128, ts_])
                P.tt(t3, t3, vr[g], ALU.subtract)
                P.tt(t3, t3, t2, ALU.mult)
                P.tt(vr[g], vr[g], t3, ALU.add)
            else:
                P.dma("pool", vfo[g * 128:(g + 1) * 128, ts_], vr[g])
            P.ts(t1, kr[g], C(g, 2), ALU.mult)
            P.tt(tb, t1, t1, ALU.mult, eng="pool")
            pk = bank()[:, 0:N]
            P.mm(pk, blk, tb)
            rsq(t2, pk, 1.0, 1e-6)
            P.tt(kkn[g], t1, t2, ALU.mult)
            P.ts(t1, aa[g], -1.0, ALU.add, C(g, 3), ALU.mult)
            P.stt(k2[g], t1, 1.0, kr[g], ALU.add, ALU.mult)
            P.tt(t1, rr[g], k2[g], ALU.mult, eng="pool")
            P.ts(tb, t1, C(g, 7), ALU.mult)
            pk = bank()[:, 0:N]
            P.mm(pk, blk, tb)
            P.tt(bon[g], pk, vr[g], ALU.mult)
            cl = cumsum(lw[g], clA, clB)
            P.act(Wt, cl, AF.Exp)
            P.act(Wi, cl, AF.Exp, scale=-1.0)
            P.tt(t1, cl, lw[g], ALU.subtract)
            P.act(Wp, t1, AF.Exp)
            cl3 = v3(cl)
            P.tt(v3(t1), V(cl3.ap[:, :, 63:64].to_broadcast([128, NC, 64]), cl.keys), cl3, ALU.subtract)
            P.act(Wd, t1, AF.Exp)
            P.act(WC[g], cl3[:, :, 63], AF.Exp)
            P.tt(rt[g], rr[g], Wt, ALU.mult)
            P.tt(kt[g], k2[g], Wi, ALU.mult, eng="pool")
            P.stt(at[g], kkn[g], -1.0, Wp, ALU.mult, ALU.mult)
            P.tt(t2, kkn[g], aa[g], ALU.mult, eng="pool")
            P.tt(bt[g], t2, Wi, ALU.mult)
            P.tt(bd[g], t2, Wd, ALU.mult, eng="pool")
            P.tt(kd[g], k2[g], Wd, ALU.mult)
            P.copy(vb[g], vr[g], eng="pool")
            for cc in range(NC):
                cs = slice(cc * 64, (cc + 1) * 64)
                ptr = ptb[0:64, (cc % 2) * 384:(cc % 2) * 384 + 384].re("p (s c) -> p s c", s=3)
                for j, src in enumerate((bd[g], kd[g], vb[g])):
                    P.tr(ptr[:, j, :], src[:, cs], ident)
                P.copy(tok[g][:, cc, :, :], ptr, eng=("act" if cc % 2 else "dve"))
        for g in range(2):
            gs = slice(g * 128, (g + 1) * 128)
            outs = []
            for j in range(3):
                pz = bank()[:, 0:HW]
                for k in range(8):
                    P.mm(pz, wqkvb[j][:, k, gs], hT[:, k, :], start=(k == 0), stop=(k == 7))
                cw = lambda tap: convs[:, (j * 2 + g) * 4 + tap:(j * 2 + g) * 4 + tap + 1]
                P.ts(t1, pz[:, 0:N], cw(0), ALU.mult)
                for tap in (1, 2, 3):
                    P.stt(t1, pz[:, tap:tap + N], cw(tap), t1, ALU.mult, ALU.add)
                dst = (t2, t3, vg[g])[j]
                P.act(dst, t1, AF.Silu)
            for src, dstb, sc_ in ((t2, qn[g], 128.0 ** -0.5), (t3, kn[g], 1.0)):
                P.tt(tb, src, src, ALU.mult, eng="pool")
                pk = bank()[:, 0:N]
                P.mm(pk, ones, tb)
                rsq(t1, pk, 1.0, 1e-6)
                P.stt(dstb, src, sc_, t1, ALU.mult, ALU.mult)
            P.copy(vgb[g], vg[g], eng="pool")
            pz = bank()[:, 0:N]
            for k in range(8):
                P.mm(pz, wgateb[:, k, gs], hT[:, k, cur], start=(k == 0), stop=(k == 7))
            P.act(sgate[g], pz, AF.Silu)
            pz = bank()[:, 0:N]
            for k in range(8):
                P.mm(pz, wbar[:, k, g, :], hT[:, k, cur], start=(k == 0), stop=(k == 7))
            P.act(betab[g], pz, AF.Sigmoid)
            pz = bank()[:, 0:N]
            for k in range(8):
                P.mm(pz, wbar[:, k, 2 + g, :], hT[:, k, cur], start=(k == 0), stop=(k == 7))
            P.act(t1, pz, AF.Exp, bias=gcs[:, 2 + g:3 + g])
            P.act(t1, t1, AF.Ln, bias=1.0)
            P.ts(t2, t1, nea[:, g:g + 1], ALU.mult)
            gc = cumsum(t2, clA, clB)
            P.copy(gcb[g], gc, eng="pool")
            g3 = v3(gcb[g])
            P.act(egc[g], gcb[g], AF.Exp)
            P.tt(qd[g], qn[g], egc[g], ALU.mult)
            P.act(egl[g], g3[:, :, 63], AF.Exp)
            idb = V(identf.ap[0:64, 0:64].unsqueeze(1).to_broadcast([64, NC, 64]), identf.keys)
            P.tt(d3, g3[0:64], idb, ALU.mult)
            P.red(gcol[g], d3)
            P.tt(d3, v3(betab[g])[0:64], idb, ALU.mult)
            P.red(bcol[g], d3)
            P.act(nbw[g], gcol[g], AF.Exp)
            P.stt(nbw[g], nbw[g], -1.0, bcol[g], ALU.mult, ALU.mult)
            P.tt(dcol[g], g3[0:64, :, 63], gcol[g], ALU.subtract)
            P.act(dcol[g], dcol[g], AF.Exp)
            gcolb = V(gcol[g].ap.unsqueeze(2).to_broadcast([64, NC, 64]), gcol[g].keys)
            bcolb = V(bcol[g].ap.unsqueeze(2).to_broadcast([64, NC, 64]), bcol[g].keys)
            nmUb = V(nmU.ap.unsqueeze(1).to_broadcast([64, NC, 64]), nmU.keys)
            nmLb = V(nmLs.ap.unsqueeze(1).to_broadcast([64, NC, 64]), nmLs.keys)
            sUb = V(sU01.ap.unsqueeze(1).to_broadcast([64, NC, 64]), sU01.keys)
            P.tt(d3, g3[0:64], nmUb, ALU.add)
            P.tt(d3, d3, gcolb, ALU.subtract)
            P.act(M3[g][:, :, 2, :], d3, AF.Exp)
            P.tt(d3b, M3[g][:, :, 2, :], sUb, ALU.mult)
            P.stt(M3[g][:, :, 1, :], d3b, -1.0, v3(betab[g])[0:64], ALU.mult, ALU.mult)
            P.tt(d3, nmLb, g3[0:64], ALU.subtract)
            P.tt(d3, d3, gcolb, ALU.add)
            P.act(d3b, d3, AF.Exp)
            P.stt(M3[g][:, :, 0, :], d3b, -1.0, bcolb, ALU.mult, ALU.mult)
            for cc in range(NC):
                cs = slice(cc * 64, (cc + 1) * 64)
                ptr = ptb[0:64, (cc % 2) * 384:(cc % 2) * 384 + 256].re("p (s c) -> p s c", s=2)
                P.tr(ptr[:, 0, :], kn[g][:, cs], ident)
                P.tr(ptr[:, 1, :], vgb[g][:, cs], ident)
                P.ts(ktok[g][:, cc, :], ptr[:, 0, :], dcol[g][:, cc:cc + 1], ALU.mult)
                P.ts(bvf[g][:, cc, :], ptr[:, 1, :], bcol[g][:, cc:cc + 1], ALU.mult)
        for cc in range(NC):
            cs = slice(cc * 64, (cc + 1) * 64)
            R = []
            for hh in range(4):
                R.append((hh, hh // 2, (hh % 2) * 64))
            pas = {}
            for (i, g, off) in R:
                o_ = slice(off, off + 64)
                pa = bank()[0:64, 0:320].re("p (s q) -> p s q", s=5)
                pas[i] = pa
                P.mm(pa[:, 0, :], at[g][o_, cs], bt[g][o_, cs])
                P.mm(pa[:, 1, :], bt[g][o_, cs], at[g][o_, cs])
                P.mm(pa[:, 2, :], kt[g][o_, cs], at[g][o_, cs])
                P.mm(pa[:, 3, :], bt[g][o_, cs], rt[g][o_, cs])
                P.mm(pa[:, 4, :], kt[g][o_, cs], rt[g][o_, cs])
            for (i, g, off) in R:
                P.tt(AM[i], pas[i], m5, ALU.mult)
            for g in range(2):
                i = 4 + g
                pa = bank()[0:64, 0:192].re("p (s q) -> p s q", s=3)
                pas[i] = pa
                P.mm(pa[:, 0, :], kn[g][:, cs], kn[g][:, cs])
                P.mm(pa[:, 1, :], kn[g][:, cs], kn[g][:, cs])
                P.mm(pa[:, 2, :], kn[g][:, cs], qn[g][:, cs])
            for g in range(2):
                i = 4 + g
                P.tt(AM[i][:, 0:3, :], pas[i], M3[g][:, cc, :, :], ALU.mult)
            levels([0, 1, 2, 3, 4, 5])
            px = {}
            for (i, g, off) in R:
                o_ = slice(off, off + 64)
                p_ = bank()
                px[i] = p_
                X = p_[0:64, 0:64]
                tk = P.mm(X, AM[i][:, 2, :], tok[g][:, cc, 2, o_], start=True, stop=False)
                P.mm(X, at[g][o_, cs], Sb[g][o_, o_], start=False, stop=True, after=(tk if off else None))
                P.copy(Xs[i][:, 0:64], X, eng="act")
            for (i, g, off) in R:
                Z = px[i][0:64, 64:128]
                P.mm(Z, PQ[i][:, 1, :], Xs[i][:, 0:64])
                P.copy(Zs[i][:, off:off + 64], Z, eng="act")
            for (i, g, off) in R:
                o_ = slice(off, off + 64)
                Y = px[i][:, 128:192]
                tk = P.mm(Y, Sb[g][o_, :], rt[g][o_, cs], start=True, stop=False)
                P.mm(Y, Zs[i], AM[i][:, 3, :], start=False, stop=False, after=(tk if off else None))
                P.mm(Y, tok[g][:, cc, 2, :], AM[i][:, 4, :], start=False, stop=True)
                P.copy(yT[g][o_, cs], Y[o_, :], eng="dve")
                Sn = px[i][:, 192:256]
                P.mm(Sn, tok[g][:, cc, 0, :], Zs[i][:, off:off + 64], start=True, stop=False)
                P.mm(Sn, tok[g][:, cc, 1, :], tok[g][:, cc, 2, o_], start=False, stop=True)
                P.stt(Sf[g][o_, :], Sf[g][o_, :], WC[g][o_, cc:cc + 1], Sn[o_, :], ALU.mult, ALU.add)
                P.copy(Sb[g][o_, o_], Sf[g][o_, :], eng="pool")
            for g in range(2):
                i = 4 + g
                p_ = bank()
                px[i] = p_
                KS = p_[0:64, 0:128]
                P.mm(KS, kn[g][:, cs], Gb[g])
                P.stt(Xs[i], KS, nbw[g][:, cc:cc + 1], bvf[g][:, cc, :], ALU.mult, ALU.add)
            for g in range(2):
                i = 4 + g
                Z = px[i][0:64, 128:256]
                P.mm(Z, PQ[i][:, 1, :], Xs[i])
                P.copy(Zs[i], Z, eng="act")
            for g in range(2):
                i = 4 + g
                Y = px[i][:, 256:320]
                P.mm(Y, Gb[g], qd[g][:, cs], start=True, stop=False)
                P.mm(Y, Zs[i], AM[i][:, 2, :], start=False, stop=True)
                P.copy(yg[g][:, cs], Y, eng="dve")
                Sn = px[i][:, 320:448]
                P.mm(Sn, ktok[g][:, cc, :], Zs[i])
                P.stt(Gf[g], Gf[g], egl[g][:, cc:cc + 1], Sn, ALU.mult, ALU.add)
                P.copy(Gb[g], Gf[g], eng="pool")
        for g in range(2):
            P.copy(tb, yT[g], eng="pool")
            pm_ = bank()[:, 0:N]
            P.mm(pm_, blk, tb)
            P.ts(t1, pm_, 1.0 / 64, ALU.mult)
            P.tt(t2, yT[g], t1, ALU.subtract)
            P.tt(tb, t2, t2, ALU.mult, eng="pool")
            pv_ = bank()[:, 0:N]
            P.mm(pv_, blk, tb)
            rsq(t3, pv_, 1.0 / 64, 64e-5)
            P.tt(t2, t2, t3, ALU.mult)
            P.ts(t2, t2, C(g, 4), ALU.mult, C(g, 5), ALU.add)
            P.tt(t2, t2, bon[g], ALU.add)
            P.tt(ycat[:, g, :], t2, gg[g], ALU.mult)
            P.tt(tb, yg[g], yg[g], ALU.mult, eng="pool")
            pv_ = bank()[:, 0:N]
            P.mm(pv_, ones, tb)
            rsq(t3, pv_, 1.0 / 128, EPS)
            P.stt(t1, yg[g], gcs[:, 4 + g:5 + g], t3, ALU.mult, ALU.mult)
            P.tt(ycat[:, 2 + g, :], t1, sgate[g], ALU.mult)
        for dc in range(8):
            pm_ = bank()[:, 0:N]
            for q4 in range(4):
                P.mm(pm_, wob[:, q4, dc * 128:(dc + 1) * 128], ycat[:, q4, :], start=(q4 == 0), stop=(q4 == 3))
            P.copy(mo[:, dc, :], pm_, eng=("act" if dc % 2 else "dve"))
        P.dma("pool", m_at(t * N, N), mo)
        if io.get("after_tile"):
            io["after_tile"]((t + 1) * N)
        P.copy(hT[:, :, 0:3], hT[:, :, N:N + 3], eng="pool")


def even_inputs(inp, e, b, hh, xT_b, vfi=None):
    c = np.ascontiguousarray
    f = lambda k: np.asarray(inp[k][e], np.float32)
    W = f("even_w_in")
    o_bq = 1536; o_bg = o_bq + 1536; o_bb = o_bg + 512; o_ba = o_bb + 4
    a = slice(hh * 256, hh * 256 + 256)
    col2 = lambda v: c(np.asarray(v, np.float32)[a].reshape(2, 128).T)
    d = {"gm": c(np.asarray(inp["norm_mix_g"][2 * e], np.float32).reshape(8, 128).T)}
    d["wr"] = c(W[:, 0:512][:, a]); d["wk"] = c(W[:, 512:1024][:, a]); d["wv"] = c(W[:, 1024:1536][:, a])
    d["wbq"] = c(W[:, o_bq:o_bq + 512][:, a]); d["wbk"] = c(W[:, o_bq + 512:o_bq + 1024][:, a]); d["wbv"] = c(W[:, o_bq + 1024:o_bq + 1536][:, a])
    d["wgate"] = c(W[:, o_bg:o_bg + 512][:, a])
    d["wba"] = c(np.concatenate([W[:, o_bb + 2 * hh:o_bb + 2 * hh + 2], W[:, o_ba + 2 * hh:o_ba + 2 * hh + 2]], 1))
    mp = f("rwkv_mu_proj")
    d["mup"] = c(np.concatenate([col2(mp[j]) for j in range(3)], 1))
    ml = f("rwkv_mu_lora")
    mus = [ml[0], ml[1], ml[2], (np.asarray(inp["rwkv_v_mu"][e - 1], np.float32) if e > 0 else np.zeros(1024, np.float32))]
    d["mul"] = c(np.concatenate([m.reshape(8, 128).T for m in mus], 1))
    v0 = np.asarray(inp["rwkv_v0"][e - 1], np.float32) if e > 0 else np.zeros(512, np.float32)
    rk = f("rwkv_r_k").reshape(512)
    d["cols"] = c(np.concatenate([col2(v) for v in (f("rwkv_w0"), f("rwkv_a0"), f("rwkv_k_k"), f("rwkv_k_a"),
                                                     f("rwkv_ln_g"), f("rwkv_ln_b"), v0, rk)], 1))
    d["w1"] = f("rwkv_w1"); d["a1"] = f("rwkv_a1"); d["g1"] = f("rwkv_g1")
    d["w2"] = c(f("rwkv_w2")[:, a]); d["a2"] = c(f("rwkv_a2")[:, a]); d["g2"] = c(f("rwkv_g2")[:, a])
    if e > 0:
        d["v1"] = np.asarray(inp["rwkv_v1"][e - 1], np.float32); d["v2"] = c(np.asarray(inp["rwkv_v2"][e - 1], np.float32)[:, a])
    else:
        d["v1"] = np.zeros((1024, 32), np.float32); d["v2"] = np.zeros((32, 256), np.float32)
    cw = f("gdn_conv_w")
    cws = []
    for j in range(3):
        for g in range(2):
            ch = slice(j * 512 + hh * 256 + g * 128, j * 512 + hh * 256 + g * 128 + 128)
            cws.append(cw[:, ch].T)
    d["convw"] = c(np.concatenate(cws, 1))
    al = f("gdn_a_log")[2 * hh:2 * hh + 2]; dtb = f("gdn_dt_bias")[2 * hh:2 * hh + 2]; ng = f("gdn_norm_g")
    gcols = np.zeros((128, 8), np.float32)
    gcols[:, 0] = al[0]; gcols[:, 1] = al[1]; gcols[:, 2] = dtb[0]; gcols[:, 3] = dtb[1]; gcols[:, 4] = ng; gcols[:, 5] = ng
    d["gcols"] = gcols
    wo = f("even_w_out")
    d["wo"] = c(np.concatenate([wo[0:512][a], wo[512:1024][a]], 0))
    d["lvm"] = level_masks()
    if xT_b is not None:
        d["xT"] = xT_b
    if vfi is not None:
        d["vfi"] = vfi
    return d


def level_masks():
    t = np.arange(64)[:, None]
    j = np.arange(64)[None, :]
    out = np.zeros((64, 6, 2, 64), np.float32)
    for k in range(6):
        mq = ((t >> (k + 1)) == (j >> (k + 1))) & (((t >> k) & 1) == 1) & (((j >> k) & 1) == 0)
        out[:, k, 0, :] = mq
        out[:, k, 1, :] = mq.T
    return np.ascontiguousarray(out.reshape(64, 6 * 2 * 64))


PAIRS = [[0, 1], [2, 3], [4, 5], [6, 7]]


def build_fused(T=8192):
    H = T // 2
    nc = bass.Bass("TRN2", target_bir_lowering=False)
    P = Prog(nc)
    ext = lambda name, shape: P.dram(name, shape, F32, "ExternalInput")
    xT = ext("xT", [1024, T])
    xown = ext("xown", [1024, H])
    pTs = [ext("pT%d" % i, [256, H]) for i in range(4)]
    oT = P.dram("oT", [1024, H], F32, "ExternalOutput")
    CW = 512
    NCH = H // CW
    mk = lambda nm, rows: [[P.dram("%s%d_%d" % (nm, i, c), [rows, CW], F32, "Internal") for c in range(NCH)] for i in range(2)]
    xg, mp, mr, xo = mk("xg", 2048), mk("mp", 2048), mk("mr", 1024), mk("xo", 1024)
    vf = P.dram("vf", [256, T], F32, "Internal")

    def acc2(chunks):
        vs = [c.re("(h k p) t -> h p k t", h=2, p=128) for c in chunks]
        return lambda t0, n: vs[(t0 % H) // CW][t0 // H][:, :, (t0 % CW):(t0 % CW) + n]

    def acc1(chunks):
        vs = [c.re("(k p) t -> p k t", p=128) for c in chunks]
        return lambda t0, n: vs[t0 // CW][:, :, (t0 % CW):(t0 % CW) + n]

    def rs_hook(i):
        def hook(tok_end):
            if tok_end > H and (tok_end - H) % CW == 0:
                c = (tok_end - H) // CW - 1
                P.cc("ReduceScatter", ALU.add, PAIRS, mp[i % 2][c], mr[i % 2][c])
        return hook

    def ag_hook(i):
        def hook(tok_end):
            if tok_end % CW == 0:
                c = tok_end // CW - 1
                P.cc("AllGather", ALU.bypass, PAIRS, xo[i % 2][c], xg[(i + 1) % 2][c])
        return hook

    P.prefix = "L0m_"
    xown_at = (lambda t0, n, _v=xown.re("(k p) t -> p k t", p=128): _v[:, :, t0:t0 + n])
    for i in range(4):
        if i == 0:
            x_at = (lambda t0, n, _v=xT.re("(k p) t -> p k t", p=128): _v[:, :, t0:t0 + n])
        else:
            x_at = acc2(xg[i % 2])
        pre = "L%dm_" % i
        if i % 2 == 0:
            io = {k: ext(pre + k, sh) for k, sh in EVEN_IN.items()}
            io["vfo" if i == 0 else "vfi"] = vf
            io["x_at"] = x_at
            io["m_at"] = acc2(mp[i % 2])
            io["after_tile"] = rs_hook(i)
            phase_even(P, io, T, vres=(i > 0))
        else:
            io = {k: ext(pre + k, sh) for k, sh in ATTN_IN.items()}
            io["x_at"] = x_at
            io["m_at"] = acc2(mp[i % 2])
            io["after_tile"] = rs_hook(i)
            phase_attn(P, io, T)
        P.end_phase("L%dp_" % i)
        pre = "L%dp_" % i
        io = {k: ext(pre + k, sh) for k, sh in POST_IN.items()}
        io.update(pT=pTs[i], x_at=xown_at, m0_at=acc1(mr[i % 2]))
        if i == 3:
            io["oT"] = oT
        else:
            io["o_at"] = acc1(xo[i % 2])
        if i < 3:
            io["after_tile"] = ag_hook(i)
        phase_post(P, io, NT=H, N=256)
        if i < 3:
            xown_at = acc1(xo[i % 2])
        P.end_phase("L%dm_" % (i + 1))
    P.emit()
    return nc


def fused_inputs(inp, b, hh):
    c = np.ascontiguousarray
    f32 = lambda a: np.asarray(a, np.float32)
    S = inp["x"].shape[1]
    H = S // 2
    sl = slice(hh * H, (hh + 1) * H)
    xt = c(f32(inp["x"][b]).T)
    d = {"xT": xt, "xown": c(xt[:, sl])}
    for i in range(4):
        d["pT%d" % i] = c(f32(inp["p"][i, b, sl]).T)
        pre = "L%dm_" % i
        if i % 2 == 0:
            di = even_inputs(inp, i // 2, b, hh, None)
            for k in EVEN_IN:
                d[pre + k] = di[k]
        else:
            o = i // 2
            wqkv = f32(inp["attn_w_qkv"][o]); wo = f32(inp["attn_w_out"][o])
            bE, bO = attn_bias_layouts(f32(inp["attn_rel_bias"][o])[hh * 8:(hh + 1) * 8])
            da = {"gm": c(f32(inp["norm_mix_g"][i]).reshape(8, 128).T), "wq": c(wqkv[:, hh * 512:(hh + 1) * 512]),
                  "wk": c(wqkv[:, 1024 + hh * 512:1024 + (hh + 1) * 512]), "wv": c(wqkv[:, 2048 + hh * 512:2048 + (hh + 1) * 512]),
                  "wo": c(wo[hh * 512:(hh + 1) * 512]), "qg": c(np.tile(f32(inp["attn_q_g"][o]), 2)[:, None]),
                  "kg": c(np.tile(f32(inp["attn_k_g"][o]), 2)[:, None]), "bE": bE, "bO": bO}
            for k in ATTN_IN:
                d[pre + k] = da[k]
        pre = "L%dp_" % i
        dp = {"gf": c(f32(inp["norm_ffn_g"][i]).reshape(8, 128).T), "gp": c(f32(inp["ple_norm_g"][i]).reshape(8, 128).T),
              "w1": c(f32(inp["mlp_w1"][i])), "w2": c(f32(inp["mlp_w2"][i])), "wg": c(f32(inp["ple_w_gate"][i])), "wp": c(f32(inp["ple_w_proj"][i]))}
        for k in POST_IN:
            d[pre + k] = dp[k]
    return d


_NC = {}


def kernel(**inp):
    inp = {k: np.asarray(v) for k, v in inp.items()}
    B, S, D = inp["x"].shape
    if "fused" not in _NC:
        _NC["fused"] = build_fused(T=S)
    in_maps = [fused_inputs(inp, c // 2, c % 2) for c in range(2 * B)]
    res = run_bass_kernel_spmd(_NC["fused"], in_maps, core_ids=list(range(2 * B)))
    out = [np.concatenate([res.results[2 * b]["oT"], res.results[2 * b + 1]["oT"]], axis=1).T for b in range(B)]
    return np.ascontiguousarray(np.stack(out, 0)).astype(np.float32)
```
